# Optimizing a Trainium2 kernel written in Bass

```python
import math
import jax, jax.numpy as jnp
from jax import lax
import numpy as np

D_MODEL = 1024
BATCH = 8
SEQ = 4096
DEPTH = 1
DEC_BATCH = 8
DEC_SEQ = 16
PAST_LEN = 4096

CHUNK = 64
EPS = 1e-6
GLA_HEADS = 4
GLA_QK = D_MODEL // 2
GLA_V = D_MODEL
GLA_DK = GLA_QK // GLA_HEADS
GLA_DV = GLA_V // GLA_HEADS
GLA_RANK = 16
GLA_GATE_NORM = 16.0
GDN_HEADS = 8
GDN_DK = 128
GDN_DV = 128
GDN_QK = GDN_HEADS * GDN_DK
GDN_V = GDN_HEADS * GDN_DV
GDN_CONV_CH = 2 * GDN_QK + GDN_V
CONV_W = 4
SPLIT_SIZES = (GLA_QK, GLA_QK, GLA_V, GLA_V, GLA_RANK, GDN_CONV_CH, GDN_V, GDN_HEADS, GDN_HEADS, D_MODEL, D_MODEL)
D_IN = 2 * GLA_QK + 2 * GLA_V + GLA_RANK + GDN_CONV_CH + GDN_V + 2 * GDN_HEADS + 2 * D_MODEL

kernel_name = "gla_gdn_parallel_streaming_step"


def rmsnorm(x, gain):
    xf = x.astype(jnp.float32)
    y = xf * lax.rsqrt(jnp.mean(xf * xf, axis=-1, keepdims=True) + EPS)
    return (y * gain.astype(jnp.float32)).astype(x.dtype)


def l2norm(x):
    return x * lax.rsqrt(jnp.sum(x * x, axis=-1, keepdims=True) + EPS)


def pad_time(a, pad):
    return jnp.pad(a, [(0, 0), (0, pad)] + [(0, 0)] * (a.ndim - 2))


def to_blocks(a):
    b, t = a.shape[0], a.shape[1]
    a = a.reshape((b, t // CHUNK, CHUNK) + a.shape[2:])
    return jnp.moveaxis(a, 2, 3)


def from_blocks(a):
    a = jnp.moveaxis(a, 3, 2)
    return a.reshape((a.shape[0], a.shape[1] * CHUNK) + a.shape[3:])


def gla_chunked(q, k, v, g, s0):
    qc, kc, vc, gc = (to_blocks(a) for a in (q, k, v, g))
    b = jnp.cumsum(gc, axis=3)
    b_ref = b[:, :, :, CHUNK // 2 - 1:CHUNK // 2, :]
    b_last = b[:, :, :, CHUNK - 1:, :]
    incl = jnp.tril(jnp.ones((CHUNK, CHUNK), dtype=bool))
    att = jnp.einsum('bnhid,bnhjd->bnhij', qc * jnp.exp(b - b_ref), kc * jnp.exp(b_ref - b))
    att = jnp.where(incl, att, 0.0)
    o_intra = jnp.einsum('bnhij,bnhjv->bnhiv', att, vc)
    q_dec = qc * jnp.exp(b)
    k_dec = kc * jnp.exp(b_last - b)
    block_decay = jnp.exp(b_last[:, :, :, 0, :])

    def step(s, inp):
        q_n, k_n, v_n, d_n = inp
        o_n = jnp.einsum('bhid,bhdv->bhiv', q_n, s)
        s = d_n[..., None] * s + jnp.einsum('bhjd,bhjv->bhdv', k_n, v_n)
        return s, o_n

    xs = tuple(jnp.moveaxis(a, 1, 0) for a in (q_dec, k_dec, vc, block_decay))
    s_fin, o_inter = lax.scan(step, s0, xs)
    o = o_intra + jnp.moveaxis(o_inter, 0, 1)
    return from_blocks(o), s_fin


def gdn_chunked(q, k, v, beta, g, s0):
    dv = v.shape[-1]
    qc, kc, vc, bc, gc = (to_blocks(a) for a in (q, k, v, beta, g))
    b = jnp.cumsum(gc, axis=-1)
    incl = jnp.tril(jnp.ones((CHUNK, CHUNK), dtype=bool))
    strict = jnp.tril(jnp.ones((CHUNK, CHUNK), dtype=bool), -1)
    diff = b[..., :, None] - b[..., None, :]
    decay_mat = jnp.where(incl, jnp.exp(jnp.where(incl, diff, 0.0)), 0.0)
    kk = jnp.einsum('bnhid,bnhjd->bnhij', kc, kc)
    a_low = jnp.where(strict, bc[..., :, None] * kk * decay_mat, 0.0)
    eye = jnp.eye(CHUNK, dtype=a_low.dtype)
    rhs = jnp.concatenate([bc[..., None] * vc, (bc * jnp.exp(b))[..., None] * kc], axis=-1)
    sol = lax.linalg.triangular_solve(a_low + eye, rhs, left_side=True, lower=True, unit_diagonal=True)
    u_v, w_k = sol[..., :dv], sol[..., dv:]
    qk = jnp.einsum('bnhid,bnhjd->bnhij', qc, kc) * decay_mat
    q_dec = qc * jnp.exp(b)[..., None]
    k_dec = kc * jnp.exp(b[..., -1:] - b)[..., None]
    block_decay = jnp.exp(b[..., -1])

    def step(s, inp):
        q_n, k_n, uv_n, wk_n, qk_n, d_n = inp
        u = uv_n - jnp.einsum('bhid,bhdv->bhiv', wk_n, s)
        o_n = jnp.einsum('bhid,bhdv->bhiv', q_n, s) + jnp.einsum('bhij,bhjv->bhiv', qk_n, u)
        s = d_n[..., None, None] * s + jnp.einsum('bhjd,bhjv->bhdv', k_n, u)
        return s, o_n

    xs = tuple(jnp.moveaxis(a, 1, 0) for a in (q_dec, k_dec, u_v, w_k, qk, block_decay))
    s_fin, o = lax.scan(step, s0, xs)
    return from_blocks(jnp.moveaxis(o, 0, 1)), s_fin


def causal_conv(u, buf, w):
    t = u.shape[1]
    up = jnp.concatenate([buf, u], axis=1)
    y = up[:, 0:t] * w[0]
    for i in range(1, CONV_W):
        y = y + up[:, i:i + t] * w[i]
    return jax.nn.silu(y), up[:, t:]


def head_gated_norm(o, gain, z):
    of = o.astype(jnp.float32)
    of = of * lax.rsqrt(jnp.mean(of * of, axis=-1, keepdims=True) + EPS) * gain.astype(jnp.float32)
    return of * jax.nn.silu(z)


def streaming_layer(x, c, s_gla, s_gdn, conv_buf, w_ada, b_ada, g_norm1, w_in, w_gk2, b_gk,
                    w_conv, a_log, dt_bias, g_norm_a, g_norm_b, w_pa, w_pb, w_out):
    bsz, t, _ = x.shape
    pad = (-t) % CHUNK
    f32 = jnp.float32
    mod = jax.nn.silu(c) @ w_ada + b_ada
    shift, scale, gate = jnp.split(mod, 3, axis=-1)
    h = rmsnorm(x, g_norm1) * (1.0 + scale[:, None, :]) + shift[:, None, :]
    p = (h @ w_in).astype(f32)
    split_idx = [int(i) for i in np.cumsum(SPLIT_SIZES)[:-1]]
    qa, ka, va, za, gk_low, qkv_b, zb, beta_in, a_in, ga, gb = jnp.split(p, split_idx, axis=-1)

    gk = jax.nn.log_sigmoid(gk_low @ w_gk2 + b_gk) / GLA_GATE_NORM
    qa = qa.reshape(bsz, t, GLA_HEADS, GLA_DK) * (GLA_DK ** -0.5)
    ka = ka.reshape(bsz, t, GLA_HEADS, GLA_DK)
    va = va.reshape(bsz, t, GLA_HEADS, GLA_DV)
    gk = gk.reshape(bsz, t, GLA_HEADS, GLA_DK)
    oa, s_gla_new = gla_chunked(pad_time(qa, pad), pad_time(ka, pad), pad_time(va, pad),
                                pad_time(gk, pad), s_gla.astype(f32))
    oa = head_gated_norm(oa[:, :t], g_norm_a, za.reshape(bsz, t, GLA_HEADS, GLA_DV))
    oa = oa.reshape(bsz, t, GLA_V)

    qkv_c, conv_new = causal_conv(qkv_b, conv_buf.astype(f32), w_conv)
    qb, kb, vb = jnp.split(qkv_c, [GDN_QK, 2 * GDN_QK], axis=-1)
    qb = l2norm(qb.reshape(bsz, t, GDN_HEADS, GDN_DK)) * (GDN_DK ** -0.5)
    kb = l2norm(kb.reshape(bsz, t, GDN_HEADS, GDN_DK))
    vb = vb.reshape(bsz, t, GDN_HEADS, GDN_DV)
    beta = jax.nn.sigmoid(beta_in)
    g_b = -jnp.exp(a_log.astype(f32)) * jax.nn.softplus(a_in + dt_bias)
    ob, s_gdn_new = gdn_chunked(pad_time(qb, pad), pad_time(kb, pad), pad_time(vb, pad),
                                pad_time(beta, pad), pad_time(g_b, pad), s_gdn.astype(f32))
    ob = head_gated_norm(ob[:, :t], g_norm_b, zb.reshape(bsz, t, GDN_HEADS, GDN_DV))
    ob = ob.reshape(bsz, t, GDN_V)

    merged = jax.nn.sigmoid(ga) * (oa @ w_pa) + jax.nn.sigmoid(gb) * (ob @ w_pb)
    out = (merged @ w_out).astype(x.dtype)
    x = x + gate[:, None, :] * out
    return x, s_gla_new.astype(s_gla.dtype), s_gdn_new.astype(s_gdn.dtype), conv_new.astype(conv_buf.dtype)


def setup_inputs(seed: int = 0) -> dict:
    key = jax.random.key(seed)
    ks = jax.random.split(key, 24)

    def nrm(k, shape, s):
        return jax.random.normal(k, shape, jnp.float32) * s

    dt = jnp.exp(jax.random.uniform(ks[15], (DEPTH, GDN_HEADS), jnp.float32, math.log(1e-3), math.log(1e-1)))
    return {
        "x_prompt": nrm(ks[0], (BATCH, SEQ, D_MODEL), 1.0),
        "x_sample": nrm(ks[1], (DEC_BATCH, DEC_SEQ, D_MODEL), 1.0),
        "c_prompt": nrm(ks[2], (BATCH, D_MODEL), 1.0),
        "c_sample": nrm(ks[3], (DEC_BATCH, D_MODEL), 1.0),
        "state_gla": nrm(ks[4], (DEPTH, DEC_BATCH, GLA_HEADS, GLA_DK, GLA_DV), 1.0),
        "state_gdn": nrm(ks[5], (DEPTH, DEC_BATCH, GDN_HEADS, GDN_DK, GDN_DV), 0.3),
        "cache_conv_gdn": nrm(ks[6], (DEPTH, DEC_BATCH, CONV_W - 1, GDN_CONV_CH), 1.0),
        "w_ada": nrm(ks[7], (DEPTH, D_MODEL, 3 * D_MODEL), 0.5 * D_MODEL ** -0.5),
        "b_ada": nrm(ks[8], (DEPTH, 3 * D_MODEL), 0.02),
        "g_norm1": 1.0 + nrm(ks[9], (DEPTH, D_MODEL), 0.02),
        "w_in": nrm(ks[10], (DEPTH, D_MODEL, D_IN), D_MODEL ** -0.5),
        "w_gk2": nrm(ks[11], (DEPTH, GLA_RANK, GLA_QK), GLA_RANK ** -0.5),
        "b_gk": nrm(ks[12], (DEPTH, GLA_QK), 0.1),
        "w_conv": nrm(ks[13], (DEPTH, CONV_W, GDN_CONV_CH), CONV_W ** -0.5),
        "a_log": jnp.log(jax.random.uniform(ks[14], (DEPTH, GDN_HEADS), jnp.float32, 1.0, 16.0)),
        "dt_bias": dt + jnp.log(-jnp.expm1(-dt)),
        "g_norm_a": 1.0 + nrm(ks[16], (DEPTH, GLA_DV), 0.02),
        "g_norm_b": 1.0 + nrm(ks[17], (DEPTH, GDN_DV), 0.02),
        "w_pa": nrm(ks[18], (DEPTH, GLA_V, D_MODEL), GLA_V ** -0.5),
        "w_pb": nrm(ks[19], (DEPTH, GDN_V, D_MODEL), GDN_V ** -0.5),
        "w_out": nrm(ks[20], (DEPTH, D_MODEL, D_MODEL), D_MODEL ** -0.5),
        "g_final": 1.0 + nrm(ks[21], (D_MODEL,), 0.02),
    }


def reference(x_prompt, x_sample, c_prompt, c_sample, state_gla, state_gdn, cache_conv_gdn,
              w_ada, b_ada, g_norm1, w_in, w_gk2, b_gk, w_conv, a_log, dt_bias,
              g_norm_a, g_norm_b, w_pa, w_pb, w_out, g_final):
    bp = x_prompt.shape[0]
    hp, hs = x_prompt, x_sample
    gla_p, gdn_p, conv_p, gla_s, gdn_s, conv_s = [], [], [], [], [], []
    for layer in range(DEPTH):
        lw = (w_ada[layer], b_ada[layer], g_norm1[layer], w_in[layer], w_gk2[layer], b_gk[layer],
              w_conv[layer], a_log[layer], dt_bias[layer], g_norm_a[layer], g_norm_b[layer],
              w_pa[layer], w_pb[layer], w_out[layer])
        z_gla = jnp.zeros((bp, GLA_HEADS, GLA_DK, GLA_DV), jnp.float32)
        z_gdn = jnp.zeros((bp, GDN_HEADS, GDN_DK, GDN_DV), jnp.float32)
        z_conv = jnp.zeros((bp, CONV_W - 1, GDN_CONV_CH), jnp.float32)
        hp, sg, sd, cv = streaming_layer(hp, c_prompt, z_gla, z_gdn, z_conv, *lw)
        gla_p.append(sg)
        gdn_p.append(sd)
        conv_p.append(cv)
        hs, sg, sd, cv = streaming_layer(hs, c_sample, state_gla[layer], state_gdn[layer], cache_conv_gdn[layer], *lw)
        gla_s.append(sg)
        gdn_s.append(sd)
        conv_s.append(cv)
    y_prompt = rmsnorm(hp, g_final)
    y_sample = rmsnorm(hs, g_final)
    return (y_prompt, y_sample, jnp.stack(gla_p), jnp.stack(gdn_p), jnp.stack(conv_p), jnp.stack(gla_s), jnp.stack(gdn_s), jnp.stack(conv_s))
```

```python
import numpy as np
from contextlib import ExitStack
import concourse.bass as bass
import concourse.mybir as mybir
from concourse.bass_utils import run_bass_kernel_spmd

F32 = mybir.dt.float32
F32R = mybir.dt.float32r
BF16 = mybir.dt.bfloat16
AF = mybir.ActivationFunctionType
ALU = mybir.AluOpType

ENGS = ("pe", "act", "dve", "pool", "sp")
NDS = 12
import os as _os0
AUTOWARM = int(_os0.environ.get("AUTOWARM", "0"))
GEN = 2000
NGEN = 16

D = 1024
T_PROMPT = 4096
NT = T_PROMPT // 128
T_S = 16
D_IN = 9248
EPS = 1e-6
TPP = 8
NPASS = NT // TPP
NSLOT = TPP + 1

C_QA, C_KA, C_VA, C_ZA, C_GL = 0, 512, 1024, 2048, 3072
C_QKV, C_ZB, C_BETA, C_AIN, C_GA, C_GB = 3088, 6160, 7184, 7192, 7200, 8224


class Buf:
    __slots__ = ("name", "last_w", "readers")

    def __init__(self, name):
        self.name = name
        self.last_w = None
        self.readers = []


class Op:
    __slots__ = ("eng", "emit", "deps", "is_dma", "inc", "count", "sem", "val", "idx")

    def __init__(self, eng, emit, is_dma):
        self.eng = eng
        self.emit = emit
        self.deps = []
        self.is_dma = is_dma
        self.inc = False
        self.count = None
        self.sem = None
        self.val = None


class Sched:
    def __init__(self):
        self.ops = {e: [] for e in ENGS}
        self.dma_count = {e: 0 for e in ENGS}
        self.all_dma = []
        self.autowarm = None

    def _add(self, op, reads, writes, locks=()):
        e = op.eng
        deps = []
        for lk in locks:
            la = lk.last_w
            if la is not None and la.eng != e:
                deps.append(la)
            lk.last_w = op
        for b in reads:
            if b.last_w is not None:
                deps.append(b.last_w)
        for b in writes:
            if b.last_w is not None and (b.last_w.is_dma or b.last_w.eng != e or op.is_dma):
                deps.append(b.last_w)
            for r in b.readers:
                if r.is_dma or r.eng != e or op.is_dma:
                    deps.append(r)
        out = []
        seen = set()
        latest = {}
        for d in deps:
            if d is op or id(d) in seen:
                continue
            if (not d.is_dma) and d.eng == "pe" and e == "pe" and not op.is_dma:
                continue
            seen.add(id(d))
            if d.is_dma:
                out.append(d)
            else:
                cur = latest.get(d.eng)
                if cur is None or d.idx > cur.idx:
                    latest[d.eng] = d
        out.extend(latest.values())
        op.deps = out
        op.idx = len(self.ops[e])
        for b in writes:
            b.last_w = op
            b.readers = []
        for b in reads:
            if b.last_w is not op:
                if op.is_dma:
                    b.readers.append(op)
                else:
                    b.readers = [r for r in b.readers if r.is_dma or r.eng != e]
                    b.readers.append(op)
        self.ops[e].append(op)
        return op

    def op(self, eng, emit, reads=(), writes=(), locks=()):
        return self._add(Op(eng, emit, False), reads, writes, locks)

    def dma(self, eng, emit, reads=(), writes=()):
        op = Op(eng, emit, True)
        k = self.dma_count[eng]
        self.dma_count[eng] += 1
        op.sem = (eng, k % NDS)
        op.val = 16 * (k // NDS + 1)
        self.all_dma.append(op)
        return self._add(op, reads, writes)

    def emit_all(self, block, sems):
        for e in ENGS:
            for op in self.ops[e]:
                for d in op.deps:
                    if not d.is_dma:
                        d.inc = True
        for e in ENGS:
            c = 0
            for op in self.ops[e]:
                if not op.is_dma and op.inc:
                    c += 1
                    op.count = c
        import os as _o
        if _o.environ.get("KDEBUG"):
            for e in ENGS:
                print("ENG", e, "nops", len(self.ops[e]), "incs", sum(1 for op in self.ops[e] if (not op.is_dma) and op.inc), "dmas", self.dma_count[e])
        last_dma = {}
        for op in self.all_dma:
            last_dma[op.sem] = max(last_dma.get(op.sem, 0), op.val)
        sched = self

        def run(e, eng):
            seen = {}
            for op in sched.ops[e]:
                waits = []
                for d in op.deps:
                    if d.is_dma:
                        waits.append((d.sem, d.val))
                    else:
                        waits.append(((d.eng, "g", (d.count - 1) // GEN), (d.count - 1) % GEN + 1))
                if op.is_dma and op.val > 16:
                    waits.append((op.sem, op.val - 16))
                first_wait = True
                for key, val in waits:
                    if seen.get(key, 0) >= val:
                        continue
                    seen[key] = val
                    if first_wait and e == "pe" and sched.autowarm is not None:
                        for _ in range(AUTOWARM):
                            sched.autowarm(eng)
                    first_wait = False
                    eng.wait_ge(sems[key], val)
                ins = op.emit(eng)
                if op.is_dma:
                    ins.then_inc(sems[op.sem], 16)
                elif op.inc:
                    ins.then_inc(sems[(e, "g", (op.count - 1) // GEN)], 1)
            if e == "sp":
                for key, val in last_dma.items():
                    if seen.get(key, 0) < val:
                        eng.wait_ge(sems[key], val)

        @block.tensor
        def _(eng):
            run("pe", eng)

        @block.scalar
        def _(eng):
            run("act", eng)

        @block.vector
        def _(eng):
            run("dve", eng)

        @block.gpsimd
        def _(eng):
            run("pool", eng)

        @block.sync
        def _(eng):
            run("sp", eng)


class T:
    def __init__(self, t, name):
        self.t = t
        self.b = Buf(name)

    def __getitem__(self, k):
        return self.t[k]


class V(T):
    def __init__(self, ap, name, host):
        self.t = ap
        self.b = Buf(name)
        self.host = host
        if not hasattr(host, "views"):
            host.views = []
        host.views.append(self)


class PS:
    def __init__(self, t, off, bufs, lock=None):
        self.t = t
        self.off = off
        self.bufs = bufs
        self.lock = lock

    def ap(self, c0, c1, p0=0, p1=128):
        return self.t[p0:p1, self.off + c0:self.off + c1]


def _locks(*lists):
    out = []
    for xs in lists:
        for x in xs:
            if isinstance(x, PS) and x.lock is not None and x.lock not in out:
                out.append(x.lock)
    return out


def _hosts(*lists):
    out = []
    for xs in lists:
        for x in xs:
            if isinstance(x, V):
                out.append(x.host.b)
    return out


def _bufs(xs):
    out = []
    for x in xs:
        if isinstance(x, T):
            out.append(x.b)
            for v in getattr(x, "views", ()):
                out.append(v.b)
        elif isinstance(x, PS):
            out.extend(x.bufs)
        elif isinstance(x, Buf):
            out.append(x)
        else:
            raise TypeError(type(x))
    return out


def build_nc():
    nc = bass.Bass("TRN2", target_bir_lowering=False)

    def din(name, shape):
        return nc.dram_tensor(name, list(shape), F32, kind="ExternalInput").ap()

    def dout(name, shape):
        return nc.dram_tensor(name, list(shape), F32, kind="ExternalOutput").ap()

    xp = din("xp", [T_PROMPT, D])
    xs = din("xs", [T_S, D])
    cpr = din("cp", [D])
    csm = din("cs", [D])
    sgla = din("sgla", [4, 128, 256])
    sgdn = din("sgdn", [8, 128, 128])
    cconv = din("cconv", [3, 3072])
    w_ada = din("w_ada", [D, 3 * D])
    b_ada = din("b_ada", [3 * D])
    g1 = din("g1", [D])
    w_in = din("w_in", [D, D_IN])
    w_gk2 = din("w_gk2", [16, 512])
    b_gk = din("b_gk", [512])
    w_conv = din("w_conv", [4, 3072])
    a_log = din("a_log", [8])
    dt_bias = din("dt_bias", [8])
    gna = din("gna", [256])
    gnb = din("gnb", [128])
    w_pa = din("w_pa", [D, D])
    w_pb = din("w_pb", [D, D])
    w_out = din("w_out", [D, D])
    gfin = din("gfin", [D])
    yp = dout("yp", [T_PROMPT, D])
    ys = dout("ys", [T_S, D])
    o_gla = {"p": dout("o_gla_p", [4, 128, 256]), "s": dout("o_gla_s", [4, 128, 256])}
    o_gdn = {"p": dout("o_gdn_p", [8, 128, 128]), "s": dout("o_gdn_s", [8, 128, 128])}
    o_conv = {"p": dout("o_conv_p", [3, 3072]), "s": dout("o_conv_s", [3, 3072])}

    es = ExitStack()
    with es:
        S = Sched()

        def sb(name, shape, dt=F32):
            return T(es.enter_context(nc.sbuf_tensor(name, list(shape), dt)), name)

        banks = [es.enter_context(nc.psum_tensor(f"bank{i}", [128, 512], F32)) for i in range(8)]
        qbufs = [[Buf(f"ps{i}_{q}") for q in range(4)] for i in range(8)]
        pptr = [0]

        blocks = [Buf(f"lock{i}") for i in range(8)]

        def palloc(n=1):
            b = pptr[0] % 7
            pptr[0] += 1
            return PS(banks[b], 0, [qbufs[b][0]], blocks[b])

        def chk(rd, wr, *aps):
            lk = _locks(rd, wr)
            for a in aps:
                if a is None or isinstance(a, (int, float)):
                    continue
                nm = a.name
                if nm.startswith("bank"):
                    assert blocks[int(nm[4:])] in lk, f"undeclared PSUM access {nm}"

        import os as _osw
        NWARM = int(_osw.environ.get("NWARM", "3"))

        def warm(n=None):
            for _ in range(NWARM if n is None else n):
                S.op("pe", lambda e: e.matmul(banks[7][:, 0:512], lhsT=identb[:], rhs=Wbig[:, 0, 0:512], start=True, stop=True), (), ())

        if AUTOWARM > 0:
            S.autowarm = lambda e: e.matmul(banks[7][:, 0:512], lhsT=identb[:], rhs=Wbig[:, 0, 0:512], start=True, stop=True)

        def mm(dst, lhsT, rhs, rd, wr, start=True, stop=True):
            chk(rd, wr, dst, lhsT, rhs)
            S.op("pe", lambda e: e.matmul(dst, lhsT=lhsT, rhs=rhs, start=start, stop=stop), _bufs(rd) + _hosts(rd, wr), _bufs(wr), _locks(rd, wr))

        def ACT(out, in_, func, rd, wr, scale=None, bias=None, accum=None):
            chk(rd, wr, out, in_, scale, bias, accum)
            kw = {}
            if scale is not None:
                kw["scale"] = scale
            if bias is not None:
                kw["bias"] = bias
            if accum is not None:
                kw["accum_out"] = accum
            S.op("act", lambda e: e.activation(out=out, in_=in_, func=func, **kw), _bufs(rd) + _hosts(rd, wr), _bufs(wr), _locks(rd, wr))

        def TS(eng, out, in0, s1, s2, op0, op1, rd, wr):
            chk(rd, wr, out, in0, s1, s2)
            if op1 is None:
                S.op(eng, lambda e: e.tensor_scalar(out=out, in0=in0, scalar1=s1, scalar2=None, op0=op0), _bufs(rd) + _hosts(rd, wr), _bufs(wr), _locks(rd, wr))
            else:
                S.op(eng, lambda e: e.tensor_scalar(out=out, in0=in0, scalar1=s1, scalar2=s2, op0=op0, op1=op1), _bufs(rd) + _hosts(rd, wr), _bufs(wr), _locks(rd, wr))

        def TT(eng, out, in0, in1, op, rd, wr):
            chk(rd, wr, out, in0, in1)
            S.op(eng, lambda e: e.tensor_tensor(out=out, in0=in0, in1=in1, op=op), _bufs(rd) + _hosts(rd, wr), _bufs(wr), _locks(rd, wr))

        def STT(out, in0, sc, in1, op0, op1, rd, wr):
            chk(rd, wr, out, in0, sc, in1)
            S.op("dve", lambda e: e.scalar_tensor_tensor(out=out, in0=in0, scalar=sc, in1=in1, op0=op0, op1=op1), _bufs(rd) + _hosts(rd, wr), _bufs(wr), _locks(rd, wr))

        def CP(eng, out, in_, rd, wr):
            chk(rd, wr, out, in_)
            if eng == "act":
                S.op("act", lambda e: e.copy(out=out, in_=in_), _bufs(rd) + _hosts(rd, wr), _bufs(wr), _locks(rd, wr))
            else:
                S.op(eng, lambda e: e.tensor_copy(out=out, in_=in_), _bufs(rd) + _hosts(rd, wr), _bufs(wr), _locks(rd, wr))

        def MEMSET(eng, ap, val, wr):
            S.op(eng, lambda e: e.memset(ap, val), _hosts(wr), _bufs(wr))

        def DMA(q, out, in_, rd, wr, slow=False):
            if slow:
                S.dma(q, lambda e: e.dma_start(out=out, in_=in_, allow_slow_non_contiguous=True), _bufs(rd) + _hosts(rd, wr), _bufs(wr))
            else:
                S.dma(q, lambda e: e.dma_start(out=out, in_=in_), _bufs(rd) + _hosts(rd, wr), _bufs(wr))

        def POW(out, in0, in1, rd, wr):
            S.op("pool", lambda e: e.tensor_tensor(out=out, in0=in0, in1=in1, op=ALU.pow), _bufs(rd), _bufs(wr))

        Wbig = es.enter_context(nc.sbuf_tensor("Wbig", [128, 8, 4096], BF16))
        WA = T(Wbig, "WA")
        WB = T(Wbig, "WB")
        WBUF = [(WA, 0), (WB, 2048)]
        wsm = sb("wsm", [128, 8, 32], BF16)
        hT = [sb(f"hT{j}", [128, 8, 128], BF16) for j in range(NSLOT)]
        OA = [sb(f"OA{j}", [128, 1024], BF16) for j in range(NSLOT)]
        OB = [sb(f"OB{j}", [128, 1024], BF16) for j in range(NSLOT)]
        identf = sb("identf", [128, 128])
        identb = sb("identb", [128, 128], BF16)
        identr = sb("identr", [128, 128], F32R)
        M_le = sb("M_le", [128, 128])
        M_gt = sb("M_gt", [128, 128])
        BD_le = sb("BD_le", [128, 128])
        BD_ge = sb("BD_ge", [128, 128])
        BD_gt = sb("BD_gt", [128, 128])
        M_le_r = sb("M_le_r", [128, 128], F32R)
        M_gt_r = sb("M_gt_r", [128, 128], F32R)
        BD_le_r = sb("BD_le_r", [128, 128], F32R)
        BD_gt_r = sb("BD_gt_r", [128, 128], F32R)
        ones_r = sb("ones_r", [128, 128], F32R)
        onesf = sb("onesf", [128, 128])
        ones2 = sb("ones2", [128, 2], BF16)
        chunkind = sb("chunkind", [128, 2])
        valid_s = sb("valid_s", [128, 1])
        neghalf = sb("neghalf", [128, 16])
        poshalf = sb("poshalf", [128, 16])
        gate_bc1 = sb("gate_bc", [128, 1024])
        gate_bc = {"p": gate_bc1, "s": gate_bc1}
        gf_bc = sb("gf_bc", [128, 1024])
        gbc_a = sb("gbc_a", [128, 256])
        gbc_b = sb("gbc_b", [128, 128])
        wcv = sb("wcv", [128, 24, 4])
        negA = sb("negA", [128, 8])
        dtb = sb("dtb", [128, 8])
        a1 = {"p": sb("a1p", [128, 8]), "s": sb("a1s", [128, 8])}
        sh = {"p": sb("shp", [128, 8]), "s": sb("shs", [128, 8])}
        wgk_r = sb("wgk_r", [17, 512], F32R)
        glT = sb("glT", [17, 128], F32R)
        Sg = [sb(f"Sg{h}", [128, 256]) for h in range(4)]
        Sgb = [sb(f"Sgb{h}", [128, 256], BF16) for h in range(4)]
        Sd4 = [sb(f"Sd4_{i}", [128, 4, 128]) for i in range(2)]
        Sdb4 = [sb(f"Sdb4_{i}", [128, 4, 128], BF16) for i in range(2)]
        carry = sb("carry", [128, 24, 3])
        xt = sb("xt", [128, 1024])
        xt2 = xt
        big0 = sb("big0", [128, 1024])
        big1 = sb("big1", [128, 1024])
        xb = sb("xb", [128, 1024], BF16)
        mgb = xb
        FB = [sb(f"FB{i}", [128, 512]) for i in range(3)]
        HB = [sb(f"HBb{i}", [128, 512], BF16) for i in range(1)]
        Fq = [sb(f"Fq{i}", [128, 128]) for i in range(6)]
        ACTB2 = [[sb(f"ACTB{p}_{k}", [128, 4, 128], BF16) for k in range(3)] for p in range(2)]
        ACTB = ACTB2[0]
        Hq2 = [[V(ACTB2[p][i // 4].t[:, i % 4, :], f"Hq{p}_{i}", ACTB2[p][i // 4]) for i in range(12)] for p in range(2)]
        Hq = Hq2[0] + [sb(f"Hq{i}", [128, 128], BF16) for i in (12, 13)]
        RW = sb("RW", [128, 4, 131])
        FW = sb("FW", [128, 512])
        KES, KSD0, KSD1, VS, QKT, WT, U_ = [sb(n, [128, 4, 128], BF16) for n in ("KES", "KSD0", "KSD1", "VS", "QKT", "WT", "U_")]
        PRp, PTRp, XRp, QRp = [[sb(f"{n}{a}", [128, 2, 128], F32R) for a in range(2)] for n in ("PR", "PTR", "XR", "QR")]
        XFp = [sb(f"XF{a}", [128, 2, 128], BF16) for a in range(2)]
        r3 = lambda ap: ap.rearrange("p (h n) -> p h n", n=128)
        EE = V(r3(big0.t[:, 0:512]), "EE", big0)
        DL = V(r3(big0.t[:, 512:1024]), "DL", big0)
        UV = V(r3(big1.t[:, 0:512]), "UV", big1)
        OSB = V(r3(big1.t[:, 512:1024]), "OSB", big1)
        xtb = xt.t[:].bitcast(BF16)
        KS_TM = V(r3(xtb[:, 0:512]), "KS_TM", xt)
        KST = V(r3(xtb[:, 512:1024]), "KST", xt)
        QKSD = V(r3(xtb[:, 1024:1536]), "QKSD", xt)
        ZS2 = [V(big0.t[:, 0:512], "ZS2_0", big0), V(big0.t[:, 512:1024], "ZS2_1", big0)]
        T1G = V(xb.t[:].bitcast(F32), "T1G", xb)
        VB2 = [HB[0], V(xtb[:, 1536:2048], "VB2_1", xt)]
        gkr = sb("gkr", [128, 256], F32R)
        sm2 = [[sb(f"sm{p}_{i}", [128, 16]) for i in range(12)] for p in range(2)]
        smr2 = [[sb(f"smr{p}_{i}", [128, 16], F32R) for i in range(2)] for p in range(2)]
        sm, smr = sm2[0], smr2[0]
        ssq = [sb(f"ssq{i}", [128, 4]) for i in range(4)]
        junk = sb("junk", [128, 256], BF16)
        oT = sb("oT", [128, 8, 128], BF16)

        sems = {}
        for e in ("pe", "act", "dve", "pool"):
            for g in range(NGEN):
                sems[(e, "g", g)] = es.enter_context(nc.semaphore(f"s_{e}_{g}"))
        for e in ("sp", "pool"):
            for i in range(NDS):
                sems[(e, i)] = es.enter_context(nc.semaphore(f"d_{e}_{i}"))
        block = es.enter_context(nc.Block())

        def mask(dst, pattern, cm, cmp_op, base=0):
            MEMSET("pool", dst[:], 1.0, [dst])
            S.op("pool", lambda e: e.affine_select(out=dst[:], in_=dst[:], pattern=pattern, compare_op=cmp_op,
                                                   fill=0.0, base=base, channel_multiplier=cm), _bufs([dst]), _bufs([dst]))

        mask(identf, [[-1, 128]], 1, ALU.is_equal)
        mask(M_le, [[1, 128]], -1, ALU.is_ge)
        mask(M_gt, [[-1, 128]], 1, ALU.is_gt)
        mask(BD_le, [[1, 128]], -1, ALU.is_ge)
        MEMSET("pool", BD_le[0:64, 64:128], 0.0, [BD_le])
        mask(BD_ge, [[-1, 128]], 1, ALU.is_ge)
        MEMSET("pool", BD_ge[64:128, 0:64], 0.0, [BD_ge])
        mask(BD_gt, [[-1, 128]], 1, ALU.is_gt)
        MEMSET("pool", BD_gt[64:128, 0:64], 0.0, [BD_gt])
        MEMSET("pool", onesf[:], 1.0, [onesf])
        MEMSET("pool", chunkind[:], 0.0, [chunkind])
        MEMSET("pool", chunkind[0:64, 0:1], 1.0, [chunkind])
        MEMSET("pool", chunkind[64:128, 1:2], 1.0, [chunkind])
        MEMSET("pool", valid_s[:], 0.0, [valid_s])
        MEMSET("pool", valid_s[0:16, :], 1.0, [valid_s])
        MEMSET("pool", neghalf[:], -0.5, [neghalf])
        MEMSET("pool", poshalf[:], 0.5, [poshalf])
        CP("dve", identb[:], identf[:], [identf], [identb])
        CP("dve", identr[:], identf[:], [identf], [identr])
        CP("dve", M_le_r[:], M_le[:], [M_le], [M_le_r])
        CP("dve", M_gt_r[:], M_gt[:], [M_gt], [M_gt_r])
        CP("dve", BD_le_r[:], BD_le[:], [BD_le], [BD_le_r])
        CP("dve", BD_gt_r[:], BD_gt[:], [BD_gt], [BD_gt_r])
        CP("dve", ones_r[:], onesf[:], [onesf], [ones_r])
        CP("dve", ones2[:], onesf[:, 0:2], [onesf], [ones2])
        CP("dve", glT[:], onesf[0:17, :], [onesf], [glT])
        for u in ACTB2[0] + ACTB2[1] + Hq[12:] + HB + [U_]:
            MEMSET("pool", u[:], 0.0, [u])

        DMA("sp", gf_bc[:], gfin.partition_broadcast(128), [], [gf_bc])
        DMA("sp", gbc_a[:], gna.partition_broadcast(128), [], [gbc_a])
        DMA("sp", gbc_b[:], gnb.partition_broadcast(128), [], [gbc_b])
        TS("dve", gbc_a[:], gbc_a[:], 0.5, None, ALU.mult, None, [gbc_a], [gbc_a])
        TS("dve", gbc_b[:], gbc_b[:], 0.5, None, ALU.mult, None, [gbc_b], [gbc_b])
        DMA("sp", negA[:], a_log.partition_broadcast(128), [], [negA])
        DMA("sp", dtb[:], dt_bias.partition_broadcast(128), [], [dtb])
        ACT(negA[:], negA[:], AF.Exp, [negA], [negA])
        TS("dve", negA[:], negA[:], -1.0, None, ALU.mult, None, [negA], [negA])
        for c in range(24):
            DMA("sp", wcv[:, c, :], w_conv[:, c * 128:(c + 1) * 128].rearrange("i p -> p i"), [], [wcv], slow=True)
        DMA("sp", xt[0:16, 0:512], w_gk2, [], [xt])
        DMA("sp", xt[16:17, 0:512], b_gk.rearrange("(o n) -> o n", o=1), [], [xt])
        CP("dve", wgk_r[:], xt[0:17, 0:512], [xt], [wgk_r])
        DMA("pool", wsm[:, :, 0:16], w_in[:, C_GL:C_GL + 16].rearrange("(c p) n -> p c n", p=128), [], [wsm])
        DMA("pool", wsm[:, :, 16:32], w_in[:, C_BETA:C_BETA + 16].rearrange("(c p) n -> p c n", p=128), [], [wsm])

        for g in range(3):
            DMA("pool", Wbig[:, :, g * 1024:(g + 1) * 1024], w_ada[:, g * 1024:(g + 1) * 1024].rearrange("(c p) n -> p c n", p=128), [], [WA, WB])
        c2 = sb("c2", [128, 8, 2])
        c2b = sb("c2b", [128, 8, 2], BF16)
        c2t = sb("c2t", [128, 8, 2])
        DMA("sp", c2[:, :, 0], cpr.rearrange("(c p) -> p c", p=128), [], [c2], slow=True)
        DMA("sp", c2[:, :, 1], csm.rearrange("(c p) -> p c", p=128), [], [c2], slow=True)
        ACT(c2t[:], c2[:], AF.Tanh, [c2], [c2t], scale=0.5)
        STT(c2t[:], c2t[:], 1.0, c2[:], ALU.add, ALU.mult, [c2t, c2], [c2t])
        TS("dve", c2b[:], c2t[:], 0.5, None, ALU.mult, None, [c2t], [c2b])
        badT = sb("badT", [128, 24])
        g1T = sb("g1T", [128, 8])
        DMA("sp", badT[:], b_ada.rearrange("(c p) -> p c", p=128), [], [badT], slow=True)
        DMA("sp", g1T[:], g1.rearrange("(c p) -> p c", p=128), [], [g1T], slow=True)
        pm = palloc(1)
        for n in range(24):
            for kc in range(8):
                mm(pm.ap(2 * n, 2 * n + 2), Wbig[:, kc, n * 128:(n + 1) * 128], c2b[:, kc, :], [WA, WB, c2b], [pm], start=(kc == 0), stop=(kc == 7))
        modT = sb("modT", [128, 24, 2])
        pmv = pm.ap(0, 48).rearrange("p (n k) -> p n k", k=2)
        for k in range(2):
            TT("dve", modT[:, :, k], pmv[:, :, k], badT[:], ALU.add, [pm, badT], [modT])
        for k, kind in enumerate(("p", "s")):
            STT(a1[kind][:], modT[:, 8:16, k], 1.0, g1T[:], ALU.add, ALU.mult, [modT, g1T], [a1[kind]])
            CP("dve", sh[kind][:], modT[:, 0:8, k], [modT], [sh[kind]])
        def build_gate_bc(kind):
            k = 0 if kind == "p" else 1
            for g in range(2):
                pr = palloc()
                for c in range(4):
                    cc = g * 4 + c
                    gcol = Fq[c % 2]
                    TS("dve", gcol[:], onesf[:], modT[:, 16 + cc, k:k + 1], None, ALU.mult, None, [onesf, modT], [gcol])
                    mm(pr.ap(c * 128, (c + 1) * 128), gcol[:], identf[:], [gcol, identf], [pr])
                CP("act", gate_bc1[:, g * 512:(g + 1) * 512], pr.ap(0, 512), [pr], [gate_bc1])

        def xsrc(kind, ti):
            return xs if kind == "s" else xp[ti * 128:(ti + 1) * 128, :]

        def load_w(wt_off, col0, src, scol, n):
            wt, off = wt_off
            for c0 in range(0, n, 512):
                m = min(512, n - c0)
                DMA("pool", Wbig[:, :, off + col0 + c0:off + col0 + c0 + m],
                    src[:, scol + c0:scol + c0 + m].rearrange("(c p) n -> p c n", p=128), [], [wt])

        def rstd_from(ssq_ap, rd, out_t, mult, eps):
            n = ssq_ap.shape[1]
            TS("dve", out_t[:, 0:n], ssq_ap, mult, eps, ALU.mult, ALU.add, rd, [out_t])
            POW(out_t[:, 0:n], out_t[:, 0:n], neghalf[:, 0:n], [out_t, neghalf], [out_t])
            return out_t

        def stage_h(tiles):
            XB = [xt, big0]
            for j, (kind, ti) in enumerate(tiles):
                xin = XB[j % 2]
                if kind == "s":
                    MEMSET("dve", xin[:], 0.0, [xin])
                    DMA("sp", xin[0:16, :], xs, [], [xin])
                else:
                    DMA("sp", xin[:], xsrc(kind, ti), [], [xin])
                ACT(xb[:], xin[:], AF.Square, [xin], [xb, ssq[0]], accum=ssq[0][:, 0:1])
                r = rstd_from(ssq[0][:, 0:1], [ssq[0]], sm[0], 1.0 / D, EPS)
                TS("dve", xb[:], xin[:], r[:, 0:1], None, ALU.mult, None, [xin, r], [xb])
                for half in range(2):
                    pt = palloc(4)
                    for c in range(4):
                        cc = half * 4 + c
                        mm(pt.ap(c * 128, (c + 1) * 128), xb[:, cc * 128:(cc + 1) * 128], identb[:], [xb, identb], [pt])
                    for c in range(4):
                        cc = half * 4 + c
                        if c % 2 == 0:
                            ACT(hT[j][:, cc, :], pt.ap(c * 128, (c + 1) * 128), AF.Identity, [pt, a1[kind], sh[kind]], [hT[j]],
                                scale=a1[kind][:, cc:cc + 1], bias=sh[kind][:, cc:cc + 1])
                        else:
                            TS("dve", hT[j][:, cc, :], pt.ap(c * 128, (c + 1) * 128), a1[kind][:, cc:cc + 1], sh[kind][:, cc:cc + 1],
                               ALU.mult, ALU.add, [pt, a1[kind], sh[kind]], [hT[j]])

        def proj_tm(j, wt_off, col0, ncols, dst):
            wt, off = wt_off
            for kc in range(8):
                mm(dst.ap(0, ncols), hT[j][:, kc, :], Wbig[:, kc, off + col0:off + col0 + ncols], [hT[j], wt], [dst], start=(kc == 0), stop=(kc == 7))

        def proj_fm(j, wt_off, col0, dst_ap, dst):
            wt, off = wt_off
            for kc in range(8):
                mm(dst_ap, Wbig[:, kc, off + col0:off + col0 + 128], hT[j][:, kc, :], [hT[j], wt], [dst], start=(kc == 0), stop=(kc == 7))

        def state_io_gla(kind, heads, load):
            for h in heads:
                if load:
                    if kind == "s":
                        DMA("sp", Sg[h][:], sgla[h], [], [Sg[h]])
                    else:
                        MEMSET("dve", Sg[h][:], 0.0, [Sg[h]])
                    CP("act", Sgb[h][:], Sg[h][:], [Sg[h]], [Sgb[h]])
                else:
                    DMA("sp", o_gla[kind][h], Sg[h][:], [Sg[h]], [])

        def stage_g(tiles, pair, wt_off, first_pass, last_pass, barrier=True):
            heads = (2 * pair, 2 * pair + 1)
            sc_q = 128.0 ** -0.5

            def front(j):
                kind, ti = tiles[j]
                P = j % 2
                hq, smp = Hq2[P], sm2[P]
                qtil, qdec, ktil, kbf, kdec, attm = hq[0:2], hq[2:4], hq[4:6], hq[6:8], hq[8:10], hq[10:12]
                vbf, zs = VB2[P], ZS2[P]
                pg1 = palloc()
                for kc in range(8):
                    mm(pg1.ap(0, 128, 0, 16), wsm[:, kc, 0:16], hT[j][:, kc, :], [wsm, hT[j]], [pg1], start=(kc == 0), stop=(kc == 7))
                CP("act", glT[0:16, :], pg1.ap(0, 128, 0, 16), [pg1], [glT])
                pg2 = palloc()
                mm(pg2.ap(0, 256), glT[:], wgk_r[:, pair * 256:(pair + 1) * 256], [glT, wgk_r], [pg2])
                e0 = FB[0]
                ACT(e0[:, 0:256], pg2.ap(0, 256), AF.Exp, [pg2], [e0], scale=-1.0)
                ACT(e0[:, 0:256], e0[:, 0:256], AF.Ln, [e0], [e0], bias=1.0)
                yield
                pv = palloc()
                proj_tm(j, wt_off, 512, 512, pv)
                CP("act", vbf[:], pv.ap(0, 512), [pv], [vbf])
                if kind == "s":
                    TS("dve", gkr[:], e0[:, 0:256], -1.0 / 16.0, valid_s[:, 0:1], ALU.mult, ALU.mult, [e0, valid_s], [gkr])
                else:
                    TS("dve", gkr[:], e0[:, 0:256], -1.0 / 16.0, None, ALU.mult, None, [e0], [gkr])
                yield
                pb = palloc()
                for hh in range(2):
                    mm(pb.ap(hh * 128, (hh + 1) * 128), gkr[:, hh * 128:(hh + 1) * 128], M_le_r[:], [gkr, M_le_r], [pb])
                prv = palloc()
                mm(prv.ap(0, 256), M_gt_r[:], gkr[:], [gkr, M_gt_r], [prv])
                bref, nbref, ebl = smp[1], smp[2], smp[5]
                E1, E2, E3 = Fq[0:2], Fq[2:4], Fq[4:6]
                for hh in range(2):
                    CP("act", bref[:, hh:hh + 1], pb.ap(hh * 128 + 63, hh * 128 + 64), [pb], [bref])
                    ACT(E3[hh][:], pb.ap(hh * 128, (hh + 1) * 128), AF.Exp, [pb], [E3[hh]])
                E4 = FB[1]
                ACT(E4[:, 0:256], prv.ap(0, 256), AF.Exp, [prv], [E4])
                TS("dve", nbref[:, 0:2], bref[:, 0:2], -1.0, None, ALU.mult, None, [bref], [nbref])
                if kind == "s":
                    TS("dve", E4[:, 0:256], E4[:, 0:256], valid_s[:, 0:1], None, ALU.mult, None, [E4, valid_s], [E4])
                for hh in range(2):
                    ACT(E1[hh][:], pb.ap(hh * 128, (hh + 1) * 128), AF.Exp, [pb, nbref], [E1[hh]], bias=nbref[:, hh:hh + 1])
                    ACT(E2[hh][:], pb.ap(hh * 128, (hh + 1) * 128), AF.Exp, [pb, bref], [E2[hh]], scale=-1.0, bias=bref[:, hh:hh + 1])
                    CP("dve", ebl[:, hh:hh + 1], E3[hh][:, 127:128], [E3[hh]], [ebl])
                yield
                pq = palloc()
                pk = palloc()
                for hh in range(2):
                    proj_fm(j, wt_off, hh * 128, pq.ap(hh * 128, (hh + 1) * 128), pq)
                    proj_fm(j, wt_off, 256 + hh * 128, pk.ap(hh * 128, (hh + 1) * 128), pk)
                for hh in range(2):
                    STT(qtil[hh][:], pq.ap(hh * 128, (hh + 1) * 128), sc_q, E1[hh][:], ALU.mult, ALU.mult, [pq, E1[hh]], [qtil[hh]])
                    STT(qdec[hh][:], pq.ap(hh * 128, (hh + 1) * 128), sc_q, E3[hh][:], ALU.mult, ALU.mult, [pq, E3[hh]], [qdec[hh]])
                    CP("act", kbf[hh][:], pk.ap(hh * 128, (hh + 1) * 128), [pk], [kbf[hh]])
                    TT("dve", ktil[hh][:], pk.ap(hh * 128, (hh + 1) * 128), E2[hh][:], ALU.mult, [pk, E2[hh]], [ktil[hh]])
                yield
                pkt = palloc()
                for hh in range(2):
                    mm(pkt.ap(hh * 128, (hh + 1) * 128), kbf[hh][:], identb[:], [kbf[hh], identb], [pkt])
                pa = palloc()
                for hh in range(2):
                    mm(pa.ap(hh * 128, (hh + 1) * 128), ktil[hh][:], qtil[hh][:], [ktil[hh], qtil[hh]], [pa])
                for hh in range(2):
                    TT("dve", kdec[hh][:], pkt.ap(hh * 128, (hh + 1) * 128), E4[:, hh * 128:(hh + 1) * 128], ALU.mult, [pkt, E4], [kdec[hh]])
                    TT("dve", attm[hh][:], pa.ap(hh * 128, (hh + 1) * 128), M_le[:], ALU.mult, [pa, M_le], [attm[hh]])
                yield
                pz = palloc()
                proj_tm(j, wt_off, 1024, 512, pz)
                tz = FB[2]
                ACT(tz[:], pz.ap(0, 512), AF.Tanh, [pz], [tz], scale=0.5)
                STT(zs[:], tz[:], 1.0, pz.ap(0, 512), ALU.add, ALU.mult, [tz, pz], [zs])
                yield

            def back(j):
                kind, ti = tiles[j]
                P = j % 2
                hq, smp = Hq2[P], sm2[P]
                qdec, kdec, attm = hq[2:4], hq[8:10], hq[10:12]
                vbf, zs, ebl = VB2[P], ZS2[P], smp[5]
                if kind == "s" or (kind == "p" and ti == 0):
                    state_io_gla(kind, heads, True)
                pos = []
                for hh in range(2):
                    h = heads[hh]
                    pS = palloc()
                    mm(pS.ap(0, 256), kdec[hh][:], vbf[:, hh * 256:(hh + 1) * 256], [kdec[hh], vbf], [pS])
                    po = palloc()
                    pos.append(po)
                    mm(po.ap(0, 256), attm[hh][:], vbf[:, hh * 256:(hh + 1) * 256], [attm[hh], vbf], [po], start=True, stop=False)
                    mm(po.ap(0, 256), qdec[hh][:], Sgb[h][:], [qdec[hh], Sgb[h]], [po], start=False, stop=True)
                    STT(Sgb[h][:], Sg[h][:], ebl[:, hh:hh + 1], pS.ap(0, 256), ALU.mult, ALU.add, [Sg[h], ebl, pS], [Sgb[h]])
                    STT(Sg[h][:], Sg[h][:], ebl[:, hh:hh + 1], pS.ap(0, 256), ALU.mult, ALU.add, [Sg[h], ebl, pS], [Sg[h]])
                    ACT(junk[:, 0:256], po.ap(0, 256), AF.Square, [po], [junk, ssq[1]], accum=ssq[1][:, hh:hh + 1])
                r = rstd_from(ssq[1][:, 0:2], [ssq[1]], smp[3], 1.0 / 256.0, EPS)
                yield
                t1 = T1G
                for hh in range(2):
                    h = heads[hh]
                    STT(t1[:, hh * 256:(hh + 1) * 256], pos[hh].ap(0, 256), r[:, hh:hh + 1], gbc_a[:], ALU.mult, ALU.mult, [pos[hh], r, gbc_a], [t1])
                    TT("dve", OA[j][:, h * 256:(h + 1) * 256], t1[:, hh * 256:(hh + 1) * 256], zs[:, hh * 256:(hh + 1) * 256], ALU.mult, [t1, zs], [OA[j]])
                if kind == "s" or (last_pass and j == len(tiles) - 1):
                    state_io_gla(kind, heads, False)
                yield

            n = len(tiles)

            def prologue():
                if barrier:
                    S.op("dve", lambda e: e.memset(junk[:, 0:2], 0.0), (), _bufs([junk, big0, big1, xt, xb] + ACTB2[0] + ACTB2[1]))
                yield from front(0)

            def body(next_pro=None):
                for j in range(n):
                    gens = [back(j)]
                    if j + 1 < n:
                        gens.append(front(j + 1))
                    elif next_pro is not None:
                        gens.append(next_pro)
                    while gens:
                        for g in list(gens):
                            try:
                                next(g)
                            except StopIteration:
                                gens.remove(g)

            return prologue, body

        def state_io_gdn(kind, half, load):
            for hh in range(4):
                h = 4 * half + hh
                if load:
                    if kind == "s":
                        DMA("sp", Sd4[half][:, hh, :], sgdn[h], [], [Sd4[half]])
                    else:
                        MEMSET("dve", Sd4[half][:, hh, :], 0.0, [Sd4[half]])
                else:
                    DMA("sp", o_gdn[kind][h], Sd4[half][:, hh, :], [Sd4[half]], [])
            if load:
                CP("act", Sdb4[half][:], Sd4[half][:], [Sd4[half]], [Sdb4[half]])

        def bc4(ap):
            return ap.unsqueeze(2).to_broadcast([ap.shape[0], 4, 128])

        def bcm(t):
            return t[:].unsqueeze(1).to_broadcast([128, 4, 128])

        def p3(ps_, p0=0, p1=128):
            return r3(ps_.ap(0, 512, p0, p1))

        def stage_d(tiles, half, wt_off, first_pass, last_pass, barrier=True):
            heads = list(range(4 * half, 4 * half + 4))
            gch = [[kind_i * 8 + h for h in heads] for kind_i in range(3)]
            Sd_, Sdb_ = Sd4[half], Sdb4[half]
            hs = slice(4 * half, 4 * half + 4)

            def front(j):
                kind, ti = tiles[j]
                P = j % 2
                smp, smrp, actb, hq = sm2[P], smr2[P], ACTB2[P], Hq2[P]
                nvalid = 16 if kind == "s" else 128
                if kind == "s" or (kind == "p" and ti == 0):
                    for grp in gch:
                        for c in grp:
                            if kind == "s":
                                DMA("sp", carry[:, c, :], cconv[:, c * 128:(c + 1) * 128].rearrange("i p -> p i"), [], [carry], slow=True)
                            else:
                                MEMSET("dve", carry[:, c, :], 0.0, [carry])
                psm = palloc()
                for kc in range(8):
                    mm(psm.ap(0, 16), hT[j][:, kc, :], wsm[:, kc, 16:32], [hT[j], wsm], [psm], start=(kc == 0), stop=(kc == 7))
                beta, sqb, gpre, g_r = smp[0], smp[1], smp[2], smrp[0]
                ACT(beta[:, 0:8], psm.ap(0, 8), AF.Tanh, [psm], [beta], scale=0.5)
                CP("act", gpre[:, 0:8], psm.ap(8, 16), [psm], [gpre])
                yield
                TS("dve", beta[:, 0:8], beta[:, 0:8], 0.5, 0.5, ALU.mult, ALU.add, [beta], [beta])
                TT("dve", gpre[:, 0:8], gpre[:, 0:8], dtb[:], ALU.add, [gpre, dtb], [gpre])
                yield
                POW(sqb[:, 0:8], beta[:, 0:8], poshalf[:, 0:8], [beta, poshalf], [sqb])
                ACT(gpre[:, 0:8], gpre[:, 0:8], AF.Exp, [gpre], [gpre])
                ACT(gpre[:, 0:8], gpre[:, 0:8], AF.Ln, [gpre], [gpre], bias=1.0)
                yield
                TT("dve", g_r[:, 0:8], gpre[:, 0:8], negA[:], ALU.mult, [gpre, negA], [g_r])
                if kind == "s":
                    TS("dve", g_r[:, 0:8], g_r[:, 0:8], valid_s[:, 0:1], None, ALU.mult, None, [g_r, valid_s], [g_r])
                gm = smrp[1]
                for c in range(2):
                    TS("dve", gm[:, c * 8:(c + 1) * 8], g_r[:, 0:8], chunkind[:, c:c + 1], None, ALU.mult, None, [g_r, chunkind], [gm])
                yield
                pcs = palloc()
                mm(pcs.ap(0, 8), BD_le_r[:], g_r[:, 0:8], [BD_le_r, g_r], [pcs])
                mm(pcs.ap(8, 16), BD_gt_r[:], g_r[:, 0:8], [BD_gt_r, g_r], [pcs])
                ebb = smp[3]
                ACT(ebb[:, 0:16], pcs.ap(0, 16), AF.Exp, [pcs], [ebb])
                pdl = palloc()
                mm(pdl.ap(0, 16), ones_r[:], gm[:, 0:16], [ones_r, gm], [pdl])
                dlast = smp[4]
                ACT(dlast[:, 0:16], pdl.ap(0, 16), AF.Exp, [pdl], [dlast])
                yield
                for kind_i in range(3):
                    pp = palloc()
                    for hh in range(4):
                        proj_fm(j, wt_off, kind_i * 512 + hh * 128, pp.ap(hh * 128, (hh + 1) * 128), pp)
                    c0 = gch[kind_i][0]
                    CP("dve", RW[:, :, 0:3], carry[:, c0:c0 + 4, :], [carry], [RW])
                    CP("act", RW[:, :, 3:131], p3(pp), [pp], [RW])
                    yield
                    y3, t3, ty3 = r3(FW[:]), r3(FB[1][:]), r3(FB[2][:])
                    TT("dve", y3, RW[:, :, 0:128], bc4(wcv[:, c0:c0 + 4, 0]), ALU.mult, [RW, wcv], [FW])
                    for tap in range(1, 4):
                        TT("dve", t3, RW[:, :, tap:tap + 128], bc4(wcv[:, c0:c0 + 4, tap]), ALU.mult, [RW, wcv], [FB[1]])
                        TT("dve", y3, y3, t3, ALU.add, [FW, FB[1]], [FW])
                        if tap == 2:
                            yield
                    CP("dve", carry[:, c0:c0 + 4, :], RW[:, :, nvalid:nvalid + 3], [RW], [carry])
                    ACT(ty3, y3, AF.Tanh, [FW], [FB[2]], scale=0.5)
                    yield
                    STT(actb[kind_i][:], ty3, 1.0, y3, ALU.add, ALU.mult, [FB[2], FW], [actb[kind_i]])
                if kind == "s" or (last_pass and j == len(tiles) - 1):
                    for grp in gch:
                        for c in grp:
                            DMA("sp", o_conv[kind][:, c * 128:(c + 1) * 128].rearrange("i p -> p i"), carry[:, c, :], [carry], [], slow=True)
                yield
                pss = palloc()
                for kind_i in range(2):
                    for hh in range(4):
                        sq = Hq[12 + (hh % 2)]
                        src = hq[kind_i * 4 + hh]
                        ACT(sq[:], src[:], AF.Square, [src], [sq])
                        cidx = (kind_i * 4 + hh) * 2
                        mm(pss.ap(cidx, cidx + 2), sq[:], ones2[:], [sq, ones2], [pss])
                rqk = smp[5]
                CP("act", rqk[:, 0:16], pss.ap(0, 16), [pss], [rqk])
                yield
                pssv = rqk[:, 0:16].rearrange("p (n k) -> p n k", k=2)[:, :, 0]
                TS("dve", smp[0][:, 8:16], pssv, 1.0, 4.0 * EPS, ALU.mult, ALU.add, [rqk], [smp[0]])
                CP("dve", rqk[:, 0:8], smp[0][:, 8:16], [smp[0]], [rqk])
                POW(rqk[:, 0:8], rqk[:, 0:8], neghalf[:, 0:8], [rqk, neghalf], [rqk])
                yield
                c_qs, c_o1, ck, ckes, ckd0, ckd1, cv = smp[6], smp[7], smp[8], smp[9], smp[10], smp[11], smp[2]
                TS("dve", c_qs[:, 0:4], rqk[:, 0:4], 128.0 ** -0.5, None, ALU.mult, None, [rqk], [c_qs])
                TT("dve", c_o1[:, 0:4], c_qs[:, 0:4], ebb[:, hs], ALU.mult, [c_qs, ebb], [c_o1])
                TT("dve", ck[:, 0:4], rqk[:, 4:8], sqb[:, hs], ALU.mult, [rqk, sqb], [ck])
                TT("dve", ckes[:, 0:4], ck[:, 0:4], ebb[:, hs], ALU.mult, [ck, ebb], [ckes])
                ebl = ebb[:, 8 + 4 * half:8 + 4 * half + 4]
                for c, ckd in enumerate((ckd0, ckd1)):
                    STT(ckd[:, 0:4], ck[:, 0:4], chunkind[:, c:c + 1], ebl, ALU.mult, ALU.mult, [ck, chunkind, ebb], [ckd])
                    if kind == "s":
                        TS("dve", ckd[:, 0:4], ckd[:, 0:4], valid_s[:, 0:1], None, ALU.mult, None, [ckd, valid_s], [ckd])
                TS("dve", cv[:, 0:4], sqb[:, hs], 0.5, None, ALU.mult, None, [sqb], [cv])
                yield

            def back(j):
                kind, ti = tiles[j]
                P = j % 2
                smp, smrp, hq = sm2[P], smr2[P], Hq2[P]
                g_r, ebb, dlast = smrp[0], smp[3], smp[4]
                c_qs, c_o1, ck, ckes, ckd0, ckd1, cv = smp[6], smp[7], smp[8], smp[9], smp[10], smp[11], smp[2]
                qs = [hq[hh] for hh in range(4)]
                k0 = [hq[4 + hh] for hh in range(4)]
                v0 = [hq[8 + hh] for hh in range(4)]
                if kind == "s" or (kind == "p" and ti == 0):
                    state_io_gdn(kind, half, True)
                pz = palloc()
                proj_tm(j, wt_off, 1536, 512, pz)
                zs = FB[0]
                ACT(zs[:], pz.ap(0, 512), AF.Tanh, [pz], [zs], scale=0.5)
                STT(zs[:], zs[:], 1.0, pz.ap(0, 512), ALU.add, ALU.mult, [zs, pz], [zs])
                yield
                for a in range(2):
                    ga_ = g_r[:, 4 * half + 2 * a:4 * half + 2 * a + 2]
                    TT("dve", QRp[a][:], M_gt[:].unsqueeze(1).to_broadcast([128, 2, 128]), ga_.unsqueeze(2).to_broadcast([128, 2, 128]), ALU.mult, [M_gt, g_r], [QRp[a]])
                pdf = palloc()
                for hh in range(4):
                    mm(pdf.ap(hh * 128, (hh + 1) * 128), M_le_r[:], QRp[hh // 2][:, hh % 2, :], [M_le_r, QRp[hh // 2]], [pdf])
                ACT(EE[:], p3(pdf), AF.Exp, [pdf], [EE])
                TT("dve", DL[:], EE[:], bcm(BD_ge), ALU.mult, [EE, BD_ge], [DL])
                TT("dve", EE[:], EE[:], bcm(BD_gt), ALU.mult, [EE, BD_gt], [EE])
                pkt = palloc()
                pvt = palloc()
                for hh in range(4):
                    mm(pkt.ap(hh * 128, (hh + 1) * 128), k0[hh][:], identb[:], [k0[hh], identb], [pkt])
                for hh in range(4):
                    mm(pvt.ap(hh * 128, (hh + 1) * 128), v0[hh][:], identb[:], [v0[hh], identb], [pvt])
                TT("dve", KS_TM[:], p3(pkt), bc4(ck[:, 0:4]), ALU.mult, [pkt, ck], [KS_TM])
                TT("dve", KES[:], p3(pkt), bc4(ckes[:, 0:4]), ALU.mult, [pkt, ckes], [KES])
                TT("dve", KSD0[:], p3(pkt), bc4(ckd0[:, 0:4]), ALU.mult, [pkt, ckd0], [KSD0])
                TT("dve", KSD1[:], p3(pkt), bc4(ckd1[:, 0:4]), ALU.mult, [pkt, ckd1], [KSD1])
                TT("dve", VS[:], p3(pvt), bc4(cv[:, 0:4]), ALU.mult, [pvt, cv], [VS])
                yield
                pkT = palloc()
                for hh in range(4):
                    mm(pkT.ap(hh * 128, (hh + 1) * 128), KS_TM[:, hh, :], identb[:], [KS_TM, identb], [pkT])
                CP("act", KST[:], p3(pkT), [pkT], [KST])
                yield
                pkk = palloc()
                pqk = palloc()
                for hh in range(4):
                    mm(pkk.ap(hh * 128, (hh + 1) * 128), KST[:, hh, :], KST[:, hh, :], [KST], [pkk])
                for hh in range(4):
                    mm(pqk.ap(hh * 128, (hh + 1) * 128), qs[hh][:], KST[:, hh, :], [qs[hh], KST], [pqk])
                for a in range(2):
                    TT("dve", PTRp[a][:], r3(pkk.ap(a * 256, a * 256 + 256)), EE[:, 2 * a:2 * a + 2, :], ALU.mult, [pkk, EE], [PTRp[a]])
                TT("dve", DL[:], DL[:], bc4(c_qs[:, 0:4]), ALU.mult, [DL, c_qs], [DL])
                TT("dve", QKSD[:], p3(pqk), DL[:], ALU.mult, [pqk, DL], [QKSD])
                yield
                pqT = palloc()
                for hh in range(4):
                    mm(pqT.ap(hh * 128, (hh + 1) * 128), QKSD[:, hh, :], identb[:], [QKSD, identb], [pqT])
                pUs = []
                for a in range(2):
                    pU = palloc()
                    pUs.append(pU)
                    for h2 in range(2):
                        mm(pU.ap(h2 * 128, (h2 + 1) * 128), PTRp[a][:, h2, :], identr[:], [PTRp[a], identr], [pU])
                bcm2 = lambda t: t[:].unsqueeze(1).to_broadcast([128, 2, 128])
                p2 = lambda ps_: r3(ps_.ap(0, 256))
                for a in range(2):
                    CP("act", PRp[a][:], p2(pUs[a]), [pUs[a]], [PRp[a]])
                    TT("dve", XRp[a][:], bcm2(identf), p2(pUs[a]), ALU.subtract, [identf, pUs[a]], [XRp[a]])
                CP("act", QKT[:], p3(pqT), [pqT], [QKT])
                yield
                def x_mm(lvl):
                    pXs = [None, None]
                    for a in range(2):
                        pXs[a] = palloc()
                        for h2 in range(2):
                            mm(pXs[a].ap(h2 * 128, (h2 + 1) * 128), QRp[a][:, h2, :], XRp[a][:, h2, :], [QRp[a], XRp[a]], [pXs[a]])
                    return pXs

                def x_evac(lvl, pXs):
                    for a in range(2):
                        xeng = "act" if a == 0 else "dve"
                        if lvl < 5:
                            CP(xeng, XRp[a][:], p2(pXs[a]), [pXs[a]], [XRp[a]])
                        else:
                            CP(xeng, XFp[a][:], p2(pXs[a]), [pXs[a]], [XFp[a]])

                for lvl in range(1, 6):
                    pPs, pPTs = [None, None], [None, None]
                    for a in range(2):
                        if lvl < 5:
                            pPs[a] = palloc()
                            for h2 in range(2):
                                mm(pPs[a].ap(h2 * 128, (h2 + 1) * 128), PTRp[a][:, h2, :], PRp[a][:, h2, :], [PTRp[a], PRp[a]], [pPs[a]])
                        pPTs[a] = palloc()
                        for h2 in range(2):
                            mm(pPTs[a].ap(h2 * 128, (h2 + 1) * 128), PRp[a][:, h2, :], PTRp[a][:, h2, :], [PRp[a], PTRp[a]], [pPTs[a]])
                    pXprev = x_mm(lvl - 1) if lvl > 1 else None
                    for a in range(2):
                        if lvl < 5:
                            CP("act", PRp[a][:], p2(pPs[a]), [pPs[a]], [PRp[a]])
                            CP("act", PTRp[a][:], p2(pPTs[a]), [pPTs[a]], [PTRp[a]])
                    for a in range(2):
                        TT("dve", QRp[a][:], p2(pPTs[a]), bcm2(identf), ALU.add, [pPTs[a], identf], [QRp[a]])
                    if pXprev is not None:
                        x_evac(lvl - 1, pXprev)
                    yield
                x_evac(5, x_mm(5))
                pw = palloc()
                for hh in range(4):
                    mm(pw.ap(hh * 128, (hh + 1) * 128), KES[:, hh, :], XFp[hh // 2][:, hh % 2, :], [KES, XFp[hh // 2]], [pw])
                puv = palloc()
                for hh in range(4):
                    mm(puv.ap(hh * 128, (hh + 1) * 128), XFp[hh // 2][:, hh % 2, :], VS[:, hh, :], [XFp[hh // 2], VS], [puv])
                CP("act", WT[:], p3(pw), [pw], [WT])
                CP("act", UV[:], p3(puv), [puv], [UV])
                yield
                for c in range(2):
                    r0, r1 = 64 * c, 64 * c + 64
                    pws = palloc()
                    for hh in range(4):
                        mm(pws.ap(hh * 128, (hh + 1) * 128, r0, r1), WT[:, hh, r0:r1], Sdb_[:, hh, :], [WT, Sdb_], [pws])
                    po1 = palloc()
                    for hh in range(4):
                        mm(po1.ap(hh * 128, (hh + 1) * 128, r0, r1), qs[hh][:, r0:r1], Sdb_[:, hh, :], [qs[hh], Sdb_], [po1])
                    TT("dve", U_[r0:r1, :, :], UV[r0:r1, :, :], p3(pws, r0, r1), ALU.subtract, [UV, pws], [U_])
                    dl_c = dlast[:, c * 8 + 4 * half:c * 8 + 4 * half + 4]
                    TT("dve", Sd_[:], Sd_[:], bc4(dl_c), ALU.mult, [Sd_, dlast], [Sd_])
                    pS = palloc()
                    KSD = KSD0 if c == 0 else KSD1
                    for hh in range(4):
                        mm(pS.ap(hh * 128, (hh + 1) * 128), KSD[:, hh, :], U_[:, hh, :], [KSD, U_], [pS])
                    po2 = palloc()
                    for hh in range(4):
                        mm(po2.ap(hh * 128, (hh + 1) * 128, r0, r1), QKT[:, hh, r0:r1], U_[:, hh, :], [QKT, U_], [po2])
                    TT("dve", Sdb_[:], Sd_[:], p3(pS), ALU.add, [Sd_, pS], [Sdb_])
                    TT("dve", Sd_[:], Sd_[:], p3(pS), ALU.add, [Sd_, pS], [Sd_])
                    TT("dve", OSB[r0:r1, :, :], p3(po1, r0, r1), bc4(c_o1[r0:r1, 0:4]), ALU.mult, [po1, c_o1], [OSB])
                    TT("dve", OSB[r0:r1, :, :], OSB[r0:r1, :, :], p3(po2, r0, r1), ALU.add, [OSB, po2], [OSB])
                    yield
                for hh in range(4):
                    ACT(junk[:, 0:128], OSB[:, hh, :], AF.Square, [OSB], [junk, ssq[2]], accum=ssq[2][:, hh:hh + 1])
                rt = ssq[3]
                TS("dve", rt[:, 0:4], ssq[2][:, 0:4], 1.0 / 128.0, EPS, ALU.mult, ALU.add, [ssq[2]], [rt])
                POW(rt[:, 0:4], rt[:, 0:4], neghalf[:, 0:4], [rt, neghalf], [rt])
                yield
                TT("dve", OSB[:], OSB[:], bc4(rt[:, 0:4]), ALU.mult, [OSB, rt], [OSB])
                TT("dve", OSB[:], OSB[:], bcm(gbc_b), ALU.mult, [OSB, gbc_b], [OSB])
                TT("dve", r3(OB[j][:, half * 512:(half + 1) * 512]), OSB[:], r3(zs[:]), ALU.mult, [OSB, zs], [OB[j]])
                if kind == "s" or (last_pass and j == len(tiles) - 1):
                    state_io_gdn(kind, half, False)
                yield

            n = len(tiles)

            def prologue():
                yield from front(0)

            def body(next_pro=None):
                if barrier:
                    S.op("dve", lambda e: e.memset(junk[:, 0:2], 0.0), (), _bufs([junk, big0, big1, xt, xb]))
                for j in range(n):
                    gens = [back(j)]
                    if j + 1 < n:
                        gens.append(front(j + 1))
                    elif next_pro is not None:
                        gens.append(next_pro)
                    while gens:
                        for g in list(gens):
                            try:
                                next(g)
                            except StopIteration:
                                gens.remove(g)

            return prologue, body

        def transpose_1024(src_ap_fn, rd, dst):
            for half in range(2):
                pt = palloc(4)
                for c in range(4):
                    cc = half * 4 + c
                    mm(pt.ap(c * 128, (c + 1) * 128), src_ap_fn(cc), identb[:], rd + [identb], [pt])
                if half == 0:
                    CP("act", dst[:, 0:4, :], pt.ap(0, 512).rearrange("p (c n) -> p c n", n=128), [pt], [dst])
                else:
                    CP("dve", dst[:, 4:8, :], pt.ap(0, 512).rearrange("p (c n) -> p c n", n=128), [pt], [dst])

        def stage_m12(tiles, which, wt_off):
            wt, off = wt_off
            for j, (kind, ti) in enumerate(tiles):
                src = OA[j] if which == 0 else OB[j]
                transpose_1024(lambda cc: src[:, cc * 128:(cc + 1) * 128], [src], oT)
                sg = big0
                pp_sb = big1
                for g in range(2):
                    pg = palloc(4)
                    proj_tm(j, wt_off, g * 512, 512, pg)
                    ACT(sg[:, g * 512:(g + 1) * 512], pg.ap(0, 512), AF.Tanh, [pg], [sg], scale=0.5)
                TS("dve", sg[:], sg[:], 0.5, 0.5, ALU.mult, ALU.add, [sg], [sg])
                for g in range(2):
                    pp = palloc(4)
                    for kc in range(8):
                        mm(pp.ap(0, 512), oT[:, kc, :], Wbig[:, kc, off + 1024 + g * 512:off + 1024 + (g + 1) * 512], [oT, wt], [pp], start=(kc == 0), stop=(kc == 7))
                    if which == 0:
                        TT("dve", OA[j][:, g * 512:(g + 1) * 512], sg[:, g * 512:(g + 1) * 512], pp.ap(0, 512), ALU.mult, [sg, pp], [OA[j]])
                    else:
                        TT("dve", pp_sb[:, g * 512:(g + 1) * 512], sg[:, g * 512:(g + 1) * 512], pp.ap(0, 512), ALU.mult, [sg, pp], [pp_sb])
                if which == 1:
                    TT("dve", mgb[:], pp_sb[:], OA[j][:], ALU.add, [pp_sb, OA[j]], [mgb])
                    dstv = OB[j].t[:].rearrange("p (c n) -> p c n", n=128)
                    for half in range(2):
                        pt = palloc(4)
                        for c in range(4):
                            cc = half * 4 + c
                            mm(pt.ap(c * 128, (c + 1) * 128), mgb[:, cc * 128:(cc + 1) * 128], identb[:], [mgb, identb], [pt])
                        CP("act" if half == 0 else "dve", dstv[:, half * 4:half * 4 + 4, :], pt.ap(0, 512).rearrange("p (c n) -> p c n", n=128), [pt], [OB[j]])

        def stage_m3(tiles, wt_off):
            wt, off = wt_off
            XB = [xt, big0]
            for j, (kind, ti) in enumerate(tiles):
                if kind == "s" or (kind == "p" and ti == 0):
                    build_gate_bc(kind)
                mT = OB[j].t[:].rearrange("p (c n) -> p c n", n=128)
                xn = XB[j % 2]
                if kind == "s":
                    MEMSET("dve", xn[:], 0.0, [xn])
                    DMA("sp", xn[0:16, :], xs, [], [xn])
                else:
                    DMA("sp", xn[:], xsrc(kind, ti), [], [xn])
                gp = big1
                for g in range(2):
                    po = palloc(4)
                    for kc in range(8):
                        mm(po.ap(0, 512), mT[:, kc, :], Wbig[:, kc, off + g * 512:off + (g + 1) * 512], [OB[j], wt], [po], start=(kc == 0), stop=(kc == 7))
                    TT("dve", gp[:, g * 512:(g + 1) * 512], po.ap(0, 512), gate_bc[kind][:, g * 512:(g + 1) * 512], ALU.mult, [po, gate_bc[kind]], [gp])
                TT("dve", xn[:], xn[:], gp[:], ALU.add, [xn, gp], [xn])
                ACT(mgb[:], xn[:], AF.Square, [xn], [mgb, ssq[0]], accum=ssq[0][:, 1:2])
                rt = ssq[3]
                TS("dve", rt[:, 3:4], ssq[0][:, 1:2], 1.0 / D, EPS, ALU.mult, ALU.add, [ssq[0]], [rt])
                POW(rt[:, 3:4], rt[:, 3:4], neghalf[:, 0:1], [rt, neghalf], [rt])
                STT(xn[:], xn[:], rt[:, 3:4], gf_bc[:], ALU.mult, ALU.mult, [xn, rt, gf_bc], [xn])
                if kind == "s":
                    DMA("sp", ys, xn[0:16, :], [xn], [])
                else:
                    DMA("sp", yp[ti * 128:(ti + 1) * 128, :], xn[:], [xn], [])

        def lw_g(pair, wo):
            load_w(wo, 0, w_in, C_QA + pair * 256, 256)
            load_w(wo, 256, w_in, C_KA + pair * 256, 256)
            load_w(wo, 512, w_in, C_VA + pair * 512, 512)
            load_w(wo, 1024, w_in, C_ZA + pair * 512, 512)

        def lw_d(half, wo):
            for kind_i in range(3):
                load_w(wo, kind_i * 512, w_in, C_QKV + kind_i * 1024 + half * 512, 512)
            load_w(wo, 1536, w_in, C_ZB + half * 512, 512)

        def lw_m(which, wo):
            load_w(wo, 0, w_in, C_GA if which == 0 else C_GB, 1024)
            load_w(wo, 1024, w_pa if which == 0 else w_pb, 0, 1024)

        def lw_o(wo):
            load_w(wo, 0, w_out, 0, 1024)

        stage_list = []
        for ps_ in range(NPASS):
            tiles = [("p", ps_ * TPP + i) for i in range(TPP)]
            if ps_ == 0:
                tiles = [("s", 0)] + tiles
            fp, lp = (ps_ == 0), (ps_ == NPASS - 1)
            ev = (len(tiles) % 2 == 0)
            stage_list.append(("gen", lambda wo: lw_g(0, wo), lambda wo, t=tiles, f=fp, l=lp: stage_g(t, 0, wo, f, l, True), tiles, ev))
            stage_list.append(("gen", lambda wo: lw_g(1, wo), lambda wo, t=tiles, f=fp, l=lp: stage_g(t, 1, wo, f, l, False), None, ev))
            stage_list.append(("gen", lambda wo: lw_d(0, wo), lambda wo, t=tiles, f=fp, l=lp: stage_d(t, 0, wo, f, l, True), None, ev))
            stage_list.append(("gen", lambda wo: lw_d(1, wo), lambda wo, t=tiles, f=fp, l=lp: stage_d(t, 1, wo, f, l, False), None, False))
            stage_list.append(("run", lambda wo: lw_m(0, wo), lambda wo, t=tiles: stage_m12(t, 0, wo), None, False))
            stage_list.append(("run", lambda wo: lw_m(1, wo), lambda wo, t=tiles: stage_m12(t, 1, wo), None, False))
            stage_list.append(("run", lambda wo: lw_o(wo), lambda wo, t=tiles: stage_m3(t, wo), None, False))

        import os as _os
        _stop = int(_os.environ.get("KSTOP", "1000"))
        stage_list[0][1](WBUF[0])
        started = None
        for si, (typ, lw, mk, htiles, overlap_next) in enumerate(stage_list):
            if si >= _stop:
                break
            if si + 1 < len(stage_list):
                stage_list[si + 1][1](WBUF[(si + 1) % 2])
            if htiles is not None:
                stage_h(htiles)
            if typ == "run":
                mk(WBUF[si % 2])
                started = None
                continue
            if started is None:
                pro, body = mk(WBUF[si % 2])
                for _ in pro():
                    pass
            else:
                body = started
            nxt = None
            started = None
            if overlap_next and si + 1 < min(len(stage_list), _stop) and stage_list[si + 1][0] == "gen":
                pro2, body2 = stage_list[si + 1][2](WBUF[(si + 1) % 2])
                nxt = pro2()
                started = body2
            body(nxt)

        S.emit_all(block, sems)
    return nc


_NC_CACHE = {}


def kernel(x_prompt, x_sample, c_prompt, c_sample, state_gla, state_gdn, cache_conv_gdn,
           w_ada, b_ada, g_norm1, w_in, w_gk2, b_gk, w_conv, a_log, dt_bias,
           g_norm_a, g_norm_b, w_pa, w_pb, w_out, g_final):
    f = lambda a: np.ascontiguousarray(np.asarray(a, dtype=np.float32))
    if "nc" not in _NC_CACHE:
        _NC_CACHE["nc"] = build_nc()
    nc = _NC_CACHE["nc"]
    shared = {
        "w_ada": f(w_ada[0]), "b_ada": f(b_ada[0]), "g1": f(g_norm1[0]), "w_in": f(w_in[0]),
        "w_gk2": f(w_gk2[0]), "b_gk": f(b_gk[0]), "w_conv": f(w_conv[0]), "a_log": f(a_log[0]),
        "dt_bias": f(dt_bias[0]), "gna": f(g_norm_a[0]), "gnb": f(g_norm_b[0]),
        "w_pa": f(w_pa[0]), "w_pb": f(w_pb[0]), "w_out": f(w_out[0]), "gfin": f(g_final),
    }
    in_maps = []
    for b in range(8):
        m = dict(shared)
        m.update({
            "xp": f(x_prompt[b]), "xs": f(x_sample[b]), "cp": f(c_prompt[b]), "cs": f(c_sample[b]),
            "sgla": f(state_gla[0, b]), "sgdn": f(state_gdn[0, b]), "cconv": f(cache_conv_gdn[0, b]),
        })
        in_maps.append(m)
    res = run_bass_kernel_spmd(nc, in_maps, core_ids=list(range(8)))
    R = res.results
    st = lambda k: np.stack([np.asarray(R[b][k], dtype=np.float32) for b in range(8)])
    return (st("yp"), st("ys"), st("o_gla_p")[None], st("o_gdn_p")[None], st("o_conv_p")[None],
            st("o_gla_s")[None], st("o_gdn_s")[None], st("o_conv_s")[None])
```

```python
import numpy as np
from contextlib import ExitStack
import concourse.bass as bass
import concourse.mybir as mybir
from concourse.bass_utils import run_bass_kernel_spmd

F32 = mybir.dt.float32
F32R = mybir.dt.float32r
BF16 = mybir.dt.bfloat16
AF = mybir.ActivationFunctionType
ALU = mybir.AluOpType

ENGS = ("pe", "act", "dve", "pool", "sp")
NDS = 12
import os as _os0
AUTOWARM = int(_os0.environ.get("AUTOWARM", "0"))
GEN = 2000
NGEN = 16

D = 1024
T_PROMPT = 4096
NT = T_PROMPT // 128
T_S = 16
D_IN = 9248
EPS = 1e-6
TPP = 8
NPASS = NT // TPP
NSLOT = TPP + 1

C_QA, C_KA, C_VA, C_ZA, C_GL = 0, 512, 1024, 2048, 3072
C_QKV, C_ZB, C_BETA, C_AIN, C_GA, C_GB = 3088, 6160, 7184, 7192, 7200, 8224


class Buf:
    __slots__ = ("name", "last_w", "readers")

    def __init__(self, name):
        self.name = name
        self.last_w = None
        self.readers = []


class Op:
    __slots__ = ("eng", "emit", "deps", "is_dma", "inc", "count", "sem", "val", "idx")

    def __init__(self, eng, emit, is_dma):
        self.eng = eng
        self.emit = emit
        self.deps = []
        self.is_dma = is_dma
        self.inc = False
        self.count = None
        self.sem = None
        self.val = None


class Sched:
    def __init__(self):
        self.ops = {e: [] for e in ENGS}
        self.dma_count = {e: 0 for e in ENGS}
        self.all_dma = []
        self.autowarm = None

    def _add(self, op, reads, writes, locks=()):
        e = op.eng
        deps = []
        for lk in locks:
            la = lk.last_w
            if la is not None and la.eng != e:
                deps.append(la)
            lk.last_w = op
        for b in reads:
            if b.last_w is not None:
                deps.append(b.last_w)
        for b in writes:
            if b.last_w is not None and (b.last_w.is_dma or b.last_w.eng != e or op.is_dma):
                deps.append(b.last_w)
            for r in b.readers:
                if r.is_dma or r.eng != e or op.is_dma:
                    deps.append(r)
        out = []
        seen = set()
        latest = {}
        for d in deps:
            if d is op or id(d) in seen:
                continue
            if (not d.is_dma) and d.eng == "pe" and e == "pe" and not op.is_dma:
                continue
            seen.add(id(d))
            if d.is_dma:
                out.append(d)
            else:
                cur = latest.get(d.eng)
                if cur is None or d.idx > cur.idx:
                    latest[d.eng] = d
        out.extend(latest.values())
        op.deps = out
        op.idx = len(self.ops[e])
        for b in writes:
            b.last_w = op
            b.readers = []
        for b in reads:
            if b.last_w is not op:
                if op.is_dma:
                    b.readers.append(op)
                else:
                    b.readers = [r for r in b.readers if r.is_dma or r.eng != e]
                    b.readers.append(op)
        self.ops[e].append(op)
        return op

    def op(self, eng, emit, reads=(), writes=(), locks=()):
        return self._add(Op(eng, emit, False), reads, writes, locks)

    def dma(self, eng, emit, reads=(), writes=()):
        op = Op(eng, emit, True)
        k = self.dma_count[eng]
        self.dma_count[eng] += 1
        op.sem = (eng, k % NDS)
        op.val = 16 * (k // NDS + 1)
        self.all_dma.append(op)
        return self._add(op, reads, writes)

    def emit_all(self, block, sems):
        for e in ENGS:
            for op in self.ops[e]:
                for d in op.deps:
                    if not d.is_dma:
                        d.inc = True
        for e in ENGS:
            c = 0
            for op in self.ops[e]:
                if not op.is_dma and op.inc:
                    c += 1
                    op.count = c
        import os as _o
        if _o.environ.get("KDEBUG"):
            for e in ENGS:
                print("ENG", e, "nops", len(self.ops[e]), "incs", sum(1 for op in self.ops[e] if (not op.is_dma) and op.inc), "dmas", self.dma_count[e])
        last_dma = {}
        for op in self.all_dma:
            last_dma[op.sem] = max(last_dma.get(op.sem, 0), op.val)
        sched = self

        def run(e, eng):
            seen = {}
            for op in sched.ops[e]:
                waits = []
                for d in op.deps:
                    if d.is_dma:
                        waits.append((d.sem, d.val))
                    else:
                        waits.append(((d.eng, "g", (d.count - 1) // GEN), (d.count - 1) % GEN + 1))
                if op.is_dma and op.val > 16:
                    waits.append((op.sem, op.val - 16))
                first_wait = True
                for key, val in waits:
                    if seen.get(key, 0) >= val:
                        continue
                    seen[key] = val
                    if first_wait and e == "pe" and sched.autowarm is not None:
                        for _ in range(AUTOWARM):
                            sched.autowarm(eng)
                    first_wait = False
                    eng.wait_ge(sems[key], val)
                ins = op.emit(eng)
                if op.is_dma:
                    ins.then_inc(sems[op.sem], 16)
                elif op.inc:
                    ins.then_inc(sems[(e, "g", (op.count - 1) // GEN)], 1)
            if e == "sp":
                for key, val in last_dma.items():
                    if seen.get(key, 0) < val:
                        eng.wait_ge(sems[key], val)

        @block.tensor
        def _(eng):
            run("pe", eng)

        @block.scalar
        def _(eng):
            run("act", eng)

        @block.vector
        def _(eng):
            run("dve", eng)

        @block.gpsimd
        def _(eng):
            run("pool", eng)

        @block.sync
        def _(eng):
            run("sp", eng)


class T:
    def __init__(self, t, name):
        self.t = t
        self.b = Buf(name)

    def __getitem__(self, k):
        return self.t[k]


class V(T):
    def __init__(self, ap, name, host):
        self.t = ap
        self.b = Buf(name)
        self.host = host
        if not hasattr(host, "views"):
            host.views = []
        host.views.append(self)


class PS:
    def __init__(self, t, off, bufs, lock=None):
        self.t = t
        self.off = off
        self.bufs = bufs
        self.lock = lock

    def ap(self, c0, c1, p0=0, p1=128):
        return self.t[p0:p1, self.off + c0:self.off + c1]


def _locks(*lists):
    out = []
    for xs in lists:
        for x in xs:
            if isinstance(x, PS) and x.lock is not None and x.lock not in out:
                out.append(x.lock)
    return out


def _hosts(*lists):
    out = []
    for xs in lists:
        for x in xs:
            if isinstance(x, V):
                out.append(x.host.b)
    return out


def _bufs(xs):
    out = []
    for x in xs:
        if isinstance(x, T):
            out.append(x.b)
            for v in getattr(x, "views", ()):
                out.append(v.b)
        elif isinstance(x, PS):
            out.extend(x.bufs)
        elif isinstance(x, Buf):
            out.append(x)
        else:
            raise TypeError(type(x))
    return out


def build_nc():
    nc = bass.Bass("TRN2", target_bir_lowering=False)

    def din(name, shape):
        return nc.dram_tensor(name, list(shape), F32, kind="ExternalInput").ap()

    def dout(name, shape):
        return nc.dram_tensor(name, list(shape), F32, kind="ExternalOutput").ap()

    xp = din("xp", [T_PROMPT, D])
    xs = din("xs", [T_S, D])
    cpr = din("cp", [D])
    csm = din("cs", [D])
    sgla = din("sgla", [4, 128, 256])
    sgdn = din("sgdn", [8, 128, 128])
    cconv = din("cconv", [3, 3072])
    w_ada = din("w_ada", [D, 3 * D])
    b_ada = din("b_ada", [3 * D])
    g1 = din("g1", [D])
    w_in = din("w_in", [D, D_IN])
    w_gk2 = din("w_gk2", [16, 512])
    b_gk = din("b_gk", [512])
    w_conv = din("w_conv", [4, 3072])
    a_log = din("a_log", [8])
    dt_bias = din("dt_bias", [8])
    gna = din("gna", [256])
    gnb = din("gnb", [128])
    w_pa = din("w_pa", [D, D])
    w_pb = din("w_pb", [D, D])
    w_out = din("w_out", [D, D])
    gfin = din("gfin", [D])
    yp = dout("yp", [T_PROMPT, D])
    ys = dout("ys", [T_S, D])
    o_gla = {"p": dout("o_gla_p", [4, 128, 256]), "s": dout("o_gla_s", [4, 128, 256])}
    o_gdn = {"p": dout("o_gdn_p", [8, 128, 128]), "s": dout("o_gdn_s", [8, 128, 128])}
    o_conv = {"p": dout("o_conv_p", [3, 3072]), "s": dout("o_conv_s", [3, 3072])}

    es = ExitStack()
    with es:
        S = Sched()

        def sb(name, shape, dt=F32):
            return T(es.enter_context(nc.sbuf_tensor(name, list(shape), dt)), name)

        banks = [es.enter_context(nc.psum_tensor(f"bank{i}", [128, 512], F32)) for i in range(8)]
        qbufs = [[Buf(f"ps{i}_{q}") for q in range(4)] for i in range(8)]
        pptr = [0]

        blocks = [Buf(f"lock{i}") for i in range(8)]

        def palloc(n=1):
            b = pptr[0] % 7
            pptr[0] += 1
            return PS(banks[b], 0, [qbufs[b][0]], blocks[b])

        def chk(rd, wr, *aps):
            lk = _locks(rd, wr)
            for a in aps:
                if a is None or isinstance(a, (int, float)):
                    continue
                nm = a.name
                if nm.startswith("bank"):
                    assert blocks[int(nm[4:])] in lk, f"undeclared PSUM access {nm}"

        import os as _osw
        NWARM = int(_osw.environ.get("NWARM", "3"))

        def warm(n=None):
            for _ in range(NWARM if n is None else n):
                S.op("pe", lambda e: e.matmul(banks[7][:, 0:512], lhsT=identb[:], rhs=Wbig[:, 0, 0:512], start=True, stop=True), (), ())

        if AUTOWARM > 0:
            S.autowarm = lambda e: e.matmul(banks[7][:, 0:512], lhsT=identb[:], rhs=Wbig[:, 0, 0:512], start=True, stop=True)

        def mm(dst, lhsT, rhs, rd, wr, start=True, stop=True):
            chk(rd, wr, dst, lhsT, rhs)
            S.op("pe", lambda e: e.matmul(dst, lhsT=lhsT, rhs=rhs, start=start, stop=stop), _bufs(rd) + _hosts(rd, wr), _bufs(wr), _locks(rd, wr))

        def ACT(out, in_, func, rd, wr, scale=None, bias=None, accum=None):
            chk(rd, wr, out, in_, scale, bias, accum)
            kw = {}
            if scale is not None:
                kw["scale"] = scale
            if bias is not None:
                kw["bias"] = bias
            if accum is not None:
                kw["accum_out"] = accum
            S.op("act", lambda e: e.activation(out=out, in_=in_, func=func, **kw), _bufs(rd) + _hosts(rd, wr), _bufs(wr), _locks(rd, wr))

        def TS(eng, out, in0, s1, s2, op0, op1, rd, wr):
            chk(rd, wr, out, in0, s1, s2)
            if op1 is None:
                S.op(eng, lambda e: e.tensor_scalar(out=out, in0=in0, scalar1=s1, scalar2=None, op0=op0), _bufs(rd) + _hosts(rd, wr), _bufs(wr), _locks(rd, wr))
            else:
                S.op(eng, lambda e: e.tensor_scalar(out=out, in0=in0, scalar1=s1, scalar2=s2, op0=op0, op1=op1), _bufs(rd) + _hosts(rd, wr), _bufs(wr), _locks(rd, wr))

        def TT(eng, out, in0, in1, op, rd, wr):
            chk(rd, wr, out, in0, in1)
            S.op(eng, lambda e: e.tensor_tensor(out=out, in0=in0, in1=in1, op=op), _bufs(rd) + _hosts(rd, wr), _bufs(wr), _locks(rd, wr))

        def STT(out, in0, sc, in1, op0, op1, rd, wr):
            chk(rd, wr, out, in0, sc, in1)
            S.op("dve", lambda e: e.scalar_tensor_tensor(out=out, in0=in0, scalar=sc, in1=in1, op0=op0, op1=op1), _bufs(rd) + _hosts(rd, wr), _bufs(wr), _locks(rd, wr))

        def CP(eng, out, in_, rd, wr):
            chk(rd, wr, out, in_)
            if eng == "act":
                S.op("act", lambda e: e.copy(out=out, in_=in_), _bufs(rd) + _hosts(rd, wr), _bufs(wr), _locks(rd, wr))
            else:
                S.op(eng, lambda e: e.tensor_copy(out=out, in_=in_), _bufs(rd) + _hosts(rd, wr), _bufs(wr), _locks(rd, wr))

        def MEMSET(eng, ap, val, wr):
            S.op(eng, lambda e: e.memset(ap, val), _hosts(wr), _bufs(wr))

        def DMA(q, out, in_, rd, wr, slow=False):
            if slow:
                S.dma(q, lambda e: e.dma_start(out=out, in_=in_, allow_slow_non_contiguous=True), _bufs(rd) + _hosts(rd, wr), _bufs(wr))
            else:
                S.dma(q, lambda e: e.dma_start(out=out, in_=in_), _bufs(rd) + _hosts(rd, wr), _bufs(wr))

        def POW(out, in0, in1, rd, wr):
            S.op("pool", lambda e: e.tensor_tensor(out=out, in0=in0, in1=in1, op=ALU.pow), _bufs(rd), _bufs(wr))

        Wbig = es.enter_context(nc.sbuf_tensor("Wbig", [128, 8, 4096], BF16))
        WA = T(Wbig, "WA")
        WB = T(Wbig, "WB")
        WBUF = [(WA, 0), (WB, 2048)]
        wsm = sb("wsm", [128, 8, 32], BF16)
        hT = [sb(f"hT{j}", [128, 8, 128], BF16) for j in range(NSLOT)]
        OA = [sb(f"OA{j}", [128, 1024], BF16) for j in range(NSLOT)]
        OB = [sb(f"OB{j}", [128, 1024], BF16) for j in range(NSLOT)]
        identf = sb("identf", [128, 128])
        identb = sb("identb", [128, 128], BF16)
        identr = sb("identr", [128, 128], F32R)
        M_le = sb("M_le", [128, 128])
        M_gt = sb("M_gt", [128, 128])
        BD_le = sb("BD_le", [128, 128])
        BD_ge = sb("BD_ge", [128, 128])
        BD_gt = sb("BD_gt", [128, 128])
        M_le_r = sb("M_le_r", [128, 128], F32R)
        M_gt_r = sb("M_gt_r", [128, 128], F32R)
        BD_le_r = sb("BD_le_r", [128, 128], F32R)
        BD_gt_r = sb("BD_gt_r", [128, 128], F32R)
        ones_r = sb("ones_r", [128, 128], F32R)
        onesf = sb("onesf", [128, 128])
        ones2 = sb("ones2", [128, 2], BF16)
        chunkind = sb("chunkind", [128, 2])
        valid_s = sb("valid_s", [128, 1])
        neghalf = sb("neghalf", [128, 16])
        poshalf = sb("poshalf", [128, 16])
        gate_bc1 = sb("gate_bc", [128, 1024])
        gate_bc = {"p": gate_bc1, "s": gate_bc1}
        gf_bc = sb("gf_bc", [128, 1024])
        gbc_a = sb("gbc_a", [128, 256])
        gbc_b = sb("gbc_b", [128, 128])
        wcv = sb("wcv", [128, 24, 4])
        negA = sb("negA", [128, 8])
        dtb = sb("dtb", [128, 8])
        a1 = {"p": sb("a1p", [128, 8]), "s": sb("a1s", [128, 8])}
        sh = {"p": sb("shp", [128, 8]), "s": sb("shs", [128, 8])}
        wgk_r = sb("wgk_r", [17, 512], F32R)
        glT = sb("glT", [17, 128], F32R)
        Sg = [sb(f"Sg{h}", [128, 256]) for h in range(4)]
        Sgb = [sb(f"Sgb{h}", [128, 256], BF16) for h in range(4)]
        Sd4 = [sb(f"Sd4_{i}", [128, 4, 128]) for i in range(2)]
        Sdb4 = [sb(f"Sdb4_{i}", [128, 4, 128], BF16) for i in range(2)]
        carry = sb("carry", [128, 24, 3])
        xt = sb("xt", [128, 1024])
        xt2 = xt
        big0 = sb("big0", [128, 1024])
        big1 = sb("big1", [128, 1024])
        xb = sb("xb", [128, 1024], BF16)
        mgb = xb
        FB = [sb(f"FB{i}", [128, 512]) for i in range(3)]
        HB = [sb(f"HBb{i}", [128, 512], BF16) for i in range(1)]
        Fq = [sb(f"Fq{i}", [128, 128]) for i in range(6)]
        ACTB2 = [[sb(f"ACTB{p}_{k}", [128, 4, 128], BF16) for k in range(3)] for p in range(2)]
        ACTB = ACTB2[0]
        Hq2 = [[V(ACTB2[p][i // 4].t[:, i % 4, :], f"Hq{p}_{i}", ACTB2[p][i // 4]) for i in range(12)] for p in range(2)]
        Hq = Hq2[0] + [sb(f"Hq{i}", [128, 128], BF16) for i in (12, 13)]
        RW = sb("RW", [128, 4, 131])
        FW = sb("FW", [128, 512])
        KES, KSD0, KSD1, VS, QKT, WT, U_ = [sb(n, [128, 4, 128], BF16) for n in ("KES", "KSD0", "KSD1", "VS", "QKT", "WT", "U_")]
        PRp, PTRp, XRp, QRp = [[sb(f"{n}{a}", [128, 2, 128], F32R) for a in range(2)] for n in ("PR", "PTR", "XR", "QR")]
        XFp = [sb(f"XF{a}", [128, 2, 128], BF16) for a in range(2)]
        r3 = lambda ap: ap.rearrange("p (h n) -> p h n", n=128)
        EE = V(r3(big0.t[:, 0:512]), "EE", big0)
        DL = V(r3(big0.t[:, 512:1024]), "DL", big0)
        UV = V(r3(big1.t[:, 0:512]), "UV", big1)
        OSB = V(r3(big1.t[:, 512:1024]), "OSB", big1)
        xtb = xt.t[:].bitcast(BF16)
        KS_TM = V(r3(xtb[:, 0:512]), "KS_TM", xt)
        KST = V(r3(xtb[:, 512:1024]), "KST", xt)
        QKSD = V(r3(xtb[:, 1024:1536]), "QKSD", xt)
        ZS2 = [V(big0.t[:, 0:512], "ZS2_0", big0), V(big0.t[:, 512:1024], "ZS2_1", big0)]
        T1G = V(xb.t[:].bitcast(F32), "T1G", xb)
        VB2 = [HB[0], V(xtb[:, 1536:2048], "VB2_1", xt)]
        gkr = sb("gkr", [128, 256], F32R)
        sm2 = [[sb(f"sm{p}_{i}", [128, 16]) for i in range(12)] for p in range(2)]
        smr2 = [[sb(f"smr{p}_{i}", [128, 16], F32R) for i in range(2)] for p in range(2)]
        sm, smr = sm2[0], smr2[0]
        ssq = [sb(f"ssq{i}", [128, 4]) for i in range(4)]
        junk = sb("junk", [128, 256], BF16)
        oT = sb("oT", [128, 8, 128], BF16)

        sems = {}
        for e in ("pe", "act", "dve", "pool"):
            for g in range(NGEN):
                sems[(e, "g", g)] = es.enter_context(nc.semaphore(f"s_{e}_{g}"))
        for e in ("sp", "pool"):
            for i in range(NDS):
                sems[(e, i)] = es.enter_context(nc.semaphore(f"d_{e}_{i}"))
        block = es.enter_context(nc.Block())

        def mask(dst, pattern, cm, cmp_op, base=0):
            MEMSET("pool", dst[:], 1.0, [dst])
            S.op("pool", lambda e: e.affine_select(out=dst[:], in_=dst[:], pattern=pattern, compare_op=cmp_op,
                                                   fill=0.0, base=base, channel_multiplier=cm), _bufs([dst]), _bufs([dst]))

        mask(identf, [[-1, 128]], 1, ALU.is_equal)
        mask(M_le, [[1, 128]], -1, ALU.is_ge)
        mask(M_gt, [[-1, 128]], 1, ALU.is_gt)
        mask(BD_le, [[1, 128]], -1, ALU.is_ge)
        MEMSET("pool", BD_le[0:64, 64:128], 0.0, [BD_le])
        mask(BD_ge, [[-1, 128]], 1, ALU.is_ge)
        MEMSET("pool", BD_ge[64:128, 0:64], 0.0, [BD_ge])
        mask(BD_gt, [[-1, 128]], 1, ALU.is_gt)
        MEMSET("pool", BD_gt[64:128, 0:64], 0.0, [BD_gt])
        MEMSET("pool", onesf[:], 1.0, [onesf])
        MEMSET("pool", chunkind[:], 0.0, [chunkind])
        MEMSET("pool", chunkind[0:64, 0:1], 1.0, [chunkind])
        MEMSET("pool", chunkind[64:128, 1:2], 1.0, [chunkind])
        MEMSET("pool", valid_s[:], 0.0, [valid_s])
        MEMSET("pool", valid_s[0:16, :], 1.0, [valid_s])
        MEMSET("pool", neghalf[:], -0.5, [neghalf])
        MEMSET("pool", poshalf[:], 0.5, [poshalf])
        CP("dve", identb[:], identf[:], [identf], [identb])
        CP("dve", identr[:], identf[:], [identf], [identr])
        CP("dve", M_le_r[:], M_le[:], [M_le], [M_le_r])
        CP("dve", M_gt_r[:], M_gt[:], [M_gt], [M_gt_r])
        CP("dve", BD_le_r[:], BD_le[:], [BD_le], [BD_le_r])
        CP("dve", BD_gt_r[:], BD_gt[:], [BD_gt], [BD_gt_r])
        CP("dve", ones_r[:], onesf[:], [onesf], [ones_r])
        CP("dve", ones2[:], onesf[:, 0:2], [onesf], [ones2])
        CP("dve", glT[:], onesf[0:17, :], [onesf], [glT])
        for u in ACTB2[0] + ACTB2[1] + Hq[12:] + HB + [U_]:
            MEMSET("pool", u[:], 0.0, [u])

        DMA("sp", gf_bc[:], gfin.partition_broadcast(128), [], [gf_bc])
        DMA("sp", gbc_a[:], gna.partition_broadcast(128), [], [gbc_a])
        DMA("sp", gbc_b[:], gnb.partition_broadcast(128), [], [gbc_b])
        TS("dve", gbc_a[:], gbc_a[:], 0.5, None, ALU.mult, None, [gbc_a], [gbc_a])
        TS("dve", gbc_b[:], gbc_b[:], 0.5, None, ALU.mult, None, [gbc_b], [gbc_b])
        DMA("sp", negA[:], a_log.partition_broadcast(128), [], [negA])
        DMA("sp", dtb[:], dt_bias.partition_broadcast(128), [], [dtb])
        ACT(negA[:], negA[:], AF.Exp, [negA], [negA])
        TS("dve", negA[:], negA[:], -1.0, None, ALU.mult, None, [negA], [negA])
        for c in range(24):
            DMA("sp", wcv[:, c, :], w_conv[:, c * 128:(c + 1) * 128].rearrange("i p -> p i"), [], [wcv], slow=True)
        DMA("sp", xt[0:16, 0:512], w_gk2, [], [xt])
        DMA("sp", xt[16:17, 0:512], b_gk.rearrange("(o n) -> o n", o=1), [], [xt])
        CP("dve", wgk_r[:], xt[0:17, 0:512], [xt], [wgk_r])
        DMA("pool", wsm[:, :, 0:16], w_in[:, C_GL:C_GL + 16].rearrange("(c p) n -> p c n", p=128), [], [wsm])
        DMA("pool", wsm[:, :, 16:32], w_in[:, C_BETA:C_BETA + 16].rearrange("(c p) n -> p c n", p=128), [], [wsm])

        for g in range(3):
            DMA("pool", Wbig[:, :, g * 1024:(g + 1) * 1024], w_ada[:, g * 1024:(g + 1) * 1024].rearrange("(c p) n -> p c n", p=128), [], [WA, WB])
        c2 = sb("c2", [128, 8, 2])
        c2b = sb("c2b", [128, 8, 2], BF16)
        c2t = sb("c2t", [128, 8, 2])
        DMA("sp", c2[:, :, 0], cpr.rearrange("(c p) -> p c", p=128), [], [c2], slow=True)
        DMA("sp", c2[:, :, 1], csm.rearrange("(c p) -> p c", p=128), [], [c2], slow=True)
        ACT(c2t[:], c2[:], AF.Tanh, [c2], [c2t], scale=0.5)
        STT(c2t[:], c2t[:], 1.0, c2[:], ALU.add, ALU.mult, [c2t, c2], [c2t])
        TS("dve", c2b[:], c2t[:], 0.5, None, ALU.mult, None, [c2t], [c2b])
        badT = sb("badT", [128, 24])
        g1T = sb("g1T", [128, 8])
        DMA("sp", badT[:], b_ada.rearrange("(c p) -> p c", p=128), [], [badT], slow=True)
        DMA("sp", g1T[:], g1.rearrange("(c p) -> p c", p=128), [], [g1T], slow=True)
        pm = palloc(1)
        for n in range(24):
            for kc in range(8):
                mm(pm.ap(2 * n, 2 * n + 2), Wbig[:, kc, n * 128:(n + 1) * 128], c2b[:, kc, :], [WA, WB, c2b], [pm], start=(kc == 0), stop=(kc == 7))
        modT = sb("modT", [128, 24, 2])
        pmv = pm.ap(0, 48).rearrange("p (n k) -> p n k", k=2)
        for k in range(2):
            TT("dve", modT[:, :, k], pmv[:, :, k], badT[:], ALU.add, [pm, badT], [modT])
        for k, kind in enumerate(("p", "s")):
            STT(a1[kind][:], modT[:, 8:16, k], 1.0, g1T[:], ALU.add, ALU.mult, [modT, g1T], [a1[kind]])
            CP("dve", sh[kind][:], modT[:, 0:8, k], [modT], [sh[kind]])
        def build_gate_bc(kind):
            k = 0 if kind == "p" else 1
            for g in range(2):
                pr = palloc()
                for c in range(4):
                    cc = g * 4 + c
                    gcol = Fq[c % 2]
                    TS("dve", gcol[:], onesf[:], modT[:, 16 + cc, k:k + 1], 0.5, ALU.mult, ALU.mult, [onesf, modT], [gcol])
                    mm(pr.ap(c * 128, (c + 1) * 128), gcol[:], identf[:], [gcol, identf], [pr])
                CP("act", gate_bc1[:, g * 512:(g + 1) * 512], pr.ap(0, 512), [pr], [gate_bc1])

        def xsrc(kind, ti):
            return xs if kind == "s" else xp[ti * 128:(ti + 1) * 128, :]

        def load_w(wt_off, col0, src, scol, n):
            wt, off = wt_off
            for c0 in range(0, n, 512):
                m = min(512, n - c0)
                DMA("pool", Wbig[:, :, off + col0 + c0:off + col0 + c0 + m],
                    src[:, scol + c0:scol + c0 + m].rearrange("(c p) n -> p c n", p=128), [], [wt])

        def rstd_from(ssq_ap, rd, out_t, mult, eps):
            n = ssq_ap.shape[1]
            TS("dve", out_t[:, 0:n], ssq_ap, mult, eps, ALU.mult, ALU.add, rd, [out_t])
            POW(out_t[:, 0:n], out_t[:, 0:n], neghalf[:, 0:n], [out_t, neghalf], [out_t])
            return out_t

        def stage_h(tiles):
            XB = [xt, big0]
            for j, (kind, ti) in enumerate(tiles):
                xin = XB[j % 2]
                if kind == "s":
                    MEMSET("dve", xin[:], 0.0, [xin])
                    DMA("sp", xin[0:16, :], xs, [], [xin])
                else:
                    DMA("sp", xin[:], xsrc(kind, ti), [], [xin])
                ACT(xb[:], xin[:], AF.Square, [xin], [xb, ssq[0]], accum=ssq[0][:, 0:1])
                r = rstd_from(ssq[0][:, 0:1], [ssq[0]], sm[0], 1.0 / D, EPS)
                TS("dve", xb[:], xin[:], r[:, 0:1], None, ALU.mult, None, [xin, r], [xb])
                for half in range(2):
                    pt = palloc(4)
                    for c in range(4):
                        cc = half * 4 + c
                        mm(pt.ap(c * 128, (c + 1) * 128), xb[:, cc * 128:(cc + 1) * 128], identb[:], [xb, identb], [pt])
                    for c in range(4):
                        cc = half * 4 + c
                        if c % 2 == 0:
                            ACT(hT[j][:, cc, :], pt.ap(c * 128, (c + 1) * 128), AF.Identity, [pt, a1[kind], sh[kind]], [hT[j]],
                                scale=a1[kind][:, cc:cc + 1], bias=sh[kind][:, cc:cc + 1])
                        else:
                            TS("dve", hT[j][:, cc, :], pt.ap(c * 128, (c + 1) * 128), a1[kind][:, cc:cc + 1], sh[kind][:, cc:cc + 1],
                               ALU.mult, ALU.add, [pt, a1[kind], sh[kind]], [hT[j]])

        def proj_tm(j, wt_off, col0, ncols, dst):
            wt, off = wt_off
            for kc in range(8):
                mm(dst.ap(0, ncols), hT[j][:, kc, :], Wbig[:, kc, off + col0:off + col0 + ncols], [hT[j], wt], [dst], start=(kc == 0), stop=(kc == 7))

        def proj_fm(j, wt_off, col0, dst_ap, dst):
            wt, off = wt_off
            for kc in range(8):
                mm(dst_ap, Wbig[:, kc, off + col0:off + col0 + 128], hT[j][:, kc, :], [hT[j], wt], [dst], start=(kc == 0), stop=(kc == 7))

        def state_io_gla(kind, heads, load):
            for h in heads:
                if load:
                    if kind == "s":
                        DMA("sp", Sg[h][:], sgla[h], [], [Sg[h]])
                    else:
                        MEMSET("dve", Sg[h][:], 0.0, [Sg[h]])
                    CP("act", Sgb[h][:], Sg[h][:], [Sg[h]], [Sgb[h]])
                else:
                    DMA("sp", o_gla[kind][h], Sg[h][:], [Sg[h]], [])

        def stage_g(tiles, pair, wt_off, first_pass, last_pass, barrier=True):
            heads = (2 * pair, 2 * pair + 1)
            sc_q = 128.0 ** -0.5

            def front(j):
                kind, ti = tiles[j]
                P = j % 2
                hq, smp = Hq2[P], sm2[P]
                qtil, qdec, ktil, kbf, kdec, attm = hq[0:2], hq[2:4], hq[4:6], hq[6:8], hq[8:10], hq[10:12]
                vbf, zs = VB2[P], ZS2[P]
                pg1 = palloc()
                for kc in range(8):
                    mm(pg1.ap(0, 128, 0, 16), wsm[:, kc, 0:16], hT[j][:, kc, :], [wsm, hT[j]], [pg1], start=(kc == 0), stop=(kc == 7))
                CP("act", glT[0:16, :], pg1.ap(0, 128, 0, 16), [pg1], [glT])
                pg2 = palloc()
                mm(pg2.ap(0, 256), glT[:], wgk_r[:, pair * 256:(pair + 1) * 256], [glT, wgk_r], [pg2])
                e0 = FB[0]
                ACT(e0[:, 0:256], pg2.ap(0, 256), AF.Exp, [pg2], [e0], scale=-1.0)
                ACT(e0[:, 0:256], e0[:, 0:256], AF.Ln, [e0], [e0], bias=1.0)
                yield
                pv = palloc()
                proj_tm(j, wt_off, 512, 512, pv)
                CP("act", vbf[:], pv.ap(0, 512), [pv], [vbf])
                if kind == "s":
                    TS("dve", gkr[:], e0[:, 0:256], -1.0 / 16.0, valid_s[:, 0:1], ALU.mult, ALU.mult, [e0, valid_s], [gkr])
                else:
                    TS("dve", gkr[:], e0[:, 0:256], -1.0 / 16.0, None, ALU.mult, None, [e0], [gkr])
                yield
                pb = palloc()
                for hh in range(2):
                    mm(pb.ap(hh * 128, (hh + 1) * 128), gkr[:, hh * 128:(hh + 1) * 128], M_le_r[:], [gkr, M_le_r], [pb])
                prv = palloc()
                mm(prv.ap(0, 256), M_gt_r[:], gkr[:], [gkr, M_gt_r], [prv])
                bref, nbref, ebl = smp[1], smp[2], smp[5]
                E1, E2, E3 = Fq[0:2], Fq[2:4], Fq[4:6]
                for hh in range(2):
                    CP("act", bref[:, hh:hh + 1], pb.ap(hh * 128 + 63, hh * 128 + 64), [pb], [bref])
                    ACT(E3[hh][:], pb.ap(hh * 128, (hh + 1) * 128), AF.Exp, [pb], [E3[hh]])
                E4 = FB[1]
                ACT(E4[:, 0:256], prv.ap(0, 256), AF.Exp, [prv], [E4])
                TS("dve", nbref[:, 0:2], bref[:, 0:2], -1.0, None, ALU.mult, None, [bref], [nbref])
                if kind == "s":
                    TS("dve", E4[:, 0:256], E4[:, 0:256], valid_s[:, 0:1], None, ALU.mult, None, [E4, valid_s], [E4])
                for hh in range(2):
                    ACT(E1[hh][:], pb.ap(hh * 128, (hh + 1) * 128), AF.Exp, [pb, nbref], [E1[hh]], bias=nbref[:, hh:hh + 1])
                    ACT(E2[hh][:], pb.ap(hh * 128, (hh + 1) * 128), AF.Exp, [pb, bref], [E2[hh]], scale=-1.0, bias=bref[:, hh:hh + 1])
                    CP("dve", ebl[:, hh:hh + 1], E3[hh][:, 127:128], [E3[hh]], [ebl])
                yield
                pq = palloc()
                pk = palloc()
                for hh in range(2):
                    proj_fm(j, wt_off, hh * 128, pq.ap(hh * 128, (hh + 1) * 128), pq)
                    proj_fm(j, wt_off, 256 + hh * 128, pk.ap(hh * 128, (hh + 1) * 128), pk)
                for hh in range(2):
                    STT(qtil[hh][:], pq.ap(hh * 128, (hh + 1) * 128), sc_q, E1[hh][:], ALU.mult, ALU.mult, [pq, E1[hh]], [qtil[hh]])
                    STT(qdec[hh][:], pq.ap(hh * 128, (hh + 1) * 128), sc_q, E3[hh][:], ALU.mult, ALU.mult, [pq, E3[hh]], [qdec[hh]])
                    CP("act", kbf[hh][:], pk.ap(hh * 128, (hh + 1) * 128), [pk], [kbf[hh]])
                    TT("dve", ktil[hh][:], pk.ap(hh * 128, (hh + 1) * 128), E2[hh][:], ALU.mult, [pk, E2[hh]], [ktil[hh]])
                yield
                pkt = palloc()
                for hh in range(2):
                    mm(pkt.ap(hh * 128, (hh + 1) * 128), kbf[hh][:], identb[:], [kbf[hh], identb], [pkt])
                pa = palloc()
                for hh in range(2):
                    mm(pa.ap(hh * 128, (hh + 1) * 128), ktil[hh][:], qtil[hh][:], [ktil[hh], qtil[hh]], [pa])
                for hh in range(2):
                    TT("dve", kdec[hh][:], pkt.ap(hh * 128, (hh + 1) * 128), E4[:, hh * 128:(hh + 1) * 128], ALU.mult, [pkt, E4], [kdec[hh]])
                    TT("dve", attm[hh][:], pa.ap(hh * 128, (hh + 1) * 128), M_le[:], ALU.mult, [pa, M_le], [attm[hh]])
                yield
                pz = palloc()
                proj_tm(j, wt_off, 1024, 512, pz)
                tz = FB[2]
                ACT(tz[:], pz.ap(0, 512), AF.Tanh, [pz], [tz], scale=0.5)
                STT(zs[:], tz[:], 1.0, pz.ap(0, 512), ALU.add, ALU.mult, [tz, pz], [zs])
                yield

            def back(j):
                kind, ti = tiles[j]
                P = j % 2
                hq, smp = Hq2[P], sm2[P]
                qdec, kdec, attm = hq[2:4], hq[8:10], hq[10:12]
                vbf, zs, ebl = VB2[P], ZS2[P], smp[5]
                if kind == "s" or (kind == "p" and ti == 0):
                    state_io_gla(kind, heads, True)
                pos = []
                for hh in range(2):
                    h = heads[hh]
                    pS = palloc()
                    mm(pS.ap(0, 256), kdec[hh][:], vbf[:, hh * 256:(hh + 1) * 256], [kdec[hh], vbf], [pS])
                    po = palloc()
                    pos.append(po)
                    mm(po.ap(0, 256), attm[hh][:], vbf[:, hh * 256:(hh + 1) * 256], [attm[hh], vbf], [po], start=True, stop=False)
                    mm(po.ap(0, 256), qdec[hh][:], Sgb[h][:], [qdec[hh], Sgb[h]], [po], start=False, stop=True)
                    STT(Sg[h][:], Sg[h][:], ebl[:, hh:hh + 1], pS.ap(0, 256), ALU.mult, ALU.add, [Sg[h], ebl, pS], [Sg[h]])
                    CP("act", Sgb[h][:], Sg[h][:], [Sg[h]], [Sgb[h]])
                    ACT(junk[:, 0:256], po.ap(0, 256), AF.Square, [po], [junk, ssq[1]], accum=ssq[1][:, hh:hh + 1])
                r = rstd_from(ssq[1][:, 0:2], [ssq[1]], smp[3], 1.0 / 256.0, EPS)
                yield
                t1 = T1G
                for hh in range(2):
                    h = heads[hh]
                    STT(t1[:, hh * 256:(hh + 1) * 256], pos[hh].ap(0, 256), r[:, hh:hh + 1], gbc_a[:], ALU.mult, ALU.mult, [pos[hh], r, gbc_a], [t1])
                    TT("dve", OA[j][:, h * 256:(h + 1) * 256], t1[:, hh * 256:(hh + 1) * 256], zs[:, hh * 256:(hh + 1) * 256], ALU.mult, [t1, zs], [OA[j]])
                if kind == "s" or (last_pass and j == len(tiles) - 1):
                    state_io_gla(kind, heads, False)
                yield

            n = len(tiles)

            def prologue():
                if barrier:
                    S.op("dve", lambda e: e.memset(junk[:, 0:2], 0.0), (), _bufs([junk, big0, big1, xt, xb] + ACTB2[0] + ACTB2[1]))
                yield from front(0)

            def body(next_pro=None):
                for j in range(n):
                    gens = [back(j)]
                    if j + 1 < n:
                        gens.append(front(j + 1))
                    elif next_pro is not None:
                        gens.append(next_pro)
                    while gens:
                        for g in list(gens):
                            try:
                                next(g)
                            except StopIteration:
                                gens.remove(g)

            return prologue, body

        def state_io_gdn(kind, half, load):
            for hh in range(4):
                h = 4 * half + hh
                if load:
                    if kind == "s":
                        DMA("sp", Sd4[half][:, hh, :], sgdn[h], [], [Sd4[half]])
                    else:
                        MEMSET("dve", Sd4[half][:, hh, :], 0.0, [Sd4[half]])
                else:
                    DMA("sp", o_gdn[kind][h], Sd4[half][:, hh, :], [Sd4[half]], [])
            if load:
                CP("act", Sdb4[half][:], Sd4[half][:], [Sd4[half]], [Sdb4[half]])

        def bc4(ap):
            return ap.unsqueeze(2).to_broadcast([ap.shape[0], 4, 128])

        def bcm(t):
            return t[:].unsqueeze(1).to_broadcast([128, 4, 128])

        def p3(ps_, p0=0, p1=128):
            return r3(ps_.ap(0, 512, p0, p1))

        def stage_d(tiles, half, wt_off, first_pass, last_pass, barrier=True):
            heads = list(range(4 * half, 4 * half + 4))
            gch = [[kind_i * 8 + h for h in heads] for kind_i in range(3)]
            Sd_, Sdb_ = Sd4[half], Sdb4[half]
            hs = slice(4 * half, 4 * half + 4)

            def front(j):
                kind, ti = tiles[j]
                P = j % 2
                smp, smrp, actb, hq = sm2[P], smr2[P], ACTB2[P], Hq2[P]
                nvalid = 16 if kind == "s" else 128
                if kind == "s" or (kind == "p" and ti == 0):
                    for grp in gch:
                        for c in grp:
                            if kind == "s":
                                DMA("sp", carry[:, c, :], cconv[:, c * 128:(c + 1) * 128].rearrange("i p -> p i"), [], [carry], slow=True)
                            else:
                                MEMSET("dve", carry[:, c, :], 0.0, [carry])
                psm = palloc()
                for kc in range(8):
                    mm(psm.ap(0, 16), hT[j][:, kc, :], wsm[:, kc, 16:32], [hT[j], wsm], [psm], start=(kc == 0), stop=(kc == 7))
                beta, sqb, gpre, g_r = smp[0], smp[1], smp[2], smrp[0]
                ACT(beta[:, 0:8], psm.ap(0, 8), AF.Tanh, [psm], [beta], scale=0.5)
                CP("act", gpre[:, 0:8], psm.ap(8, 16), [psm], [gpre])
                yield
                TS("dve", beta[:, 0:8], beta[:, 0:8], 0.5, 0.5, ALU.mult, ALU.add, [beta], [beta])
                TT("dve", gpre[:, 0:8], gpre[:, 0:8], dtb[:], ALU.add, [gpre, dtb], [gpre])
                yield
                POW(sqb[:, 0:8], beta[:, 0:8], poshalf[:, 0:8], [beta, poshalf], [sqb])
                ACT(gpre[:, 0:8], gpre[:, 0:8], AF.Exp, [gpre], [gpre])
                ACT(gpre[:, 0:8], gpre[:, 0:8], AF.Ln, [gpre], [gpre], bias=1.0)
                yield
                TT("dve", g_r[:, 0:8], gpre[:, 0:8], negA[:], ALU.mult, [gpre, negA], [g_r])
                if kind == "s":
                    TS("dve", g_r[:, 0:8], g_r[:, 0:8], valid_s[:, 0:1], None, ALU.mult, None, [g_r, valid_s], [g_r])
                gm = smrp[1]
                for c in range(2):
                    TS("dve", gm[:, c * 8:(c + 1) * 8], g_r[:, 0:8], chunkind[:, c:c + 1], None, ALU.mult, None, [g_r, chunkind], [gm])
                yield
                pcs = palloc()
                mm(pcs.ap(0, 8), BD_le_r[:], g_r[:, 0:8], [BD_le_r, g_r], [pcs])
                mm(pcs.ap(8, 16), BD_gt_r[:], g_r[:, 0:8], [BD_gt_r, g_r], [pcs])
                ebb = smp[3]
                ACT(ebb[:, 0:16], pcs.ap(0, 16), AF.Exp, [pcs], [ebb])
                pdl = palloc()
                mm(pdl.ap(0, 16), ones_r[:], gm[:, 0:16], [ones_r, gm], [pdl])
                dlast = smp[4]
                ACT(dlast[:, 0:16], pdl.ap(0, 16), AF.Exp, [pdl], [dlast])
                yield
                for kind_i in range(3):
                    pp = palloc()
                    for hh in range(4):
                        proj_fm(j, wt_off, kind_i * 512 + hh * 128, pp.ap(hh * 128, (hh + 1) * 128), pp)
                    c0 = gch[kind_i][0]
                    CP("dve", RW[:, :, 0:3], carry[:, c0:c0 + 4, :], [carry], [RW])
                    CP("act", RW[:, :, 3:131], p3(pp), [pp], [RW])
                    yield
                    y3, t3, ty3 = r3(FW[:]), r3(FB[1][:]), r3(FB[2][:])
                    TT("dve", y3, RW[:, :, 0:128], bc4(wcv[:, c0:c0 + 4, 0]), ALU.mult, [RW, wcv], [FW])
                    for tap in range(1, 4):
                        TT("dve", t3, RW[:, :, tap:tap + 128], bc4(wcv[:, c0:c0 + 4, tap]), ALU.mult, [RW, wcv], [FB[1]])
                        TT("dve", y3, y3, t3, ALU.add, [FW, FB[1]], [FW])
                        if tap == 2:
                            yield
                    CP("dve", carry[:, c0:c0 + 4, :], RW[:, :, nvalid:nvalid + 3], [RW], [carry])
                    ACT(ty3, y3, AF.Tanh, [FW], [FB[2]], scale=0.5)
                    yield
                    STT(actb[kind_i][:], ty3, 1.0, y3, ALU.add, ALU.mult, [FB[2], FW], [actb[kind_i]])
                if kind == "s" or (last_pass and j == len(tiles) - 1):
                    for grp in gch:
                        for c in grp:
                            DMA("sp", o_conv[kind][:, c * 128:(c + 1) * 128].rearrange("i p -> p i"), carry[:, c, :], [carry], [], slow=True)
                yield
                pss = palloc()
                for kind_i in range(2):
                    for hh in range(4):
                        sq = Hq[12 + (hh % 2)]
                        src = hq[kind_i * 4 + hh]
                        ACT(sq[:], src[:], AF.Square, [src], [sq])
                        cidx = (kind_i * 4 + hh) * 2
                        mm(pss.ap(cidx, cidx + 2), sq[:], ones2[:], [sq, ones2], [pss])
                rqk = smp[5]
                CP("act", rqk[:, 0:16], pss.ap(0, 16), [pss], [rqk])
                yield
                pssv = rqk[:, 0:16].rearrange("p (n k) -> p n k", k=2)[:, :, 0]
                TS("dve", smp[0][:, 8:16], pssv, 1.0, 4.0 * EPS, ALU.mult, ALU.add, [rqk], [smp[0]])
                CP("dve", rqk[:, 0:8], smp[0][:, 8:16], [smp[0]], [rqk])
                POW(rqk[:, 0:8], rqk[:, 0:8], neghalf[:, 0:8], [rqk, neghalf], [rqk])
                yield
                c_qs, c_o1, ck, ckes, ckd0, ckd1, cv = smp[6], smp[7], smp[8], smp[9], smp[10], smp[11], smp[2]
                TS("dve", c_qs[:, 0:4], rqk[:, 0:4], 128.0 ** -0.5, None, ALU.mult, None, [rqk], [c_qs])
                TT("dve", c_o1[:, 0:4], c_qs[:, 0:4], ebb[:, hs], ALU.mult, [c_qs, ebb], [c_o1])
                TT("dve", ck[:, 0:4], rqk[:, 4:8], sqb[:, hs], ALU.mult, [rqk, sqb], [ck])
                TT("dve", ckes[:, 0:4], ck[:, 0:4], ebb[:, hs], ALU.mult, [ck, ebb], [ckes])
                ebl = ebb[:, 8 + 4 * half:8 + 4 * half + 4]
                for c, ckd in enumerate((ckd0, ckd1)):
                    STT(ckd[:, 0:4], ck[:, 0:4], chunkind[:, c:c + 1], ebl, ALU.mult, ALU.mult, [ck, chunkind, ebb], [ckd])
                    if kind == "s":
                        TS("dve", ckd[:, 0:4], ckd[:, 0:4], valid_s[:, 0:1], None, ALU.mult, None, [ckd, valid_s], [ckd])
                TS("dve", cv[:, 0:4], sqb[:, hs], 0.5, None, ALU.mult, None, [sqb], [cv])
                yield

            def back(j):
                kind, ti = tiles[j]
                P = j % 2
                smp, smrp, hq = sm2[P], smr2[P], Hq2[P]
                g_r, ebb, dlast = smrp[0], smp[3], smp[4]
                c_qs, c_o1, ck, ckes, ckd0, ckd1, cv = smp[6], smp[7], smp[8], smp[9], smp[10], smp[11], smp[2]
                qs = [hq[hh] for hh in range(4)]
                k0 = [hq[4 + hh] for hh in range(4)]
                v0 = [hq[8 + hh] for hh in range(4)]
                if kind == "s" or (kind == "p" and ti == 0):
                    state_io_gdn(kind, half, True)
                pz = palloc()
                proj_tm(j, wt_off, 1536, 512, pz)
                zs = FB[0]
                ACT(zs[:], pz.ap(0, 512), AF.Tanh, [pz], [zs], scale=0.5)
                STT(zs[:], zs[:], 1.0, pz.ap(0, 512), ALU.add, ALU.mult, [zs, pz], [zs])
                yield
                for a in range(2):
                    ga_ = g_r[:, 4 * half + 2 * a:4 * half + 2 * a + 2]
                    TT("dve", QRp[a][:], M_gt[:].unsqueeze(1).to_broadcast([128, 2, 128]), ga_.unsqueeze(2).to_broadcast([128, 2, 128]), ALU.mult, [M_gt, g_r], [QRp[a]])
                pdf = palloc()
                for hh in range(4):
                    mm(pdf.ap(hh * 128, (hh + 1) * 128), M_le_r[:], QRp[hh // 2][:, hh % 2, :], [M_le_r, QRp[hh // 2]], [pdf])
                ACT(EE[:], p3(pdf), AF.Exp, [pdf], [EE])
                TT("dve", DL[:], EE[:], bcm(BD_ge), ALU.mult, [EE, BD_ge], [DL])
                TT("dve", EE[:], EE[:], bcm(BD_gt), ALU.mult, [EE, BD_gt], [EE])
                pkt = palloc()
                pvt = palloc()
                for hh in range(4):
                    mm(pkt.ap(hh * 128, (hh + 1) * 128), k0[hh][:], identb[:], [k0[hh], identb], [pkt])
                for hh in range(4):
                    mm(pvt.ap(hh * 128, (hh + 1) * 128), v0[hh][:], identb[:], [v0[hh], identb], [pvt])
                TT("dve", KS_TM[:], p3(pkt), bc4(ck[:, 0:4]), ALU.mult, [pkt, ck], [KS_TM])
                TT("dve", KES[:], p3(pkt), bc4(ckes[:, 0:4]), ALU.mult, [pkt, ckes], [KES])
                TT("dve", KSD0[:], p3(pkt), bc4(ckd0[:, 0:4]), ALU.mult, [pkt, ckd0], [KSD0])
                TT("dve", KSD1[:], p3(pkt), bc4(ckd1[:, 0:4]), ALU.mult, [pkt, ckd1], [KSD1])
                TT("dve", VS[:], p3(pvt), bc4(cv[:, 0:4]), ALU.mult, [pvt, cv], [VS])
                yield
                pkT = palloc()
                for hh in range(4):
                    mm(pkT.ap(hh * 128, (hh + 1) * 128), KS_TM[:, hh, :], identb[:], [KS_TM, identb], [pkT])
                CP("act", KST[:], p3(pkT), [pkT], [KST])
                yield
                pkk = palloc()
                pqk = palloc()
                for hh in range(4):
                    mm(pkk.ap(hh * 128, (hh + 1) * 128), KST[:, hh, :], KST[:, hh, :], [KST], [pkk])
                for hh in range(4):
                    mm(pqk.ap(hh * 128, (hh + 1) * 128), qs[hh][:], KST[:, hh, :], [qs[hh], KST], [pqk])
                for a in range(2):
                    TT("dve", PTRp[a][:], r3(pkk.ap(a * 256, a * 256 + 256)), EE[:, 2 * a:2 * a + 2, :], ALU.mult, [pkk, EE], [PTRp[a]])
                TT("dve", DL[:], DL[:], bc4(c_qs[:, 0:4]), ALU.mult, [DL, c_qs], [DL])
                TT("dve", QKSD[:], p3(pqk), DL[:], ALU.mult, [pqk, DL], [QKSD])
                yield
                pqT = palloc()
                for hh in range(4):
                    mm(pqT.ap(hh * 128, (hh + 1) * 128), QKSD[:, hh, :], identb[:], [QKSD, identb], [pqT])
                pUs = []
                for a in range(2):
                    pU = palloc()
                    pUs.append(pU)
                    for h2 in range(2):
                        mm(pU.ap(h2 * 128, (h2 + 1) * 128), PTRp[a][:, h2, :], identr[:], [PTRp[a], identr], [pU])
                bcm2 = lambda t: t[:].unsqueeze(1).to_broadcast([128, 2, 128])
                p2 = lambda ps_: r3(ps_.ap(0, 256))
                for a in range(2):
                    CP("act", PRp[a][:], p2(pUs[a]), [pUs[a]], [PRp[a]])
                    TT("dve", XRp[a][:], bcm2(identf), p2(pUs[a]), ALU.subtract, [identf, pUs[a]], [XRp[a]])
                CP("act", QKT[:], p3(pqT), [pqT], [QKT])
                yield
                def x_mm(lvl):
                    pXs = [None, None]
                    for a in range(2):
                        pXs[a] = palloc()
                        for h2 in range(2):
                            mm(pXs[a].ap(h2 * 128, (h2 + 1) * 128), QRp[a][:, h2, :], XRp[a][:, h2, :], [QRp[a], XRp[a]], [pXs[a]])
                    return pXs

                def x_evac(lvl, pXs):
                    for a in range(2):
                        xeng = "act" if a == 0 else "dve"
                        if lvl < 5:
                            CP(xeng, XRp[a][:], p2(pXs[a]), [pXs[a]], [XRp[a]])
                        else:
                            CP(xeng, XFp[a][:], p2(pXs[a]), [pXs[a]], [XFp[a]])

                for lvl in range(1, 6):
                    pPs, pPTs = [None, None], [None, None]
                    for a in range(2):
                        if lvl < 5:
                            pPs[a] = palloc()
                            for h2 in range(2):
                                mm(pPs[a].ap(h2 * 128, (h2 + 1) * 128), PTRp[a][:, h2, :], PRp[a][:, h2, :], [PTRp[a], PRp[a]], [pPs[a]])
                        pPTs[a] = palloc()
                        for h2 in range(2):
                            mm(pPTs[a].ap(h2 * 128, (h2 + 1) * 128), PRp[a][:, h2, :], PTRp[a][:, h2, :], [PRp[a], PTRp[a]], [pPTs[a]])
                    pXprev = x_mm(lvl - 1) if lvl > 1 else None
                    for a in range(2):
                        if lvl < 5:
                            CP("act", PRp[a][:], p2(pPs[a]), [pPs[a]], [PRp[a]])
                            CP("act", PTRp[a][:], p2(pPTs[a]), [pPTs[a]], [PTRp[a]])
                    for a in range(2):
                        TT("dve", QRp[a][:], p2(pPTs[a]), bcm2(identf), ALU.add, [pPTs[a], identf], [QRp[a]])
                    if pXprev is not None:
                        x_evac(lvl - 1, pXprev)
                    yield
                x_evac(5, x_mm(5))
                pw = palloc()
                for hh in range(4):
                    mm(pw.ap(hh * 128, (hh + 1) * 128), KES[:, hh, :], XFp[hh // 2][:, hh % 2, :], [KES, XFp[hh // 2]], [pw])
                puv = palloc()
                for hh in range(4):
                    mm(puv.ap(hh * 128, (hh + 1) * 128), XFp[hh // 2][:, hh % 2, :], VS[:, hh, :], [XFp[hh // 2], VS], [puv])
                CP("act", WT[:], p3(pw), [pw], [WT])
                CP("act", UV[:], p3(puv), [puv], [UV])
                yield
                for c in range(2):
                    r0, r1 = 64 * c, 64 * c + 64
                    pws = palloc()
                    for hh in range(4):
                        mm(pws.ap(hh * 128, (hh + 1) * 128, r0, r1), WT[:, hh, r0:r1], Sdb_[:, hh, :], [WT, Sdb_], [pws])
                    po1 = palloc()
                    for hh in range(4):
                        mm(po1.ap(hh * 128, (hh + 1) * 128, r0, r1), qs[hh][:, r0:r1], Sdb_[:, hh, :], [qs[hh], Sdb_], [po1])
                    TT("dve", U_[r0:r1, :, :], UV[r0:r1, :, :], p3(pws, r0, r1), ALU.subtract, [UV, pws], [U_])
                    dl_c = dlast[:, c * 8 + 4 * half:c * 8 + 4 * half + 4]
                    TT("dve", Sd_[:], Sd_[:], bc4(dl_c), ALU.mult, [Sd_, dlast], [Sd_])
                    pS = palloc()
                    KSD = KSD0 if c == 0 else KSD1
                    for hh in range(4):
                        mm(pS.ap(hh * 128, (hh + 1) * 128), KSD[:, hh, :], U_[:, hh, :], [KSD, U_], [pS])
                    po2 = palloc()
                    for hh in range(4):
                        mm(po2.ap(hh * 128, (hh + 1) * 128, r0, r1), QKT[:, hh, r0:r1], U_[:, hh, :], [QKT, U_], [po2])
                    TT("dve", Sd_[:], Sd_[:], p3(pS), ALU.add, [Sd_, pS], [Sd_])
                    CP("act", Sdb_[:], Sd_[:], [Sd_], [Sdb_])
                    TT("dve", OSB[r0:r1, :, :], p3(po1, r0, r1), bc4(c_o1[r0:r1, 0:4]), ALU.mult, [po1, c_o1], [OSB])
                    TT("dve", OSB[r0:r1, :, :], OSB[r0:r1, :, :], p3(po2, r0, r1), ALU.add, [OSB, po2], [OSB])
                    yield
                for hh in range(4):
                    ACT(junk[:, 0:128], OSB[:, hh, :], AF.Square, [OSB], [junk, ssq[2]], accum=ssq[2][:, hh:hh + 1])
                rt = ssq[3]
                TS("dve", rt[:, 0:4], ssq[2][:, 0:4], 1.0 / 128.0, EPS, ALU.mult, ALU.add, [ssq[2]], [rt])
                POW(rt[:, 0:4], rt[:, 0:4], neghalf[:, 0:4], [rt, neghalf], [rt])
                yield
                TT("dve", OSB[:], OSB[:], bc4(rt[:, 0:4]), ALU.mult, [OSB, rt], [OSB])
                TT("dve", OSB[:], OSB[:], bcm(gbc_b), ALU.mult, [OSB, gbc_b], [OSB])
                TT("dve", r3(OB[j][:, half * 512:(half + 1) * 512]), OSB[:], r3(zs[:]), ALU.mult, [OSB, zs], [OB[j]])
                if kind == "s" or (last_pass and j == len(tiles) - 1):
                    state_io_gdn(kind, half, False)
                yield

            n = len(tiles)

            def prologue():
                yield from front(0)

            def body(next_pro=None):
                if barrier:
                    S.op("dve", lambda e: e.memset(junk[:, 0:2], 0.0), (), _bufs([junk, big0, big1, xt, xb]))
                for j in range(n):
                    gens = [back(j)]
                    if j + 1 < n:
                        gens.append(front(j + 1))
                    elif next_pro is not None:
                        gens.append(next_pro)
                    while gens:
                        for g in list(gens):
                            try:
                                next(g)
                            except StopIteration:
                                gens.remove(g)

            return prologue, body

        def transpose_1024(src_ap_fn, rd, dst):
            for half in range(2):
                pt = palloc(4)
                for c in range(4):
                    cc = half * 4 + c
                    mm(pt.ap(c * 128, (c + 1) * 128), src_ap_fn(cc), identb[:], rd + [identb], [pt])
                if half == 0:
                    CP("act", dst[:, 0:4, :], pt.ap(0, 512).rearrange("p (c n) -> p c n", n=128), [pt], [dst])
                else:
                    CP("dve", dst[:, 4:8, :], pt.ap(0, 512).rearrange("p (c n) -> p c n", n=128), [pt], [dst])

        def stage_m12(tiles, which, wt_off):
            wt, off = wt_off
            for j, (kind, ti) in enumerate(tiles):
                src = OA[j] if which == 0 else OB[j]
                transpose_1024(lambda cc: src[:, cc * 128:(cc + 1) * 128], [src], oT)
                sg = big0
                pp_sb = big1
                for g in range(2):
                    pg = palloc(4)
                    proj_tm(j, wt_off, g * 512, 512, pg)
                    ACT(sg[:, g * 512:(g + 1) * 512], pg.ap(0, 512), AF.Tanh, [pg], [sg], scale=0.5)
                for g in range(2):
                    pp = palloc(4)
                    for kc in range(8):
                        mm(pp.ap(0, 512), oT[:, kc, :], Wbig[:, kc, off + 1024 + g * 512:off + 1024 + (g + 1) * 512], [oT, wt], [pp], start=(kc == 0), stop=(kc == 7))
                    if which == 0:
                        STT(OA[j][:, g * 512:(g + 1) * 512], sg[:, g * 512:(g + 1) * 512], 1.0, pp.ap(0, 512), ALU.add, ALU.mult, [sg, pp], [OA[j]])
                    else:
                        STT(pp_sb[:, g * 512:(g + 1) * 512], sg[:, g * 512:(g + 1) * 512], 1.0, pp.ap(0, 512), ALU.add, ALU.mult, [sg, pp], [pp_sb])
                if which == 1:
                    TT("dve", mgb[:], pp_sb[:], OA[j][:], ALU.add, [pp_sb, OA[j]], [mgb])
                    dstv = OB[j].t[:].rearrange("p (c n) -> p c n", n=128)
                    for half in range(2):
                        pt = palloc(4)
                        for c in range(4):
                            cc = half * 4 + c
                            mm(pt.ap(c * 128, (c + 1) * 128), mgb[:, cc * 128:(cc + 1) * 128], identb[:], [mgb, identb], [pt])
                        CP("act" if half == 0 else "dve", dstv[:, half * 4:half * 4 + 4, :], pt.ap(0, 512).rearrange("p (c n) -> p c n", n=128), [pt], [OB[j]])

        def stage_m3(tiles, wt_off):
            wt, off = wt_off
            XB = [xt, big0]
            for j, (kind, ti) in enumerate(tiles):
                if kind == "s" or (kind == "p" and ti == 0):
                    build_gate_bc(kind)
                mT = OB[j].t[:].rearrange("p (c n) -> p c n", n=128)
                xn = XB[j % 2]
                if kind == "s":
                    MEMSET("dve", xn[:], 0.0, [xn])
                    DMA("sp", xn[0:16, :], xs, [], [xn])
                else:
                    DMA("sp", xn[:], xsrc(kind, ti), [], [xn])
                gp = big1
                for g in range(2):
                    po = palloc(4)
                    for kc in range(8):
                        mm(po.ap(0, 512), mT[:, kc, :], Wbig[:, kc, off + g * 512:off + (g + 1) * 512], [OB[j], wt], [po], start=(kc == 0), stop=(kc == 7))
                    TT("dve", gp[:, g * 512:(g + 1) * 512], po.ap(0, 512), gate_bc[kind][:, g * 512:(g + 1) * 512], ALU.mult, [po, gate_bc[kind]], [gp])
                TT("dve", xn[:], xn[:], gp[:], ALU.add, [xn, gp], [xn])
                ACT(mgb[:], xn[:], AF.Square, [xn], [mgb, ssq[0]], accum=ssq[0][:, 1:2])
                rt = ssq[3]
                TS("dve", rt[:, 3:4], ssq[0][:, 1:2], 1.0 / D, EPS, ALU.mult, ALU.add, [ssq[0]], [rt])
                POW(rt[:, 3:4], rt[:, 3:4], neghalf[:, 0:1], [rt, neghalf], [rt])
                STT(xn[:], xn[:], rt[:, 3:4], gf_bc[:], ALU.mult, ALU.mult, [xn, rt, gf_bc], [xn])
                if kind == "s":
                    DMA("sp", ys, xn[0:16, :], [xn], [])
                else:
                    DMA("sp", yp[ti * 128:(ti + 1) * 128, :], xn[:], [xn], [])

        def lw_g(pair, wo):
            load_w(wo, 0, w_in, C_QA + pair * 256, 256)
            load_w(wo, 256, w_in, C_KA + pair * 256, 256)
            load_w(wo, 512, w_in, C_VA + pair * 512, 512)
            load_w(wo, 1024, w_in, C_ZA + pair * 512, 512)

        def lw_d(half, wo):
            for kind_i in range(3):
                load_w(wo, kind_i * 512, w_in, C_QKV + kind_i * 1024 + half * 512, 512)
            load_w(wo, 1536, w_in, C_ZB + half * 512, 512)

        def lw_m(which, wo):
            load_w(wo, 0, w_in, C_GA if which == 0 else C_GB, 1024)
            load_w(wo, 1024, w_pa if which == 0 else w_pb, 0, 1024)

        def lw_o(wo):
            load_w(wo, 0, w_out, 0, 1024)

        stage_list = []
        for ps_ in range(NPASS):
            tiles = [("p", ps_ * TPP + i) for i in range(TPP)]
            if ps_ == 0:
                tiles = [("s", 0)] + tiles
            fp, lp = (ps_ == 0), (ps_ == NPASS - 1)
            ev = (len(tiles) % 2 == 0)
            stage_list.append(("gen", lambda wo: lw_g(0, wo), lambda wo, t=tiles, f=fp, l=lp: stage_g(t, 0, wo, f, l, True), tiles, ev))
            stage_list.append(("gen", lambda wo: lw_g(1, wo), lambda wo, t=tiles, f=fp, l=lp: stage_g(t, 1, wo, f, l, False), None, ev))
            stage_list.append(("gen", lambda wo: lw_d(0, wo), lambda wo, t=tiles, f=fp, l=lp: stage_d(t, 0, wo, f, l, True), None, ev))
            stage_list.append(("gen", lambda wo: lw_d(1, wo), lambda wo, t=tiles, f=fp, l=lp: stage_d(t, 1, wo, f, l, False), None, False))
            stage_list.append(("run", lambda wo: lw_m(0, wo), lambda wo, t=tiles: stage_m12(t, 0, wo), None, False))
            stage_list.append(("run", lambda wo: lw_m(1, wo), lambda wo, t=tiles: stage_m12(t, 1, wo), None, False))
            stage_list.append(("run", lambda wo: lw_o(wo), lambda wo, t=tiles: stage_m3(t, wo), None, False))

        import os as _os
        _stop = int(_os.environ.get("KSTOP", "1000"))
        stage_list[0][1](WBUF[0])
        started = None
        for si, (typ, lw, mk, htiles, overlap_next) in enumerate(stage_list):
            if si >= _stop:
                break
            if si + 1 < len(stage_list):
                stage_list[si + 1][1](WBUF[(si + 1) % 2])
            if htiles is not None:
                stage_h(htiles)
            if typ == "run":
                mk(WBUF[si % 2])
                started = None
                continue
            if started is None:
                pro, body = mk(WBUF[si % 2])
                for _ in pro():
                    pass
            else:
                body = started
            nxt = None
            started = None
            if overlap_next and si + 1 < min(len(stage_list), _stop) and stage_list[si + 1][0] == "gen":
                pro2, body2 = stage_list[si + 1][2](WBUF[(si + 1) % 2])
                nxt = pro2()
                started = body2
            body(nxt)

        S.emit_all(block, sems)
    return nc


_NC_CACHE = {}


def kernel(x_prompt, x_sample, c_prompt, c_sample, state_gla, state_gdn, cache_conv_gdn,
           w_ada, b_ada, g_norm1, w_in, w_gk2, b_gk, w_conv, a_log, dt_bias,
           g_norm_a, g_norm_b, w_pa, w_pb, w_out, g_final):
    f = lambda a: np.ascontiguousarray(np.asarray(a, dtype=np.float32))
    if "nc" not in _NC_CACHE:
        _NC_CACHE["nc"] = build_nc()
    nc = _NC_CACHE["nc"]
    shared = {
        "w_ada": f(w_ada[0]), "b_ada": f(b_ada[0]), "g1": f(g_norm1[0]), "w_in": f(w_in[0]),
        "w_gk2": f(w_gk2[0]), "b_gk": f(b_gk[0]), "w_conv": f(w_conv[0]), "a_log": f(a_log[0]),
        "dt_bias": f(dt_bias[0]), "gna": f(g_norm_a[0]), "gnb": f(g_norm_b[0]),
        "w_pa": f(w_pa[0]), "w_pb": f(w_pb[0]), "w_out": f(w_out[0]), "gfin": f(g_final),
    }
    in_maps = []
    for b in range(8):
        m = dict(shared)
        m.update({
            "xp": f(x_prompt[b]), "xs": f(x_sample[b]), "cp": f(c_prompt[b]), "cs": f(c_sample[b]),
            "sgla": f(state_gla[0, b]), "sgdn": f(state_gdn[0, b]), "cconv": f(cache_conv_gdn[0, b]),
        })
        in_maps.append(m)
    res = run_bass_kernel_spmd(nc, in_maps, core_ids=list(range(8)))
    R = res.results
    st = lambda k: np.stack([np.asarray(R[b][k], dtype=np.float32) for b in range(8)])
    return (st("yp"), st("ys"), st("o_gla_p")[None], st("o_gdn_p")[None], st("o_conv_p")[None],
            st("o_gla_s")[None], st("o_gdn_s")[None], st("o_conv_s")[None])
```

```python
import numpy as np
from contextlib import ExitStack
import concourse.bass as bass
import concourse.mybir as mybir
from concourse.bass_utils import run_bass_kernel_spmd

F32 = mybir.dt.float32
F32R = mybir.dt.float32r
BF16 = mybir.dt.bfloat16
AF = mybir.ActivationFunctionType
ALU = mybir.AluOpType

ENGS = ("pe", "act", "dve", "pool", "sp")
NDS = 12
import os as _os0
AUTOWARM = int(_os0.environ.get("AUTOWARM", "0"))
GEN = 2000
NGEN = 16

D = 1024
T_PROMPT = 4096
NT = T_PROMPT // 128
T_S = 16
D_IN = 9248
EPS = 1e-6
TPP = 8
NPASS = NT // TPP
NSLOT = TPP + 1

C_QA, C_KA, C_VA, C_ZA, C_GL = 0, 512, 1024, 2048, 3072
C_QKV, C_ZB, C_BETA, C_AIN, C_GA, C_GB = 3088, 6160, 7184, 7192, 7200, 8224


class Buf:
    __slots__ = ("name", "last_w", "readers")

    def __init__(self, name):
        self.name = name
        self.last_w = None
        self.readers = []


class Op:
    __slots__ = ("eng", "emit", "deps", "is_dma", "inc", "count", "sem", "val", "idx")

    def __init__(self, eng, emit, is_dma):
        self.eng = eng
        self.emit = emit
        self.deps = []
        self.is_dma = is_dma
        self.inc = False
        self.count = None
        self.sem = None
        self.val = None


class Sched:
    def __init__(self):
        self.ops = {e: [] for e in ENGS}
        self.dma_count = {e: 0 for e in ENGS}
        self.all_dma = []
        self.autowarm = None

    def _add(self, op, reads, writes, locks=()):
        e = op.eng
        deps = []
        for lk in locks:
            la = lk.last_w
            if la is not None and la.eng != e:
                deps.append(la)
            lk.last_w = op
        for b in reads:
            if b.last_w is not None:
                deps.append(b.last_w)
        for b in writes:
            if b.last_w is not None and (b.last_w.is_dma or b.last_w.eng != e or op.is_dma):
                deps.append(b.last_w)
            for r in b.readers:
                if r.is_dma or r.eng != e or op.is_dma:
                    deps.append(r)
        out = []
        seen = set()
        latest = {}
        for d in deps:
            if d is op or id(d) in seen:
                continue
            if (not d.is_dma) and d.eng == "pe" and e == "pe" and not op.is_dma:
                continue
            seen.add(id(d))
            if d.is_dma:
                out.append(d)
            else:
                cur = latest.get(d.eng)
                if cur is None or d.idx > cur.idx:
                    latest[d.eng] = d
        out.extend(latest.values())
        op.deps = out
        op.idx = len(self.ops[e])
        for b in writes:
            b.last_w = op
            b.readers = []
        for b in reads:
            if b.last_w is not op:
                if op.is_dma:
                    b.readers.append(op)
                else:
                    b.readers = [r for r in b.readers if r.is_dma or r.eng != e]
                    b.readers.append(op)
        self.ops[e].append(op)
        return op

    def op(self, eng, emit, reads=(), writes=(), locks=()):
        return self._add(Op(eng, emit, False), reads, writes, locks)

    def dma(self, eng, emit, reads=(), writes=()):
        op = Op(eng, emit, True)
        k = self.dma_count[eng]
        self.dma_count[eng] += 1
        op.sem = (eng, k % NDS)
        op.val = 16 * (k // NDS + 1)
        self.all_dma.append(op)
        return self._add(op, reads, writes)

    def emit_all(self, block, sems):
        for e in ENGS:
            for op in self.ops[e]:
                for d in op.deps:
                    if not d.is_dma:
                        d.inc = True
        for e in ENGS:
            c = 0
            for op in self.ops[e]:
                if not op.is_dma and op.inc:
                    c += 1
                    op.count = c
        import os as _o
        if _o.environ.get("KDEBUG"):
            for e in ENGS:
                print("ENG", e, "nops", len(self.ops[e]), "incs", sum(1 for op in self.ops[e] if (not op.is_dma) and op.inc), "dmas", self.dma_count[e])
        last_dma = {}
        for op in self.all_dma:
            last_dma[op.sem] = max(last_dma.get(op.sem, 0), op.val)
        sched = self

        def run(e, eng):
            seen = {}
            for op in sched.ops[e]:
                waits = []
                for d in op.deps:
                    if d.is_dma:
                        waits.append((d.sem, d.val))
                    else:
                        waits.append(((d.eng, "g", (d.count - 1) // GEN), (d.count - 1) % GEN + 1))
                if op.is_dma and op.val > 16:
                    waits.append((op.sem, op.val - 16))
                first_wait = True
                for key, val in waits:
                    if seen.get(key, 0) >= val:
                        continue
                    seen[key] = val
                    if first_wait and e == "pe" and sched.autowarm is not None:
                        for _ in range(AUTOWARM):
                            sched.autowarm(eng)
                    first_wait = False
                    eng.wait_ge(sems[key], val)
                ins = op.emit(eng)
                if op.is_dma:
                    ins.then_inc(sems[op.sem], 16)
                elif op.inc:
                    ins.then_inc(sems[(e, "g", (op.count - 1) // GEN)], 1)
            if e == "sp":
                for key, val in last_dma.items():
                    if seen.get(key, 0) < val:
                        eng.wait_ge(sems[key], val)

        @block.tensor
        def _(eng):
            run("pe", eng)

        @block.scalar
        def _(eng):
            run("act", eng)

        @block.vector
        def _(eng):
            run("dve", eng)

        @block.gpsimd
        def _(eng):
            run("pool", eng)

        @block.sync
        def _(eng):
            run("sp", eng)


class T:
    def __init__(self, t, name):
        self.t = t
        self.b = Buf(name)

    def __getitem__(self, k):
        return self.t[k]


class V(T):
    def __init__(self, ap, name, host):
        self.t = ap
        self.b = Buf(name)
        self.host = host
        if not hasattr(host, "views"):
            host.views = []
        host.views.append(self)


class PS:
    def __init__(self, t, off, bufs, lock=None):
        self.t = t
        self.off = off
        self.bufs = bufs
        self.lock = lock

    def ap(self, c0, c1, p0=0, p1=128):
        return self.t[p0:p1, self.off + c0:self.off + c1]


def _locks(*lists):
    out = []
    for xs in lists:
        for x in xs:
            if isinstance(x, PS) and x.lock is not None and x.lock not in out:
                out.append(x.lock)
    return out


def _hosts(*lists):
    out = []
    for xs in lists:
        for x in xs:
            if isinstance(x, V):
                out.append(x.host.b)
    return out


def _bufs(xs):
    out = []
    for x in xs:
        if isinstance(x, T):
            out.append(x.b)
            for v in getattr(x, "views", ()):
                out.append(v.b)
        elif isinstance(x, PS):
            out.extend(x.bufs)
        elif isinstance(x, Buf):
            out.append(x)
        else:
            raise TypeError(type(x))
    return out


def build_nc():
    nc = bass.Bass("TRN2", target_bir_lowering=False)

    def din(name, shape):
        return nc.dram_tensor(name, list(shape), F32, kind="ExternalInput").ap()

    def dout(name, shape):
        return nc.dram_tensor(name, list(shape), F32, kind="ExternalOutput").ap()

    xp = din("xp", [T_PROMPT, D])
    xs = din("xs", [T_S, D])
    cpr = din("cp", [D])
    csm = din("cs", [D])
    sgla = din("sgla", [4, 128, 256])
    sgdn = din("sgdn", [8, 128, 128])
    cconv = din("cconv", [3, 3072])
    w_ada = din("w_ada", [D, 3 * D])
    b_ada = din("b_ada", [3 * D])
    g1 = din("g1", [D])
    w_in = din("w_in", [D, D_IN])
    w_gk2 = din("w_gk2", [16, 512])
    b_gk = din("b_gk", [512])
    w_conv = din("w_conv", [4, 3072])
    a_log = din("a_log", [8])
    dt_bias = din("dt_bias", [8])
    gna = din("gna", [256])
    gnb = din("gnb", [128])
    w_pa = din("w_pa", [D, D])
    w_pb = din("w_pb", [D, D])
    w_out = din("w_out", [D, D])
    gfin = din("gfin", [D])
    yp = dout("yp", [T_PROMPT, D])
    ys = dout("ys", [T_S, D])
    o_gla = {"p": dout("o_gla_p", [4, 128, 256]), "s": dout("o_gla_s", [4, 128, 256])}
    o_gdn = {"p": dout("o_gdn_p", [8, 128, 128]), "s": dout("o_gdn_s", [8, 128, 128])}
    o_conv = {"p": dout("o_conv_p", [3, 3072]), "s": dout("o_conv_s", [3, 3072])}

    es = ExitStack()
    with es:
        S = Sched()

        def sb(name, shape, dt=F32):
            return T(es.enter_context(nc.sbuf_tensor(name, list(shape), dt)), name)

        banks = [es.enter_context(nc.psum_tensor(f"bank{i}", [128, 512], F32)) for i in range(8)]
        qbufs = [[Buf(f"ps{i}_{q}") for q in range(4)] for i in range(8)]
        pptr = [0]

        blocks = [Buf(f"lock{i}") for i in range(8)]

        def palloc(n=1):
            b = pptr[0] % 7
            pptr[0] += 1
            return PS(banks[b], 0, [qbufs[b][0]], blocks[b])

        def chk(rd, wr, *aps):
            lk = _locks(rd, wr)
            for a in aps:
                if a is None or isinstance(a, (int, float)):
                    continue
                nm = a.name
                if nm.startswith("bank"):
                    assert blocks[int(nm[4:])] in lk, f"undeclared PSUM access {nm}"

        import os as _osw
        NWARM = int(_osw.environ.get("NWARM", "3"))

        def warm(n=None):
            for _ in range(NWARM if n is None else n):
                S.op("pe", lambda e: e.matmul(banks[7][:, 0:512], lhsT=identb[:], rhs=Wbig[:, 0, 0:512], start=True, stop=True), (), ())

        if AUTOWARM > 0:
            S.autowarm = lambda e: e.matmul(banks[7][:, 0:512], lhsT=identb[:], rhs=Wbig[:, 0, 0:512], start=True, stop=True)

        def mm(dst, lhsT, rhs, rd, wr, start=True, stop=True):
            chk(rd, wr, dst, lhsT, rhs)
            S.op("pe", lambda e: e.matmul(dst, lhsT=lhsT, rhs=rhs, start=start, stop=stop), _bufs(rd) + _hosts(rd, wr), _bufs(wr), _locks(rd, wr))

        def ACT(out, in_, func, rd, wr, scale=None, bias=None, accum=None):
            chk(rd, wr, out, in_, scale, bias, accum)
            kw = {}
            if scale is not None:
                kw["scale"] = scale
            if bias is not None:
                kw["bias"] = bias
            if accum is not None:
                kw["accum_out"] = accum
            S.op("act", lambda e: e.activation(out=out, in_=in_, func=func, **kw), _bufs(rd) + _hosts(rd, wr), _bufs(wr), _locks(rd, wr))

        def TS(eng, out, in0, s1, s2, op0, op1, rd, wr):
            chk(rd, wr, out, in0, s1, s2)
            if op1 is None:
                S.op(eng, lambda e: e.tensor_scalar(out=out, in0=in0, scalar1=s1, scalar2=None, op0=op0), _bufs(rd) + _hosts(rd, wr), _bufs(wr), _locks(rd, wr))
            else:
                S.op(eng, lambda e: e.tensor_scalar(out=out, in0=in0, scalar1=s1, scalar2=s2, op0=op0, op1=op1), _bufs(rd) + _hosts(rd, wr), _bufs(wr), _locks(rd, wr))

        def TT(eng, out, in0, in1, op, rd, wr):
            chk(rd, wr, out, in0, in1)
            S.op(eng, lambda e: e.tensor_tensor(out=out, in0=in0, in1=in1, op=op), _bufs(rd) + _hosts(rd, wr), _bufs(wr), _locks(rd, wr))

        def STT(out, in0, sc, in1, op0, op1, rd, wr):
            chk(rd, wr, out, in0, sc, in1)
            S.op("dve", lambda e: e.scalar_tensor_tensor(out=out, in0=in0, scalar=sc, in1=in1, op0=op0, op1=op1), _bufs(rd) + _hosts(rd, wr), _bufs(wr), _locks(rd, wr))

        def CP(eng, out, in_, rd, wr):
            chk(rd, wr, out, in_)
            if eng == "act":
                S.op("act", lambda e: e.copy(out=out, in_=in_), _bufs(rd) + _hosts(rd, wr), _bufs(wr), _locks(rd, wr))
            else:
                S.op(eng, lambda e: e.tensor_copy(out=out, in_=in_), _bufs(rd) + _hosts(rd, wr), _bufs(wr), _locks(rd, wr))

        def MEMSET(eng, ap, val, wr):
            S.op(eng, lambda e: e.memset(ap, val), _hosts(wr), _bufs(wr))

        def DMA(q, out, in_, rd, wr, slow=False):
            if slow:
                S.dma(q, lambda e: e.dma_start(out=out, in_=in_, allow_slow_non_contiguous=True), _bufs(rd) + _hosts(rd, wr), _bufs(wr))
            else:
                S.dma(q, lambda e: e.dma_start(out=out, in_=in_), _bufs(rd) + _hosts(rd, wr), _bufs(wr))

        def POW(out, in0, in1, rd, wr):
            S.op("pool", lambda e: e.tensor_tensor(out=out, in0=in0, in1=in1, op=ALU.pow), _bufs(rd), _bufs(wr))

        Wbig = es.enter_context(nc.sbuf_tensor("Wbig", [128, 8, 4096], BF16))
        WA = T(Wbig, "WA")
        WB = T(Wbig, "WB")
        WBUF = [(WA, 0), (WB, 2048)]
        wsm = sb("wsm", [128, 8, 32], BF16)
        hT = [sb(f"hT{j}", [128, 8, 128], BF16) for j in range(NSLOT)]
        OA = [sb(f"OA{j}", [128, 1024], BF16) for j in range(NSLOT)]
        OB = [sb(f"OB{j}", [128, 1024], BF16) for j in range(NSLOT)]
        identf = sb("identf", [128, 128])
        identb = sb("identb", [128, 128], BF16)
        identr = sb("identr", [128, 128], F32R)
        M_le = sb("M_le", [128, 128])
        M_gt = sb("M_gt", [128, 128])
        BD_le = sb("BD_le", [128, 128])
        BD_ge = sb("BD_ge", [128, 128])
        BD_gt = sb("BD_gt", [128, 128])
        M_le_r = sb("M_le_r", [128, 128], F32R)
        M_gt_r = sb("M_gt_r", [128, 128], F32R)
        BD_le_r = sb("BD_le_r", [128, 128], F32R)
        BD_gt_r = sb("BD_gt_r", [128, 128], F32R)
        ones_r = sb("ones_r", [128, 128], F32R)
        onesf = sb("onesf", [128, 128])
        ones2 = sb("ones2", [128, 2], BF16)
        chunkind = sb("chunkind", [128, 2])
        valid_s = sb("valid_s", [128, 1])
        neghalf = sb("neghalf", [128, 16])
        poshalf = sb("poshalf", [128, 16])
        gate_bc1 = sb("gate_bc", [128, 1024])
        gate_bc = {"p": gate_bc1, "s": gate_bc1}
        gf_bc = sb("gf_bc", [128, 1024])
        gbc_a = sb("gbc_a", [128, 256])
        gbc_b = sb("gbc_b", [128, 128])
        wcv = sb("wcv", [128, 24, 4])
        negA = sb("negA", [128, 8])
        dtb = sb("dtb", [128, 8])
        a1 = {"p": sb("a1p", [128, 8]), "s": sb("a1s", [128, 8])}
        sh = {"p": sb("shp", [128, 8]), "s": sb("shs", [128, 8])}
        wgk_r = sb("wgk_r", [17, 512], F32R)
        glT = sb("glT", [17, 128], F32R)
        Sg = [sb(f"Sg{h}", [128, 256]) for h in range(4)]
        Sgb = [sb(f"Sgb{h}", [128, 256], BF16) for h in range(4)]
        Sd4 = [sb(f"Sd4_{i}", [128, 4, 128]) for i in range(2)]
        Sdb4 = [sb(f"Sdb4_{i}", [128, 4, 128], BF16) for i in range(2)]
        carry = sb("carry", [128, 24, 3])
        xt = sb("xt", [128, 1024])
        xt2 = xt
        big0 = sb("big0", [128, 1024])
        big1 = sb("big1", [128, 1024])
        xb = sb("xb", [128, 1024], BF16)
        mgb = xb
        FB = [sb(f"FB{i}", [128, 512]) for i in range(3)]
        HB = [sb(f"HBb{i}", [128, 512], BF16) for i in range(1)]
        Fq = [sb(f"Fq{i}", [128, 128]) for i in range(6)]
        ACTB2 = [[sb(f"ACTB{p}_{k}", [128, 4, 128], BF16) for k in range(3)] for p in range(2)]
        ACTB = ACTB2[0]
        Hq2 = [[V(ACTB2[p][i // 4].t[:, i % 4, :], f"Hq{p}_{i}", ACTB2[p][i // 4]) for i in range(12)] for p in range(2)]
        Hq = Hq2[0] + [sb(f"Hq{i}", [128, 128], BF16) for i in (12, 13)]
        RW = sb("RW", [128, 4, 131])
        FW = sb("FW", [128, 512])
        KES, KSD0, KSD1, VS, QKT, WT, U_ = [sb(n, [128, 4, 128], BF16) for n in ("KES", "KSD0", "KSD1", "VS", "QKT", "WT", "U_")]
        PRp, PTRp, XRp, QRp = [[sb(f"{n}{a}", [128, 2, 128], F32R) for a in range(2)] for n in ("PR", "PTR", "XR", "QR")]
        XFp = [sb(f"XF{a}", [128, 2, 128], BF16) for a in range(2)]
        r3 = lambda ap: ap.rearrange("p (h n) -> p h n", n=128)
        EE = V(r3(big0.t[:, 0:512]), "EE", big0)
        DL = V(r3(big0.t[:, 512:1024]), "DL", big0)
        UV = V(r3(big1.t[:, 0:512]), "UV", big1)
        OSB = V(r3(big1.t[:, 512:1024]), "OSB", big1)
        xtb = xt.t[:].bitcast(BF16)
        KS_TM = V(r3(xtb[:, 0:512]), "KS_TM", xt)
        KST = V(r3(xtb[:, 512:1024]), "KST", xt)
        QKSD = V(r3(xtb[:, 1024:1536]), "QKSD", xt)
        ZS2 = [V(big0.t[:, 0:512], "ZS2_0", big0), V(big0.t[:, 512:1024], "ZS2_1", big0)]
        T1G = V(xb.t[:].bitcast(F32), "T1G", xb)
        VB2 = [HB[0], V(xtb[:, 1536:2048], "VB2_1", xt)]
        gkr = sb("gkr", [128, 256], F32R)
        sm2 = [[sb(f"sm{p}_{i}", [128, 16]) for i in range(12)] for p in range(2)]
        smr2 = [[sb(f"smr{p}_{i}", [128, 16], F32R) for i in range(2)] for p in range(2)]
        sm, smr = sm2[0], smr2[0]
        ssq = [sb(f"ssq{i}", [128, 4]) for i in range(4)]
        junk = sb("junk", [128, 256], BF16)
        oT = sb("oT", [128, 8, 128], BF16)

        sems = {}
        for e in ("pe", "act", "dve", "pool"):
            for g in range(NGEN):
                sems[(e, "g", g)] = es.enter_context(nc.semaphore(f"s_{e}_{g}"))
        for e in ("sp", "pool"):
            for i in range(NDS):
                sems[(e, i)] = es.enter_context(nc.semaphore(f"d_{e}_{i}"))
        block = es.enter_context(nc.Block())

        def mask(dst, pattern, cm, cmp_op, base=0):
            MEMSET("pool", dst[:], 1.0, [dst])
            S.op("pool", lambda e: e.affine_select(out=dst[:], in_=dst[:], pattern=pattern, compare_op=cmp_op,
                                                   fill=0.0, base=base, channel_multiplier=cm), _bufs([dst]), _bufs([dst]))

        mask(identf, [[-1, 128]], 1, ALU.is_equal)
        mask(M_le, [[1, 128]], -1, ALU.is_ge)
        mask(M_gt, [[-1, 128]], 1, ALU.is_gt)
        mask(BD_le, [[1, 128]], -1, ALU.is_ge)
        MEMSET("pool", BD_le[0:64, 64:128], 0.0, [BD_le])
        mask(BD_ge, [[-1, 128]], 1, ALU.is_ge)
        MEMSET("pool", BD_ge[64:128, 0:64], 0.0, [BD_ge])
        mask(BD_gt, [[-1, 128]], 1, ALU.is_gt)
        MEMSET("pool", BD_gt[64:128, 0:64], 0.0, [BD_gt])
        MEMSET("pool", onesf[:], 1.0, [onesf])
        MEMSET("pool", chunkind[:], 0.0, [chunkind])
        MEMSET("pool", chunkind[0:64, 0:1], 1.0, [chunkind])
        MEMSET("pool", chunkind[64:128, 1:2], 1.0, [chunkind])
        MEMSET("pool", valid_s[:], 0.0, [valid_s])
        MEMSET("pool", valid_s[0:16, :], 1.0, [valid_s])
        MEMSET("pool", neghalf[:], -0.5, [neghalf])
        MEMSET("pool", poshalf[:], 0.5, [poshalf])
        CP("dve", identb[:], identf[:], [identf], [identb])
        CP("dve", identr[:], identf[:], [identf], [identr])
        CP("dve", M_le_r[:], M_le[:], [M_le], [M_le_r])
        CP("dve", M_gt_r[:], M_gt[:], [M_gt], [M_gt_r])
        CP("dve", BD_le_r[:], BD_le[:], [BD_le], [BD_le_r])
        CP("dve", BD_gt_r[:], BD_gt[:], [BD_gt], [BD_gt_r])
        CP("dve", ones_r[:], onesf[:], [onesf], [ones_r])
        CP("dve", ones2[:], onesf[:, 0:2], [onesf], [ones2])
        CP("dve", glT[:], onesf[0:17, :], [onesf], [glT])
        for u in ACTB2[0] + ACTB2[1] + Hq[12:] + HB + [U_]:
            MEMSET("pool", u[:], 0.0, [u])

        DMA("sp", gf_bc[:], gfin.partition_broadcast(128), [], [gf_bc])
        DMA("sp", gbc_a[:], gna.partition_broadcast(128), [], [gbc_a])
        DMA("sp", gbc_b[:], gnb.partition_broadcast(128), [], [gbc_b])
        TS("dve", gbc_a[:], gbc_a[:], 0.5, None, ALU.mult, None, [gbc_a], [gbc_a])
        TS("dve", gbc_b[:], gbc_b[:], 0.5, None, ALU.mult, None, [gbc_b], [gbc_b])
        DMA("sp", negA[:], a_log.partition_broadcast(128), [], [negA])
        DMA("sp", dtb[:], dt_bias.partition_broadcast(128), [], [dtb])
        ACT(negA[:], negA[:], AF.Exp, [negA], [negA])
        TS("dve", negA[:], negA[:], -1.0, None, ALU.mult, None, [negA], [negA])
        for c in range(24):
            DMA("sp", wcv[:, c, :], w_conv[:, c * 128:(c + 1) * 128].rearrange("i p -> p i"), [], [wcv], slow=True)
        DMA("sp", xt[0:16, 0:512], w_gk2, [], [xt])
        DMA("sp", xt[16:17, 0:512], b_gk.rearrange("(o n) -> o n", o=1), [], [xt])
        CP("dve", wgk_r[:], xt[0:17, 0:512], [xt], [wgk_r])
        DMA("pool", wsm[:, :, 0:16], w_in[:, C_GL:C_GL + 16].rearrange("(c p) n -> p c n", p=128), [], [wsm])
        DMA("pool", wsm[:, :, 16:32], w_in[:, C_BETA:C_BETA + 16].rearrange("(c p) n -> p c n", p=128), [], [wsm])

        for g in range(3):
            DMA("pool", Wbig[:, :, g * 1024:(g + 1) * 1024], w_ada[:, g * 1024:(g + 1) * 1024].rearrange("(c p) n -> p c n", p=128), [], [WA, WB])
        c2 = sb("c2", [128, 8, 2])
        c2b = sb("c2b", [128, 8, 2], BF16)
        c2t = sb("c2t", [128, 8, 2])
        DMA("sp", c2[:, :, 0], cpr.rearrange("(c p) -> p c", p=128), [], [c2], slow=True)
        DMA("sp", c2[:, :, 1], csm.rearrange("(c p) -> p c", p=128), [], [c2], slow=True)
        ACT(c2t[:], c2[:], AF.Tanh, [c2], [c2t], scale=0.5)
        STT(c2t[:], c2t[:], 1.0, c2[:], ALU.add, ALU.mult, [c2t, c2], [c2t])
        TS("dve", c2b[:], c2t[:], 0.5, None, ALU.mult, None, [c2t], [c2b])
        badT = sb("badT", [128, 24])
        g1T = sb("g1T", [128, 8])
        DMA("sp", badT[:], b_ada.rearrange("(c p) -> p c", p=128), [], [badT], slow=True)
        DMA("sp", g1T[:], g1.rearrange("(c p) -> p c", p=128), [], [g1T], slow=True)
        pm = palloc(1)
        for n in range(24):
            for kc in range(8):
                mm(pm.ap(2 * n, 2 * n + 2), Wbig[:, kc, n * 128:(n + 1) * 128], c2b[:, kc, :], [WA, WB, c2b], [pm], start=(kc == 0), stop=(kc == 7))
        modT = sb("modT", [128, 24, 2])
        pmv = pm.ap(0, 48).rearrange("p (n k) -> p n k", k=2)
        for k in range(2):
            TT("dve", modT[:, :, k], pmv[:, :, k], badT[:], ALU.add, [pm, badT], [modT])
        for k, kind in enumerate(("p", "s")):
            STT(a1[kind][:], modT[:, 8:16, k], 1.0, g1T[:], ALU.add, ALU.mult, [modT, g1T], [a1[kind]])
            CP("dve", sh[kind][:], modT[:, 0:8, k], [modT], [sh[kind]])
        def build_gate_bc(kind):
            k = 0 if kind == "p" else 1
            for g in range(2):
                pr = palloc()
                for c in range(4):
                    cc = g * 4 + c
                    gcol = Fq[c % 2]
                    TS("dve", gcol[:], onesf[:], modT[:, 16 + cc, k:k + 1], None, ALU.mult, None, [onesf, modT], [gcol])
                    mm(pr.ap(c * 128, (c + 1) * 128), gcol[:], identf[:], [gcol, identf], [pr])
                CP("act", gate_bc1[:, g * 512:(g + 1) * 512], pr.ap(0, 512), [pr], [gate_bc1])

        def xsrc(kind, ti):
            return xs if kind == "s" else xp[ti * 128:(ti + 1) * 128, :]

        def load_w(wt_off, col0, src, scol, n):
            wt, off = wt_off
            for c0 in range(0, n, 512):
                m = min(512, n - c0)
                DMA("pool", Wbig[:, :, off + col0 + c0:off + col0 + c0 + m],
                    src[:, scol + c0:scol + c0 + m].rearrange("(c p) n -> p c n", p=128), [], [wt])

        def rstd_from(ssq_ap, rd, out_t, mult, eps):
            n = ssq_ap.shape[1]
            TS("dve", out_t[:, 0:n], ssq_ap, mult, eps, ALU.mult, ALU.add, rd, [out_t])
            POW(out_t[:, 0:n], out_t[:, 0:n], neghalf[:, 0:n], [out_t, neghalf], [out_t])
            return out_t

        def stage_h(tiles):
            XB = [xt, big0]
            for j, (kind, ti) in enumerate(tiles):
                xin = XB[j % 2]
                if kind == "s":
                    MEMSET("dve", xin[:], 0.0, [xin])
                    DMA("sp", xin[0:16, :], xs, [], [xin])
                else:
                    DMA("sp", xin[:], xsrc(kind, ti), [], [xin])
                ACT(xb[:], xin[:], AF.Square, [xin], [xb, ssq[0]], accum=ssq[0][:, 0:1])
                r = rstd_from(ssq[0][:, 0:1], [ssq[0]], sm[0], 1.0 / D, EPS)
                TS("dve", xb[:], xin[:], r[:, 0:1], None, ALU.mult, None, [xin, r], [xb])
                for half in range(2):
                    pt = palloc(4)
                    for c in range(4):
                        cc = half * 4 + c
                        mm(pt.ap(c * 128, (c + 1) * 128), xb[:, cc * 128:(cc + 1) * 128], identb[:], [xb, identb], [pt])
                    for c in range(4):
                        cc = half * 4 + c
                        if c % 2 == 0:
                            ACT(hT[j][:, cc, :], pt.ap(c * 128, (c + 1) * 128), AF.Identity, [pt, a1[kind], sh[kind]], [hT[j]],
                                scale=a1[kind][:, cc:cc + 1], bias=sh[kind][:, cc:cc + 1])
                        else:
                            TS("dve", hT[j][:, cc, :], pt.ap(c * 128, (c + 1) * 128), a1[kind][:, cc:cc + 1], sh[kind][:, cc:cc + 1],
                               ALU.mult, ALU.add, [pt, a1[kind], sh[kind]], [hT[j]])

        def proj_tm(j, wt_off, col0, ncols, dst):
            wt, off = wt_off
            for kc in range(8):
                mm(dst.ap(0, ncols), hT[j][:, kc, :], Wbig[:, kc, off + col0:off + col0 + ncols], [hT[j], wt], [dst], start=(kc == 0), stop=(kc == 7))

        def proj_fm(j, wt_off, col0, dst_ap, dst):
            wt, off = wt_off
            for kc in range(8):
                mm(dst_ap, Wbig[:, kc, off + col0:off + col0 + 128], hT[j][:, kc, :], [hT[j], wt], [dst], start=(kc == 0), stop=(kc == 7))

        def state_io_gla(kind, heads, load):
            for h in heads:
                if load:
                    if kind == "s":
                        DMA("sp", Sg[h][:], sgla[h], [], [Sg[h]])
                    else:
                        MEMSET("dve", Sg[h][:], 0.0, [Sg[h]])
                    CP("act", Sgb[h][:], Sg[h][:], [Sg[h]], [Sgb[h]])
                else:
                    DMA("sp", o_gla[kind][h], Sg[h][:], [Sg[h]], [])

        def stage_g(tiles, pair, wt_off, first_pass, last_pass, barrier=True):
            heads = (2 * pair, 2 * pair + 1)
            sc_q = 128.0 ** -0.5

            def front(j):
                kind, ti = tiles[j]
                P = j % 2
                hq, smp = Hq2[P], sm2[P]
                qtil, qdec, ktil, kbf, kdec, attm = hq[0:2], hq[2:4], hq[4:6], hq[6:8], hq[8:10], hq[10:12]
                vbf, zs = VB2[P], ZS2[P]
                pg1 = palloc()
                for kc in range(8):
                    mm(pg1.ap(0, 128, 0, 16), wsm[:, kc, 0:16], hT[j][:, kc, :], [wsm, hT[j]], [pg1], start=(kc == 0), stop=(kc == 7))
                CP("act", glT[0:16, :], pg1.ap(0, 128, 0, 16), [pg1], [glT])
                pg2 = palloc()
                mm(pg2.ap(0, 256), glT[:], wgk_r[:, pair * 256:(pair + 1) * 256], [glT, wgk_r], [pg2])
                e0 = FB[0]
                ACT(e0[:, 0:256], pg2.ap(0, 256), AF.Exp, [pg2], [e0], scale=-1.0)
                ACT(e0[:, 0:256], e0[:, 0:256], AF.Ln, [e0], [e0], bias=1.0)
                yield
                pv = palloc()
                proj_tm(j, wt_off, 512, 512, pv)
                CP("act", vbf[:], pv.ap(0, 512), [pv], [vbf])
                if kind == "s":
                    TS("dve", gkr[:], e0[:, 0:256], -1.0 / 16.0, valid_s[:, 0:1], ALU.mult, ALU.mult, [e0, valid_s], [gkr])
                else:
                    TS("dve", gkr[:], e0[:, 0:256], -1.0 / 16.0, None, ALU.mult, None, [e0], [gkr])
                yield
                pb = palloc()
                for hh in range(2):
                    mm(pb.ap(hh * 128, (hh + 1) * 128), gkr[:, hh * 128:(hh + 1) * 128], M_le_r[:], [gkr, M_le_r], [pb])
                prv = palloc()
                mm(prv.ap(0, 256), M_gt_r[:], gkr[:], [gkr, M_gt_r], [prv])
                bref, nbref, ebl = smp[1], smp[2], smp[5]
                E1, E2, E3 = Fq[0:2], Fq[2:4], Fq[4:6]
                for hh in range(2):
                    CP("act", bref[:, hh:hh + 1], pb.ap(hh * 128 + 63, hh * 128 + 64), [pb], [bref])
                    ACT(E3[hh][:], pb.ap(hh * 128, (hh + 1) * 128), AF.Exp, [pb], [E3[hh]])
                E4 = FB[1]
                ACT(E4[:, 0:256], prv.ap(0, 256), AF.Exp, [prv], [E4])
                TS("dve", nbref[:, 0:2], bref[:, 0:2], -1.0, None, ALU.mult, None, [bref], [nbref])
                if kind == "s":
                    TS("dve", E4[:, 0:256], E4[:, 0:256], valid_s[:, 0:1], None, ALU.mult, None, [E4, valid_s], [E4])
                for hh in range(2):
                    ACT(E1[hh][:], pb.ap(hh * 128, (hh + 1) * 128), AF.Exp, [pb, nbref], [E1[hh]], bias=nbref[:, hh:hh + 1])
                    ACT(E2[hh][:], pb.ap(hh * 128, (hh + 1) * 128), AF.Exp, [pb, bref], [E2[hh]], scale=-1.0, bias=bref[:, hh:hh + 1])
                    CP("dve", ebl[:, hh:hh + 1], E3[hh][:, 127:128], [E3[hh]], [ebl])
                yield
                pq = palloc()
                pk = palloc()
                for hh in range(2):
                    proj_fm(j, wt_off, hh * 128, pq.ap(hh * 128, (hh + 1) * 128), pq)
                    proj_fm(j, wt_off, 256 + hh * 128, pk.ap(hh * 128, (hh + 1) * 128), pk)
                for hh in range(2):
                    STT(qtil[hh][:], pq.ap(hh * 128, (hh + 1) * 128), sc_q, E1[hh][:], ALU.mult, ALU.mult, [pq, E1[hh]], [qtil[hh]])
                    STT(qdec[hh][:], pq.ap(hh * 128, (hh + 1) * 128), sc_q, E3[hh][:], ALU.mult, ALU.mult, [pq, E3[hh]], [qdec[hh]])
                    CP("act", kbf[hh][:], pk.ap(hh * 128, (hh + 1) * 128), [pk], [kbf[hh]])
                    TT("dve", ktil[hh][:], pk.ap(hh * 128, (hh + 1) * 128), E2[hh][:], ALU.mult, [pk, E2[hh]], [ktil[hh]])
                yield
                pkt = palloc()
                for hh in range(2):
                    mm(pkt.ap(hh * 128, (hh + 1) * 128), kbf[hh][:], identb[:], [kbf[hh], identb], [pkt])
                pa = palloc()
                for hh in range(2):
                    mm(pa.ap(hh * 128, (hh + 1) * 128), ktil[hh][:], qtil[hh][:], [ktil[hh], qtil[hh]], [pa])
                for hh in range(2):
                    TT("dve", kdec[hh][:], pkt.ap(hh * 128, (hh + 1) * 128), E4[:, hh * 128:(hh + 1) * 128], ALU.mult, [pkt, E4], [kdec[hh]])
                    TT("dve", attm[hh][:], pa.ap(hh * 128, (hh + 1) * 128), M_le[:], ALU.mult, [pa, M_le], [attm[hh]])
                yield
                pz = palloc()
                proj_tm(j, wt_off, 1024, 512, pz)
                tz = FB[2]
                ACT(tz[:], pz.ap(0, 512), AF.Tanh, [pz], [tz], scale=0.5)
                STT(zs[:], tz[:], 1.0, pz.ap(0, 512), ALU.add, ALU.mult, [tz, pz], [zs])
                yield

            def back(j):
                kind, ti = tiles[j]
                P = j % 2
                hq, smp = Hq2[P], sm2[P]
                qdec, kdec, attm = hq[2:4], hq[8:10], hq[10:12]
                vbf, zs, ebl = VB2[P], ZS2[P], smp[5]
                if kind == "s" or (kind == "p" and ti == 0):
                    state_io_gla(kind, heads, True)
                pos = []
                for hh in range(2):
                    h = heads[hh]
                    pS = palloc()
                    mm(pS.ap(0, 256), kdec[hh][:], vbf[:, hh * 256:(hh + 1) * 256], [kdec[hh], vbf], [pS])
                    po = palloc()
                    pos.append(po)
                    mm(po.ap(0, 256), attm[hh][:], vbf[:, hh * 256:(hh + 1) * 256], [attm[hh], vbf], [po], start=True, stop=False)
                    mm(po.ap(0, 256), qdec[hh][:], Sgb[h][:], [qdec[hh], Sgb[h]], [po], start=False, stop=True)
                    STT(Sg[h][:], Sg[h][:], ebl[:, hh:hh + 1], pS.ap(0, 256), ALU.mult, ALU.add, [Sg[h], ebl, pS], [Sg[h]])
                    CP("act", Sgb[h][:], Sg[h][:], [Sg[h]], [Sgb[h]])
                    ACT(junk[:, 0:256], po.ap(0, 256), AF.Square, [po], [junk, ssq[1]], accum=ssq[1][:, hh:hh + 1])
                r = rstd_from(ssq[1][:, 0:2], [ssq[1]], smp[3], 1.0 / 256.0, EPS)
                yield
                t1 = T1G
                for hh in range(2):
                    h = heads[hh]
                    STT(t1[:, hh * 256:(hh + 1) * 256], pos[hh].ap(0, 256), r[:, hh:hh + 1], gbc_a[:], ALU.mult, ALU.mult, [pos[hh], r, gbc_a], [t1])
                    TT("dve", OA[j][:, h * 256:(h + 1) * 256], t1[:, hh * 256:(hh + 1) * 256], zs[:, hh * 256:(hh + 1) * 256], ALU.mult, [t1, zs], [OA[j]])
                if kind == "s" or (last_pass and j == len(tiles) - 1):
                    state_io_gla(kind, heads, False)
                yield

            n = len(tiles)

            def prologue():
                if barrier:
                    S.op("dve", lambda e: e.memset(junk[:, 0:2], 0.0), (), _bufs([junk, big0, big1, xt, xb] + ACTB2[0] + ACTB2[1]))
                yield from front(0)

            def body(next_pro=None):
                for j in range(n):
                    gens = [back(j)]
                    if j + 1 < n:
                        gens.append(front(j + 1))
                    elif next_pro is not None:
                        gens.append(next_pro)
                    while gens:
                        for g in list(gens):
                            try:
                                next(g)
                            except StopIteration:
                                gens.remove(g)

            return prologue, body

        def state_io_gdn(kind, half, load):
            for hh in range(4):
                h = 4 * half + hh
                if load:
                    if kind == "s":
                        DMA("sp", Sd4[half][:, hh, :], sgdn[h], [], [Sd4[half]])
                    else:
                        MEMSET("dve", Sd4[half][:, hh, :], 0.0, [Sd4[half]])
                else:
                    DMA("sp", o_gdn[kind][h], Sd4[half][:, hh, :], [Sd4[half]], [])
            if load:
                CP("act", Sdb4[half][:], Sd4[half][:], [Sd4[half]], [Sdb4[half]])

        def bc4(ap):
            return ap.unsqueeze(2).to_broadcast([ap.shape[0], 4, 128])

        def bcm(t):
            return t[:].unsqueeze(1).to_broadcast([128, 4, 128])

        def p3(ps_, p0=0, p1=128):
            return r3(ps_.ap(0, 512, p0, p1))

        def stage_d(tiles, half, wt_off, first_pass, last_pass, barrier=True):
            heads = list(range(4 * half, 4 * half + 4))
            gch = [[kind_i * 8 + h for h in heads] for kind_i in range(3)]
            Sd_, Sdb_ = Sd4[half], Sdb4[half]
            hs = slice(4 * half, 4 * half + 4)

            def front(j):
                kind, ti = tiles[j]
                P = j % 2
                smp, smrp, actb, hq = sm2[P], smr2[P], ACTB2[P], Hq2[P]
                nvalid = 16 if kind == "s" else 128
                if kind == "s" or (kind == "p" and ti == 0):
                    for grp in gch:
                        for c in grp:
                            if kind == "s":
                                DMA("sp", carry[:, c, :], cconv[:, c * 128:(c + 1) * 128].rearrange("i p -> p i"), [], [carry], slow=True)
                            else:
                                MEMSET("dve", carry[:, c, :], 0.0, [carry])
                psm = palloc()
                for kc in range(8):
                    mm(psm.ap(0, 16), hT[j][:, kc, :], wsm[:, kc, 16:32], [hT[j], wsm], [psm], start=(kc == 0), stop=(kc == 7))
                beta, sqb, gpre, g_r = smp[0], smp[1], smp[2], smrp[0]
                ACT(beta[:, 0:8], psm.ap(0, 8), AF.Tanh, [psm], [beta], scale=0.5)
                CP("act", gpre[:, 0:8], psm.ap(8, 16), [psm], [gpre])
                yield
                TS("dve", beta[:, 0:8], beta[:, 0:8], 0.5, 0.5, ALU.mult, ALU.add, [beta], [beta])
                TT("dve", gpre[:, 0:8], gpre[:, 0:8], dtb[:], ALU.add, [gpre, dtb], [gpre])
                yield
                POW(sqb[:, 0:8], beta[:, 0:8], poshalf[:, 0:8], [beta, poshalf], [sqb])
                ACT(gpre[:, 0:8], gpre[:, 0:8], AF.Exp, [gpre], [gpre])
                ACT(gpre[:, 0:8], gpre[:, 0:8], AF.Ln, [gpre], [gpre], bias=1.0)
                yield
                TT("dve", g_r[:, 0:8], gpre[:, 0:8], negA[:], ALU.mult, [gpre, negA], [g_r])
                if kind == "s":
                    TS("dve", g_r[:, 0:8], g_r[:, 0:8], valid_s[:, 0:1], None, ALU.mult, None, [g_r, valid_s], [g_r])
                gm = smrp[1]
                for c in range(2):
                    TS("dve", gm[:, c * 8:(c + 1) * 8], g_r[:, 0:8], chunkind[:, c:c + 1], None, ALU.mult, None, [g_r, chunkind], [gm])
                yield
                pcs = palloc()
                mm(pcs.ap(0, 8), BD_le_r[:], g_r[:, 0:8], [BD_le_r, g_r], [pcs])
                mm(pcs.ap(8, 16), BD_gt_r[:], g_r[:, 0:8], [BD_gt_r, g_r], [pcs])
                ebb = smp[3]
                ACT(ebb[:, 0:16], pcs.ap(0, 16), AF.Exp, [pcs], [ebb])
                pdl = palloc()
                mm(pdl.ap(0, 16), ones_r[:], gm[:, 0:16], [ones_r, gm], [pdl])
                dlast = smp[4]
                ACT(dlast[:, 0:16], pdl.ap(0, 16), AF.Exp, [pdl], [dlast])
                yield
                for kind_i in range(3):
                    pp = palloc()
                    for hh in range(4):
                        proj_fm(j, wt_off, kind_i * 512 + hh * 128, pp.ap(hh * 128, (hh + 1) * 128), pp)
                    c0 = gch[kind_i][0]
                    CP("dve", RW[:, :, 0:3], carry[:, c0:c0 + 4, :], [carry], [RW])
                    CP("act", RW[:, :, 3:131], p3(pp), [pp], [RW])
                    yield
                    y3, t3, ty3 = r3(FW[:]), r3(FB[1][:]), r3(FB[2][:])
                    TT("dve", y3, RW[:, :, 0:128], bc4(wcv[:, c0:c0 + 4, 0]), ALU.mult, [RW, wcv], [FW])
                    for tap in range(1, 4):
                        TT("dve", t3, RW[:, :, tap:tap + 128], bc4(wcv[:, c0:c0 + 4, tap]), ALU.mult, [RW, wcv], [FB[1]])
                        TT("dve", y3, y3, t3, ALU.add, [FW, FB[1]], [FW])
                        if tap == 2:
                            yield
                    CP("dve", carry[:, c0:c0 + 4, :], RW[:, :, nvalid:nvalid + 3], [RW], [carry])
                    ACT(ty3, y3, AF.Tanh, [FW], [FB[2]], scale=0.5)
                    yield
                    STT(actb[kind_i][:], ty3, 1.0, y3, ALU.add, ALU.mult, [FB[2], FW], [actb[kind_i]])
                if kind == "s" or (last_pass and j == len(tiles) - 1):
                    for grp in gch:
                        for c in grp:
                            DMA("sp", o_conv[kind][:, c * 128:(c + 1) * 128].rearrange("i p -> p i"), carry[:, c, :], [carry], [], slow=True)
                yield
                pss = palloc()
                for kind_i in range(2):
                    for hh in range(4):
                        sq = Hq[12 + (hh % 2)]
                        src = hq[kind_i * 4 + hh]
                        ACT(sq[:], src[:], AF.Square, [src], [sq])
                        cidx = (kind_i * 4 + hh) * 2
                        mm(pss.ap(cidx, cidx + 2), sq[:], ones2[:], [sq, ones2], [pss])
                rqk = smp[5]
                CP("act", rqk[:, 0:16], pss.ap(0, 16), [pss], [rqk])
                yield
                pssv = rqk[:, 0:16].rearrange("p (n k) -> p n k", k=2)[:, :, 0]
                TS("dve", smp[0][:, 8:16], pssv, 1.0, 4.0 * EPS, ALU.mult, ALU.add, [rqk], [smp[0]])
                CP("dve", rqk[:, 0:8], smp[0][:, 8:16], [smp[0]], [rqk])
                POW(rqk[:, 0:8], rqk[:, 0:8], neghalf[:, 0:8], [rqk, neghalf], [rqk])
                yield
                c_qs, c_o1, ck, ckes, ckd0, ckd1, cv = smp[6], smp[7], smp[8], smp[9], smp[10], smp[11], smp[2]
                TS("dve", c_qs[:, 0:4], rqk[:, 0:4], 128.0 ** -0.5, None, ALU.mult, None, [rqk], [c_qs])
                TT("dve", c_o1[:, 0:4], c_qs[:, 0:4], ebb[:, hs], ALU.mult, [c_qs, ebb], [c_o1])
                TT("dve", ck[:, 0:4], rqk[:, 4:8], sqb[:, hs], ALU.mult, [rqk, sqb], [ck])
                TT("dve", ckes[:, 0:4], ck[:, 0:4], ebb[:, hs], ALU.mult, [ck, ebb], [ckes])
                ebl = ebb[:, 8 + 4 * half:8 + 4 * half + 4]
                for c, ckd in enumerate((ckd0, ckd1)):
                    STT(ckd[:, 0:4], ck[:, 0:4], chunkind[:, c:c + 1], ebl, ALU.mult, ALU.mult, [ck, chunkind, ebb], [ckd])
                    if kind == "s":
                        TS("dve", ckd[:, 0:4], ckd[:, 0:4], valid_s[:, 0:1], None, ALU.mult, None, [ckd, valid_s], [ckd])
                TS("dve", cv[:, 0:4], sqb[:, hs], 0.5, None, ALU.mult, None, [sqb], [cv])
                yield

            def back(j):
                kind, ti = tiles[j]
                P = j % 2
                smp, smrp, hq = sm2[P], smr2[P], Hq2[P]
                g_r, ebb, dlast = smrp[0], smp[3], smp[4]
                c_qs, c_o1, ck, ckes, ckd0, ckd1, cv = smp[6], smp[7], smp[8], smp[9], smp[10], smp[11], smp[2]
                qs = [hq[hh] for hh in range(4)]
                k0 = [hq[4 + hh] for hh in range(4)]
                v0 = [hq[8 + hh] for hh in range(4)]
                if kind == "s" or (kind == "p" and ti == 0):
                    state_io_gdn(kind, half, True)
                pz = palloc()
                proj_tm(j, wt_off, 1536, 512, pz)
                zs = FB[0]
                ACT(zs[:], pz.ap(0, 512), AF.Tanh, [pz], [zs], scale=0.5)
                STT(zs[:], zs[:], 1.0, pz.ap(0, 512), ALU.add, ALU.mult, [zs, pz], [zs])
                yield
                for a in range(2):
                    ga_ = g_r[:, 4 * half + 2 * a:4 * half + 2 * a + 2]
                    TT("dve", QRp[a][:], M_gt[:].unsqueeze(1).to_broadcast([128, 2, 128]), ga_.unsqueeze(2).to_broadcast([128, 2, 128]), ALU.mult, [M_gt, g_r], [QRp[a]])
                pdf = palloc()
                for hh in range(4):
                    mm(pdf.ap(hh * 128, (hh + 1) * 128), M_le_r[:], QRp[hh // 2][:, hh % 2, :], [M_le_r, QRp[hh // 2]], [pdf])
                ACT(EE[:], p3(pdf), AF.Exp, [pdf], [EE])
                TT("dve", DL[:], EE[:], bcm(BD_ge), ALU.mult, [EE, BD_ge], [DL])
                TT("dve", EE[:], EE[:], bcm(BD_gt), ALU.mult, [EE, BD_gt], [EE])
                pkt = palloc()
                pvt = palloc()
                for hh in range(4):
                    mm(pkt.ap(hh * 128, (hh + 1) * 128), k0[hh][:], identb[:], [k0[hh], identb], [pkt])
                for hh in range(4):
                    mm(pvt.ap(hh * 128, (hh + 1) * 128), v0[hh][:], identb[:], [v0[hh], identb], [pvt])
                TT("dve", KS_TM[:], p3(pkt), bc4(ck[:, 0:4]), ALU.mult, [pkt, ck], [KS_TM])
                TT("dve", KES[:], p3(pkt), bc4(ckes[:, 0:4]), ALU.mult, [pkt, ckes], [KES])
                TT("dve", KSD0[:], p3(pkt), bc4(ckd0[:, 0:4]), ALU.mult, [pkt, ckd0], [KSD0])
                TT("dve", KSD1[:], p3(pkt), bc4(ckd1[:, 0:4]), ALU.mult, [pkt, ckd1], [KSD1])
                TT("dve", VS[:], p3(pvt), bc4(cv[:, 0:4]), ALU.mult, [pvt, cv], [VS])
                yield
                pkT = palloc()
                for hh in range(4):
                    mm(pkT.ap(hh * 128, (hh + 1) * 128), KS_TM[:, hh, :], identb[:], [KS_TM, identb], [pkT])
                CP("act", KST[:], p3(pkT), [pkT], [KST])
                yield
                pkk = palloc()
                pqk = palloc()
                for hh in range(4):
                    mm(pkk.ap(hh * 128, (hh + 1) * 128), KST[:, hh, :], KST[:, hh, :], [KST], [pkk])
                for hh in range(4):
                    mm(pqk.ap(hh * 128, (hh + 1) * 128), qs[hh][:], KST[:, hh, :], [qs[hh], KST], [pqk])
                for a in range(2):
                    TT("dve", PTRp[a][:], r3(pkk.ap(a * 256, a * 256 + 256)), EE[:, 2 * a:2 * a + 2, :], ALU.mult, [pkk, EE], [PTRp[a]])
                TT("dve", DL[:], DL[:], bc4(c_qs[:, 0:4]), ALU.mult, [DL, c_qs], [DL])
                TT("dve", QKSD[:], p3(pqk), DL[:], ALU.mult, [pqk, DL], [QKSD])
                yield
                pqT = palloc()
                for hh in range(4):
                    mm(pqT.ap(hh * 128, (hh + 1) * 128), QKSD[:, hh, :], identb[:], [QKSD, identb], [pqT])
                pUs = []
                for a in range(2):
                    pU = palloc()
                    pUs.append(pU)
                    for h2 in range(2):
                        mm(pU.ap(h2 * 128, (h2 + 1) * 128), PTRp[a][:, h2, :], identr[:], [PTRp[a], identr], [pU])
                bcm2 = lambda t: t[:].unsqueeze(1).to_broadcast([128, 2, 128])
                p2 = lambda ps_: r3(ps_.ap(0, 256))
                for a in range(2):
                    CP("act", PRp[a][:], p2(pUs[a]), [pUs[a]], [PRp[a]])
                    TT("dve", XRp[a][:], bcm2(identf), p2(pUs[a]), ALU.subtract, [identf, pUs[a]], [XRp[a]])
                CP("act", QKT[:], p3(pqT), [pqT], [QKT])
                yield
                def x_mm(lvl):
                    pXs = [None, None]
                    for a in range(2):
                        pXs[a] = palloc()
                        for h2 in range(2):
                            mm(pXs[a].ap(h2 * 128, (h2 + 1) * 128), QRp[a][:, h2, :], XRp[a][:, h2, :], [QRp[a], XRp[a]], [pXs[a]])
                    return pXs

                def x_evac(lvl, pXs):
                    for a in range(2):
                        xeng = "act" if a == 0 else "dve"
                        if lvl < 5:
                            CP(xeng, XRp[a][:], p2(pXs[a]), [pXs[a]], [XRp[a]])
                        else:
                            CP(xeng, XFp[a][:], p2(pXs[a]), [pXs[a]], [XFp[a]])

                for lvl in range(1, 6):
                    pPs, pPTs = [None, None], [None, None]
                    for a in range(2):
                        if lvl < 5:
                            pPs[a] = palloc()
                            for h2 in range(2):
                                mm(pPs[a].ap(h2 * 128, (h2 + 1) * 128), PTRp[a][:, h2, :], PRp[a][:, h2, :], [PTRp[a], PRp[a]], [pPs[a]])
                        pPTs[a] = palloc()
                        for h2 in range(2):
                            mm(pPTs[a].ap(h2 * 128, (h2 + 1) * 128), PRp[a][:, h2, :], PTRp[a][:, h2, :], [PRp[a], PTRp[a]], [pPTs[a]])
                    pXprev = x_mm(lvl - 1) if lvl > 1 else None
                    for a in range(2):
                        if lvl < 5:
                            CP("act", PRp[a][:], p2(pPs[a]), [pPs[a]], [PRp[a]])
                            CP("act", PTRp[a][:], p2(pPTs[a]), [pPTs[a]], [PTRp[a]])
                    for a in range(2):
                        TT("dve", QRp[a][:], p2(pPTs[a]), bcm2(identf), ALU.add, [pPTs[a], identf], [QRp[a]])
                    if pXprev is not None:
                        x_evac(lvl - 1, pXprev)
                    yield
                x_evac(5, x_mm(5))
                pw = palloc()
                for hh in range(4):
                    mm(pw.ap(hh * 128, (hh + 1) * 128), KES[:, hh, :], XFp[hh // 2][:, hh % 2, :], [KES, XFp[hh // 2]], [pw])
                puv = palloc()
                for hh in range(4):
                    mm(puv.ap(hh * 128, (hh + 1) * 128), XFp[hh // 2][:, hh % 2, :], VS[:, hh, :], [XFp[hh // 2], VS], [puv])
                CP("act", WT[:], p3(pw), [pw], [WT])
                CP("act", UV[:], p3(puv), [puv], [UV])
                yield
                for c in range(2):
                    r0, r1 = 64 * c, 64 * c + 64
                    pws = palloc()
                    for hh in range(4):
                        mm(pws.ap(hh * 128, (hh + 1) * 128, r0, r1), WT[:, hh, r0:r1], Sdb_[:, hh, :], [WT, Sdb_], [pws])
                    po1 = palloc()
                    for hh in range(4):
                        mm(po1.ap(hh * 128, (hh + 1) * 128, r0, r1), qs[hh][:, r0:r1], Sdb_[:, hh, :], [qs[hh], Sdb_], [po1])
                    TT("dve", U_[r0:r1, :, :], UV[r0:r1, :, :], p3(pws, r0, r1), ALU.subtract, [UV, pws], [U_])
                    dl_c = dlast[:, c * 8 + 4 * half:c * 8 + 4 * half + 4]
                    TT("dve", Sd_[:], Sd_[:], bc4(dl_c), ALU.mult, [Sd_, dlast], [Sd_])
                    pS = palloc()
                    KSD = KSD0 if c == 0 else KSD1
                    for hh in range(4):
                        mm(pS.ap(hh * 128, (hh + 1) * 128), KSD[:, hh, :], U_[:, hh, :], [KSD, U_], [pS])
                    po2 = palloc()
                    for hh in range(4):
                        mm(po2.ap(hh * 128, (hh + 1) * 128, r0, r1), QKT[:, hh, r0:r1], U_[:, hh, :], [QKT, U_], [po2])
                    TT("dve", Sd_[:], Sd_[:], p3(pS), ALU.add, [Sd_, pS], [Sd_])
                    CP("act", Sdb_[:], Sd_[:], [Sd_], [Sdb_])
                    TT("dve", OSB[r0:r1, :, :], p3(po1, r0, r1), bc4(c_o1[r0:r1, 0:4]), ALU.mult, [po1, c_o1], [OSB])
                    TT("dve", OSB[r0:r1, :, :], OSB[r0:r1, :, :], p3(po2, r0, r1), ALU.add, [OSB, po2], [OSB])
                    yield
                for hh in range(4):
                    ACT(junk[:, 0:128], OSB[:, hh, :], AF.Square, [OSB], [junk, ssq[2]], accum=ssq[2][:, hh:hh + 1])
                rt = ssq[3]
                TS("dve", rt[:, 0:4], ssq[2][:, 0:4], 1.0 / 128.0, EPS, ALU.mult, ALU.add, [ssq[2]], [rt])
                POW(rt[:, 0:4], rt[:, 0:4], neghalf[:, 0:4], [rt, neghalf], [rt])
                yield
                TT("dve", OSB[:], OSB[:], bc4(rt[:, 0:4]), ALU.mult, [OSB, rt], [OSB])
                TT("dve", OSB[:], OSB[:], bcm(gbc_b), ALU.mult, [OSB, gbc_b], [OSB])
                TT("dve", r3(OB[j][:, half * 512:(half + 1) * 512]), OSB[:], r3(zs[:]), ALU.mult, [OSB, zs], [OB[j]])
                if kind == "s" or (last_pass and j == len(tiles) - 1):
                    state_io_gdn(kind, half, False)
                yield

            n = len(tiles)

            def prologue():
                yield from front(0)

            def body(next_pro=None):
                if barrier:
                    S.op("dve", lambda e: e.memset(junk[:, 0:2], 0.0), (), _bufs([junk, big0, big1, xt, xb]))
                for j in range(n):
                    gens = [back(j)]
                    if j + 1 < n:
                        gens.append(front(j + 1))
                    elif next_pro is not None:
                        gens.append(next_pro)
                    while gens:
                        for g in list(gens):
                            try:
                                next(g)
                            except StopIteration:
                                gens.remove(g)

            return prologue, body

        def transpose_1024(src_ap_fn, rd, dst):
            for half in range(2):
                pt = palloc(4)
                for c in range(4):
                    cc = half * 4 + c
                    mm(pt.ap(c * 128, (c + 1) * 128), src_ap_fn(cc), identb[:], rd + [identb], [pt])
                if half == 0:
                    CP("act", dst[:, 0:4, :], pt.ap(0, 512).rearrange("p (c n) -> p c n", n=128), [pt], [dst])
                else:
                    CP("dve", dst[:, 4:8, :], pt.ap(0, 512).rearrange("p (c n) -> p c n", n=128), [pt], [dst])

        def stage_m12(tiles, which, wt_off):
            wt, off = wt_off
            for j, (kind, ti) in enumerate(tiles):
                src = OA[j] if which == 0 else OB[j]
                transpose_1024(lambda cc: src[:, cc * 128:(cc + 1) * 128], [src], oT)
                sg = big0
                pp_sb = big1
                for g in range(2):
                    pg = palloc(4)
                    proj_tm(j, wt_off, g * 512, 512, pg)
                    ACT(sg[:, g * 512:(g + 1) * 512], pg.ap(0, 512), AF.Tanh, [pg], [sg], scale=0.5)
                TS("dve", sg[:], sg[:], 0.5, 0.5, ALU.mult, ALU.add, [sg], [sg])
                for g in range(2):
                    pp = palloc(4)
                    for kc in range(8):
                        mm(pp.ap(0, 512), oT[:, kc, :], Wbig[:, kc, off + 1024 + g * 512:off + 1024 + (g + 1) * 512], [oT, wt], [pp], start=(kc == 0), stop=(kc == 7))
                    if which == 0:
                        TT("dve", OA[j][:, g * 512:(g + 1) * 512], sg[:, g * 512:(g + 1) * 512], pp.ap(0, 512), ALU.mult, [sg, pp], [OA[j]])
                    else:
                        TT("dve", pp_sb[:, g * 512:(g + 1) * 512], sg[:, g * 512:(g + 1) * 512], pp.ap(0, 512), ALU.mult, [sg, pp], [pp_sb])
                if which == 1:
                    TT("dve", mgb[:], pp_sb[:], OA[j][:], ALU.add, [pp_sb, OA[j]], [mgb])
                    dstv = OB[j].t[:].rearrange("p (c n) -> p c n", n=128)
                    for half in range(2):
                        pt = palloc(4)
                        for c in range(4):
                            cc = half * 4 + c
                            mm(pt.ap(c * 128, (c + 1) * 128), mgb[:, cc * 128:(cc + 1) * 128], identb[:], [mgb, identb], [pt])
                        CP("act" if half == 0 else "dve", dstv[:, half * 4:half * 4 + 4, :], pt.ap(0, 512).rearrange("p (c n) -> p c n", n=128), [pt], [OB[j]])

        def stage_m3(tiles, wt_off):
            wt, off = wt_off
            XB = [xt, big0]

            def load_x(j):
                kind, ti = tiles[j]
                xn = XB[j % 2]
                if kind == "s":
                    MEMSET("dve", xn[:], 0.0, [xn])
                    DMA("sp", xn[0:16, :], xs, [], [xn])
                else:
                    DMA("sp", xn[:], xsrc(kind, ti), [], [xn])

            load_x(0)
            for j, (kind, ti) in enumerate(tiles):
                if kind == "s" or (kind == "p" and ti == 0):
                    build_gate_bc(kind)
                if j + 1 < len(tiles):
                    load_x(j + 1)
                mT = OB[j].t[:].rearrange("p (c n) -> p c n", n=128)
                xn = XB[j % 2]
                gp = big1
                for g in range(2):
                    po = palloc(4)
                    for kc in range(8):
                        mm(po.ap(0, 512), mT[:, kc, :], Wbig[:, kc, off + g * 512:off + (g + 1) * 512], [OB[j], wt], [po], start=(kc == 0), stop=(kc == 7))
                    TT("dve", gp[:, g * 512:(g + 1) * 512], po.ap(0, 512), gate_bc[kind][:, g * 512:(g + 1) * 512], ALU.mult, [po, gate_bc[kind]], [gp])
                TT("dve", xn[:], xn[:], gp[:], ALU.add, [xn, gp], [xn])
                ACT(mgb[:], xn[:], AF.Square, [xn], [mgb, ssq[0]], accum=ssq[0][:, 1:2])
                rt = ssq[3]
                TS("dve", rt[:, 3:4], ssq[0][:, 1:2], 1.0 / D, EPS, ALU.mult, ALU.add, [ssq[0]], [rt])
                POW(rt[:, 3:4], rt[:, 3:4], neghalf[:, 0:1], [rt, neghalf], [rt])
                STT(xn[:], xn[:], rt[:, 3:4], gf_bc[:], ALU.mult, ALU.mult, [xn, rt, gf_bc], [xn])
                if kind == "s":
                    DMA("sp", ys, xn[0:16, :], [xn], [])
                else:
                    DMA("sp", yp[ti * 128:(ti + 1) * 128, :], xn[:], [xn], [])

        def lw_g(pair, wo):
            load_w(wo, 0, w_in, C_QA + pair * 256, 256)
            load_w(wo, 256, w_in, C_KA + pair * 256, 256)
            load_w(wo, 512, w_in, C_VA + pair * 512, 512)
            load_w(wo, 1024, w_in, C_ZA + pair * 512, 512)

        def lw_d(half, wo):
            for kind_i in range(3):
                load_w(wo, kind_i * 512, w_in, C_QKV + kind_i * 1024 + half * 512, 512)
            load_w(wo, 1536, w_in, C_ZB + half * 512, 512)

        def lw_m(which, wo):
            load_w(wo, 0, w_in, C_GA if which == 0 else C_GB, 1024)
            load_w(wo, 1024, w_pa if which == 0 else w_pb, 0, 1024)

        def lw_o(wo):
            load_w(wo, 0, w_out, 0, 1024)

        stage_list = []
        for ps_ in range(NPASS):
            tiles = [("p", ps_ * TPP + i) for i in range(TPP)]
            if ps_ == 0:
                tiles = [("s", 0)] + tiles
            fp, lp = (ps_ == 0), (ps_ == NPASS - 1)
            ev = (len(tiles) % 2 == 0)
            stage_list.append(("gen", lambda wo: lw_g(0, wo), lambda wo, t=tiles, f=fp, l=lp: stage_g(t, 0, wo, f, l, True), tiles, ev))
            stage_list.append(("gen", lambda wo: lw_g(1, wo), lambda wo, t=tiles, f=fp, l=lp: stage_g(t, 1, wo, f, l, False), None, ev))
            stage_list.append(("gen", lambda wo: lw_d(0, wo), lambda wo, t=tiles, f=fp, l=lp: stage_d(t, 0, wo, f, l, True), None, ev))
            stage_list.append(("gen", lambda wo: lw_d(1, wo), lambda wo, t=tiles, f=fp, l=lp: stage_d(t, 1, wo, f, l, False), None, False))
            stage_list.append(("run", lambda wo: lw_m(0, wo), lambda wo, t=tiles: stage_m12(t, 0, wo), None, False))
            stage_list.append(("run", lambda wo: lw_m(1, wo), lambda wo, t=tiles: stage_m12(t, 1, wo), None, False))
            stage_list.append(("run", lambda wo: lw_o(wo), lambda wo, t=tiles: stage_m3(t, wo), None, False))

        import os as _os
        _stop = int(_os.environ.get("KSTOP", "1000"))
        stage_list[0][1](WBUF[0])
        started = None
        for si, (typ, lw, mk, htiles, overlap_next) in enumerate(stage_list):
            if si >= _stop:
                break
            if si + 1 < len(stage_list):
                stage_list[si + 1][1](WBUF[(si + 1) % 2])
            if htiles is not None:
                stage_h(htiles)
            if typ == "run":
                mk(WBUF[si % 2])
                started = None
                continue
            if started is None:
                pro, body = mk(WBUF[si % 2])
                for _ in pro():
                    pass
            else:
                body = started
            nxt = None
            started = None
            if overlap_next and si + 1 < min(len(stage_list), _stop) and stage_list[si + 1][0] == "gen":
                pro2, body2 = stage_list[si + 1][2](WBUF[(si + 1) % 2])
                nxt = pro2()
                started = body2
            body(nxt)

        S.emit_all(block, sems)
    return nc


_NC_CACHE = {}


def kernel(x_prompt, x_sample, c_prompt, c_sample, state_gla, state_gdn, cache_conv_gdn,
           w_ada, b_ada, g_norm1, w_in, w_gk2, b_gk, w_conv, a_log, dt_bias,
           g_norm_a, g_norm_b, w_pa, w_pb, w_out, g_final):
    f = lambda a: np.ascontiguousarray(np.asarray(a, dtype=np.float32))
    if "nc" not in _NC_CACHE:
        _NC_CACHE["nc"] = build_nc()
    nc = _NC_CACHE["nc"]
    shared = {
        "w_ada": f(w_ada[0]), "b_ada": f(b_ada[0]), "g1": f(g_norm1[0]), "w_in": f(w_in[0]),
        "w_gk2": f(w_gk2[0]), "b_gk": f(b_gk[0]), "w_conv": f(w_conv[0]), "a_log": f(a_log[0]),
        "dt_bias": f(dt_bias[0]), "gna": f(g_norm_a[0]), "gnb": f(g_norm_b[0]),
        "w_pa": f(w_pa[0]), "w_pb": f(w_pb[0]), "w_out": f(w_out[0]), "gfin": f(g_final),
    }
    in_maps = []
    for b in range(8):
        m = dict(shared)
        m.update({
            "xp": f(x_prompt[b]), "xs": f(x_sample[b]), "cp": f(c_prompt[b]), "cs": f(c_sample[b]),
            "sgla": f(state_gla[0, b]), "sgdn": f(state_gdn[0, b]), "cconv": f(cache_conv_gdn[0, b]),
        })
        in_maps.append(m)
    res = run_bass_kernel_spmd(nc, in_maps, core_ids=list(range(8)))
    R = res.results
    st = lambda k: np.stack([np.asarray(R[b][k], dtype=np.float32) for b in range(8)])
    return (st("yp"), st("ys"), st("o_gla_p")[None], st("o_gdn_p")[None], st("o_conv_p")[None],
            st("o_gla_s")[None], st("o_gdn_s")[None], st("o_conv_s")[None])
```
